# Optimizing a Trainium2 kernel written in Bass

```python
import math
import jax, jax.numpy as jnp
from jax import lax
import numpy as np

D_MODEL = 2048
BATCH = 8
SEQ = 2048
DEPTH = 4

GRID_W = 64
CTX_LEN = 256
N_MIXERS = 2
N_MODS = 6
RMS_EPS = 1e-6
NEG_INF = -1e30
HEAD_DIM = 128
N_Q_HEADS = D_MODEL // HEAD_DIM
N_KV_HEADS = 4
GQA_GROUP = N_Q_HEADS // N_KV_HEADS
WINDOW = 128
ATTN_BLOCK = 128
ROPE_BASE = 10000.0
HY_ORDER = 2
HY_SHORT_W = 3
HY_EMB_BANDS = 16
HY_EMB_DIM = 1 + 2 * HY_EMB_BANDS
HY_FILTER_HIDDEN = 64
HY_INNER_MLPS = 2
HY_MOD_SHIFT = 0.05
HY_N_DIR = 2
PEER_HEADS = 8
PEER_N_KEYS = 128
PEER_N_EXPERTS = PEER_N_KEYS * PEER_N_KEYS
PEER_D_KEY = 256
PEER_HALF = PEER_D_KEY // 2
PEER_TOPK = 16
PEER_CHUNK = 128

N_ATTN_LAYERS = (DEPTH + N_MIXERS - 1) // N_MIXERS
N_HYENA_LAYERS = DEPTH // N_MIXERS

kernel_name = "hybrid_swa_hyena_peer_dit"


def rms_norm(x, g):
    xf = x.astype(jnp.float32)
    y = xf * lax.rsqrt(jnp.mean(xf * xf, axis=-1, keepdims=True) + RMS_EPS)
    return (y * g.astype(jnp.float32)).astype(x.dtype)


def modulate(h, g, shift, scale):
    return rms_norm(h, g) * (1 + scale) + shift


def axial_rope_tables(seq_len):
    rows = seq_len // GRID_W
    row = jnp.repeat(jnp.arange(rows), GRID_W).astype(jnp.float32)
    col = jnp.tile(jnp.arange(GRID_W), rows).astype(jnp.float32)
    n_freq = HEAD_DIM // 4
    inv = ROPE_BASE ** (-jnp.arange(n_freq, dtype=jnp.float32) / n_freq)
    ar, ac = row[:, None] * inv, col[:, None] * inv
    ang = jnp.concatenate([ar, ar, ac, ac], axis=-1)
    return jnp.cos(ang), jnp.sin(ang)


def apply_rope(x, cos, sin):
    x1, x2, x3, x4 = jnp.split(x, 4, axis=-1)
    rot = jnp.concatenate([-x2, x1, -x4, x3], axis=-1)
    out = x.astype(jnp.float32) * cos[:, None, :] + rot.astype(jnp.float32) * sin[:, None, :]
    return out.astype(x.dtype)


def sink_softmax(s, sink):
    m = jnp.maximum(jnp.max(s, axis=-1, keepdims=True), sink)
    p = jnp.exp(s - m)
    return p / (jnp.sum(p, axis=-1, keepdims=True) + jnp.exp(sink - m))


def windowed_gqa_with_sink(h_lat, h_ctx, w_qkv, w_o, sink, with_ctx_queries):
    B, L, _ = h_lat.shape
    C = h_ctx.shape[1]
    nb = L // ATTN_BLOCK
    qd, kvd = N_Q_HEADS * HEAD_DIM, N_KV_HEADS * HEAD_DIM
    scale = HEAD_DIM ** -0.5
    sink = sink.astype(jnp.float32).reshape(N_KV_HEADS, GQA_GROUP, 1, 1)

    q_l, k_l, v_l = jnp.split(h_lat @ w_qkv, [qd, qd + kvd], axis=-1)
    cos, sin = axial_rope_tables(L)
    q_l = apply_rope(q_l.reshape(B, L, N_Q_HEADS, HEAD_DIM), cos, sin)
    k_l = apply_rope(k_l.reshape(B, L, N_KV_HEADS, HEAD_DIM), cos, sin)
    v_l = v_l.reshape(B, L, N_KV_HEADS, HEAD_DIM)

    k_c, v_c = jnp.split(h_ctx @ w_qkv[:, qd:], 2, axis=-1)
    k_c = k_c.reshape(B, C, N_KV_HEADS, HEAD_DIM)
    v_c = v_c.reshape(B, C, N_KV_HEADS, HEAD_DIM)

    qb = q_l.reshape(B, nb, ATTN_BLOCK, N_KV_HEADS, GQA_GROUP, HEAD_DIM)

    def band(t):
        tp = jnp.pad(t, ((0, 0), (ATTN_BLOCK, ATTN_BLOCK), (0, 0), (0, 0)))
        tp = tp.reshape(B, nb + 2, ATTN_BLOCK, N_KV_HEADS, HEAD_DIM)
        return jnp.concatenate([tp[:, :-2], tp[:, 1:-1], tp[:, 2:]], axis=2)

    kw, vw = band(k_l), band(v_l)
    s_win = jnp.einsum('bnqhgd,bnkhd->bnhgqk', qb, kw, preferred_element_type=jnp.float32) * scale
    a = jnp.arange(ATTN_BLOCK)[:, None]
    kb = jnp.arange(3 * ATTN_BLOCK)[None, :]
    near = jnp.abs(kb - a - ATTN_BLOCK) <= WINDOW
    kpos = (jnp.arange(nb)[:, None, None] - 1) * ATTN_BLOCK + kb[None]
    valid = near[None] & (kpos >= 0) & (kpos < L)
    s_win = jnp.where(valid[None, :, None, None], s_win, NEG_INF)
    s_ctx = jnp.einsum('bnqhgd,bkhd->bnhgqk', qb, k_c, preferred_element_type=jnp.float32) * scale

    p = sink_softmax(jnp.concatenate([s_win, s_ctx], axis=-1), sink).astype(v_l.dtype)
    p_win, p_ctx = p[..., :3 * ATTN_BLOCK], p[..., 3 * ATTN_BLOCK:]
    o = (jnp.einsum('bnhgqk,bnkhd->bnqhgd', p_win, vw)
         + jnp.einsum('bnhgqk,bkhd->bnqhgd', p_ctx, v_c))
    y_lat = o.reshape(B, L, qd) @ w_o

    y_ctx = None
    if with_ctx_queries:
        q_c = (h_ctx @ w_qkv[:, :qd]).reshape(B, C, N_KV_HEADS, GQA_GROUP, HEAD_DIM)
        s_cc = jnp.einsum('bqhgd,bkhd->bhgqk', q_c, k_c, preferred_element_type=jnp.float32) * scale
        p_cc = sink_softmax(s_cc, sink).astype(v_c.dtype)
        y_ctx = jnp.einsum('bhgqk,bkhd->bqhgd', p_cc, v_c).reshape(B, C, qd) @ w_o
    return y_lat, y_ctx


def centred_short_conv(x, w, b):
    L = x.shape[1]
    r = HY_SHORT_W // 2
    xp = jnp.pad(x, ((0, 0), (r, r), (0, 0)))
    out = b
    for j in range(HY_SHORT_W):
        out = out + xp[:, j:j + L] * w[j]
    return out


def hyena_filter_spectra(L, f_w1, f_b1, f_w2, f_b2, f_w3, decay):
    f32 = jnp.float32
    t = jnp.arange(L, dtype=f32)
    t01 = t / L
    bands = jnp.arange(1, HY_EMB_BANDS + 1, dtype=f32)
    ang = 2.0 * math.pi * t[:, None] * bands[None, :] / L
    z = jnp.concatenate([t01[:, None], jnp.cos(ang), jnp.sin(ang)], axis=-1)
    h = jnp.sin(z @ f_w1.astype(f32) + f_b1.astype(f32))
    for i in range(HY_INNER_MLPS):
        h = jnp.sin(h @ f_w2[i].astype(f32) + f_b2[i].astype(f32))
    h = (h @ f_w3.astype(f32)).reshape(L, HY_ORDER, HY_N_DIR, D_MODEL)
    window = jnp.exp(-t01[:, None, None] * jnp.abs(decay.astype(f32))[None]) + HY_MOD_SHIFT
    h = h * window[:, :, None, :]
    fwd, bwd = h[:, :, 0], h[:, :, 1]
    filt2l = jnp.concatenate([fwd, jnp.zeros((1, HY_ORDER, D_MODEL), f32), bwd[1:][::-1]], axis=0)
    return jnp.fft.rfft(filt2l, axis=0)


def long_conv(z, spec, bias):
    L = z.shape[1]
    zf = z.astype(jnp.float32)
    y = jnp.fft.irfft(jnp.fft.rfft(zf, n=2 * L, axis=1) * spec[None], n=2 * L, axis=1)[:, :L]
    return (y + zf * bias.astype(jnp.float32)).astype(z.dtype)


def hyena_sequence(h, w_in, conv_w, conv_b, f_w1, f_b1, f_w2, f_b2, f_w3, decay, fbias, w_out):
    L = h.shape[1]
    u = centred_short_conv(h @ w_in, conv_w, conv_b)
    v, x1, x2 = jnp.split(u, 3, axis=-1)
    spec = hyena_filter_spectra(L, f_w1, f_b1, f_w2, f_b2, f_w3, decay)
    z = x1 * long_conv(v, spec[:, 0], fbias[0])
    y = x2 * long_conv(z, spec[:, 1], fbias[1])
    return y @ w_out


def peer_ffn(h, w_q, keys1, keys2, u_tab, v_tab):
    B, L, D = h.shape
    chunks = h.reshape((B * L) // PEER_CHUNK, PEER_CHUNK, D)

    def one_chunk(xc):
        q = (xc @ w_q).reshape(PEER_CHUNK, PEER_HEADS, PEER_D_KEY)
        q1, q2 = q[..., :PEER_HALF], q[..., PEER_HALF:]
        s1 = jnp.einsum('chd,hkd->chk', q1, keys1, preferred_element_type=jnp.float32)
        s2 = jnp.einsum('chd,hkd->chk', q2, keys2, preferred_element_type=jnp.float32)
        v1, i1 = lax.top_k(s1, PEER_TOPK)
        v2, i2 = lax.top_k(s2, PEER_TOPK)
        cand_s = (v1[..., :, None] + v2[..., None, :]).reshape(PEER_CHUNK, PEER_HEADS, PEER_TOPK * PEER_TOPK)
        cand_i = (i1[..., :, None] * PEER_N_KEYS + i2[..., None, :]).reshape(PEER_CHUNK, PEER_HEADS, PEER_TOPK * PEER_TOPK)
        top_s, pos = lax.top_k(cand_s, PEER_TOPK)
        idx = jnp.take_along_axis(cand_i, pos, axis=-1)
        g = jax.nn.softmax(top_s, axis=-1)
        u = jnp.take(u_tab, idx, axis=0)
        act = jax.nn.gelu(jnp.einsum('chkd,cd->chk', u, xc, preferred_element_type=jnp.float32), approximate=False)
        v = jnp.take(v_tab, idx, axis=0)
        return jnp.einsum('chk,chkd->cd', (g * act).astype(xc.dtype), v)

    return lax.map(one_chunk, chunks).reshape(B, L, D)


def setup_inputs(seed: int = 0) -> dict:
    key = jax.random.key(seed)
    ks = jax.random.split(key, 32)
    f32 = jnp.float32
    D = D_MODEL
    qkv_w = N_Q_HEADS * HEAD_DIM + 2 * N_KV_HEADS * HEAD_DIM

    def nrm(k, shape, s):
        return jax.random.normal(k, shape, f32) * s

    decay_base = jnp.abs(jnp.linspace(math.log(1e-2) / 0.3, math.log(1e-2) / 1.5, D, dtype=f32))
    return {
        "x": nrm(ks[0], (BATCH, SEQ, D), 1.0),
        "c": nrm(ks[1], (BATCH, D), 1.0),
        "ctx": nrm(ks[2], (BATCH, CTX_LEN, D), 1.0),
        "c_ctx": nrm(ks[3], (D,), 1.0),
        "ada_w": nrm(ks[4], (DEPTH, D, N_MODS * D), 0.5 * D ** -0.5),
        "ada_b": nrm(ks[5], (DEPTH, N_MODS * D), 0.02),
        "norm_mix_g": 1.0 + nrm(ks[6], (DEPTH, D), 0.02),
        "norm_ffn_g": 1.0 + nrm(ks[7], (DEPTH, D), 0.02),
        "final_g": 1.0 + nrm(ks[8], (D,), 0.02),
        "attn_w_qkv": nrm(ks[9], (N_ATTN_LAYERS, D, qkv_w), D ** -0.5),
        "attn_w_o": nrm(ks[10], (N_ATTN_LAYERS, N_Q_HEADS * HEAD_DIM, D), (N_Q_HEADS * HEAD_DIM) ** -0.5),
        "attn_sink": nrm(ks[11], (N_ATTN_LAYERS, N_Q_HEADS), 1.0),
        "hy_w_in": nrm(ks[12], (N_HYENA_LAYERS, D, 3 * D), D ** -0.5),
        "hy_conv_w": nrm(ks[13], (N_HYENA_LAYERS, HY_SHORT_W, 3 * D), HY_SHORT_W ** -0.5),
        "hy_conv_b": nrm(ks[14], (N_HYENA_LAYERS, 3 * D), 0.02),
        "hy_f_w1": nrm(ks[15], (N_HYENA_LAYERS, HY_EMB_DIM, HY_FILTER_HIDDEN), HY_EMB_DIM ** -0.5),
        "hy_f_b1": nrm(ks[16], (N_HYENA_LAYERS, HY_FILTER_HIDDEN), 0.1),
        "hy_f_w2": nrm(ks[17], (N_HYENA_LAYERS, HY_INNER_MLPS, HY_FILTER_HIDDEN, HY_FILTER_HIDDEN), HY_FILTER_HIDDEN ** -0.5),
        "hy_f_b2": nrm(ks[18], (N_HYENA_LAYERS, HY_INNER_MLPS, HY_FILTER_HIDDEN), 0.1),
        "hy_f_w3": nrm(ks[19], (N_HYENA_LAYERS, HY_FILTER_HIDDEN, HY_ORDER * HY_N_DIR * D), 0.01),
        "hy_decay": decay_base[None, None, :] * (1.0 + nrm(ks[20], (N_HYENA_LAYERS, HY_ORDER, D), 0.05)),
        "hy_fbias": nrm(ks[21], (N_HYENA_LAYERS, HY_ORDER, D), 0.1),
        "hy_w_out": nrm(ks[22], (N_HYENA_LAYERS, D, D), D ** -0.5),
        "peer_w_q": nrm(ks[23], (DEPTH, D, PEER_HEADS * PEER_D_KEY), D ** -0.5),
        "peer_keys1": nrm(ks[24], (DEPTH, PEER_HEADS, PEER_N_KEYS, PEER_HALF), PEER_HALF ** -0.5),
        "peer_keys2": nrm(ks[25], (DEPTH, PEER_HEADS, PEER_N_KEYS, PEER_HALF), PEER_HALF ** -0.5),
        "peer_u": nrm(ks[26], (DEPTH, PEER_N_EXPERTS, D), D ** -0.5),
        "peer_v": nrm(ks[27], (DEPTH, PEER_N_EXPERTS, D), 0.25),
    }


def reference(x, c, ctx, c_ctx, ada_w, ada_b, norm_mix_g, norm_ffn_g, final_g,
              attn_w_qkv, attn_w_o, attn_sink,
              hy_w_in, hy_conv_w, hy_conv_b, hy_f_w1, hy_f_b1, hy_f_w2, hy_f_b2, hy_f_w3,
              hy_decay, hy_fbias, hy_w_out,
              peer_w_q, peer_keys1, peer_keys2, peer_u, peer_v):
    D = D_MODEL
    last_attn = max(i for i in range(DEPTH) if i % N_MIXERS == 0)
    cond_lat = jax.nn.silu(c)[:, None, :]
    cond_ctx = jax.nn.silu(c_ctx)[None, None, :]
    h_lat, h_ctx = x, ctx
    for i in range(DEPTH):
        is_attn = (i % N_MIXERS) == 0
        j = i // N_MIXERS
        ctx_update = i < last_attn
        sh1, sc1, g1, sh2, sc2, g2 = jnp.split(cond_lat @ ada_w[i] + ada_b[i], N_MODS, axis=-1)
        a_lat = modulate(h_lat, norm_mix_g[i], sh1, sc1)
        if ctx_update:
            csh1, csc1, cg1, csh2, csc2, cg2 = jnp.split(cond_ctx @ ada_w[i] + ada_b[i], N_MODS, axis=-1)
        else:
            csh1, csc1 = jnp.split(cond_ctx @ ada_w[i][:, :2 * D] + ada_b[i][:2 * D], 2, axis=-1)

        if is_attn:
            a_ctx = modulate(h_ctx, norm_mix_g[i], csh1, csc1)
            y_lat, y_ctx = windowed_gqa_with_sink(a_lat, a_ctx, attn_w_qkv[j], attn_w_o[j],
                                                  attn_sink[j], ctx_update)
        else:
            hy_args = (hy_w_in[j], hy_conv_w[j], hy_conv_b[j], hy_f_w1[j], hy_f_b1[j], hy_f_w2[j],
                       hy_f_b2[j], hy_f_w3[j], hy_decay[j], hy_fbias[j], hy_w_out[j])
            y_lat = hyena_sequence(a_lat, *hy_args)
            y_ctx = None
            if ctx_update:
                a_ctx = modulate(h_ctx, norm_mix_g[i], csh1, csc1)
                y_ctx = hyena_sequence(a_ctx, *hy_args)

        peer_args = (peer_w_q[i], peer_keys1[i], peer_keys2[i], peer_u[i], peer_v[i])
        h_lat = h_lat + g1 * y_lat
        h_lat = h_lat + g2 * peer_ffn(modulate(h_lat, norm_ffn_g[i], sh2, sc2), *peer_args)
        if ctx_update:
            h_ctx = h_ctx + cg1 * y_ctx
            h_ctx = h_ctx + cg2 * peer_ffn(modulate(h_ctx, norm_ffn_g[i], csh2, csc2), *peer_args)
    return rms_norm(h_lat, final_g)
```

```python
import math
from contextlib import ExitStack
import numpy as np
import ml_dtypes
import concourse.bass as bass
import concourse.mybir as mybir
from concourse.bass_utils import run_bass_kernel_spmd

F32 = mybir.dt.float32
BF16 = mybir.dt.bfloat16
I32 = mybir.dt.int32
U32 = mybir.dt.uint32
ALU = mybir.AluOpType
AF = mybir.ActivationFunctionType
AX = mybir.AxisListType

ENGS = ("pe", "act", "dve", "pool", "sp")
N_DMA_SEMS = 56
EPOCH_N = 12000

CFG = dict(D=2048, L=2048, C=256, DEPTH=4, NCORES=8)
NEG = -1e30
RMS_EPS = 1e-6
NKV = 4
PH, PK, TOPK, DQ = 8, 128, 16, 2048
NE = PK * PK
HYH = 64
HYE = 33


class Sched:
    def __init__(self, nc, stack):
        self.nc = nc
        self.stack = stack
        self.epoch = {e: 0 for e in ENGS}
        self.ops = {e: [] for e in ENGS}
        self.cnt = {e: 0 for e in ENGS}
        self.seen = {e: {} for e in ENGS}
        self.res = {}
        self.sems = {}
        for e in ENGS:
            self.sems[(e, 0)] = stack.enter_context(nc.semaphore("s_" + e))
        self.dma_cnt = [0] * N_DMA_SEMS
        self.dma_rr = 0
        for j in range(N_DMA_SEMS):
            self.sems[("d", j)] = stack.enter_context(nc.semaphore("s_d%d" % j))
        self.nops = 0

    def _need(self, eng, tok, waits):
        if tok is None:
            return
        k, v = tok
        if k[0] == eng and eng == "pe":
            return
        if self.seen[eng].get(k, 0) >= v:
            return
        self.seen[eng][k] = v
        waits.append((k, v))

    def op(self, eng, fn, reads=(), writes=(), dma=False):
        waits = []
        for r in reads:
            st = self.res.get(r)
            if st is not None:
                self._need(eng, st[0], waits)
        for w in writes:
            st = self.res.get(w)
            if st is not None:
                self._need(eng, st[0], waits)
                for t in st[1].items():
                    self._need(eng, t, waits)
        if dma:
            j = self.dma_rr
            self.dma_rr = (self.dma_rr + 1) % N_DMA_SEMS
            if self.dma_cnt[j] > 0:
                self._need(eng, (("d", j), 16 * self.dma_cnt[j]), waits)
            self.dma_cnt[j] += 1
            tok = (("d", j), 16 * self.dma_cnt[j])
            inc = 16
        else:
            self.cnt[eng] += 1
            tok = ((eng, self.epoch[eng]), self.cnt[eng])
            inc = 1
        k, v = tok
        for r in reads:
            st = self.res.setdefault(r, [None, {}])
            if st[1].get(k, 0) < v:
                st[1][k] = v
        for w in writes:
            self.res[w] = [tok, {}]
        self.ops[eng].append((waits, fn, tok[0], inc))
        self.nops += 1
        return tok

    def barrier(self):
        toks = [((e, self.epoch[e]), self.cnt[e]) for e in ENGS if self.cnt[e] > 0]
        toks += [(("d", j), 16 * self.dma_cnt[j]) for j in range(N_DMA_SEMS) if self.dma_cnt[j] > 0]
        for e in ENGS:
            waits = []
            for t in toks:
                self._need(e, t, waits)
            self.ops[e].append((waits, None, None, 0))
        self.res = {}
        for e in ENGS:
            if self.cnt[e] > EPOCH_N:
                self.epoch[e] += 1
                self.cnt[e] = 0
                self.sems[(e, self.epoch[e])] = self.stack.enter_context(
                    self.nc.semaphore("s_%s_%d" % (e, self.epoch[e])))

    def emit(self):
        nc = self.nc
        sems = self.sems
        with nc.Block() as block:
            def mk(engname):
                lst = self.ops[engname]

                def body(e):
                    for waits, fn, semkey, inc in lst:
                        for k, v in waits:
                            e.wait_ge(sems[k], v)
                        if fn is not None:
                            fn(e).then_inc(sems[semkey], inc)
                return body
            block.tensor(mk("pe"))
            block.scalar(mk("act"))
            block.vector(mk("dve"))
            block.gpsimd(mk("pool"))
            block.sync(mk("sp"))

    def dma(self, out, in_, r, w, eng="sp"):
        self.op(eng, lambda e: e.dma_start(out=out, in_=in_), r, w, dma=True)

    def dma_k(self, out, in_, r, w, eng="sp", g=4):
        K_ = out.shape[1]
        for k0 in range(0, K_, g):
            k1 = min(K_, k0 + g)
            self.dma(out[:, k0:k1, :], in_[:, k0:k1, :], r, w, eng)

    def mm(self, out, lhsT, rhs, start, stop, r, w):
        self.op("pe", lambda e: e.matmul(out, lhsT, rhs, start=start, stop=stop), r, w)

    def tr(self, out, in_, ident, r, w):
        self.op("pe", lambda e: e.transpose(out, in_, ident), r, w)

    def act(self, out, in_, func, r, w, bias=None, scale=None, accum_out=None):
        kw = {}
        if bias is not None:
            kw["bias"] = bias
        if scale is not None:
            kw["scale"] = scale
        if accum_out is not None:
            kw["accum_out"] = accum_out
        self.op("act", lambda e: e.activation(out, in_, func, **kw), r, w)

    def tt(self, eng, out, in0, in1, op, r, w):
        self.op(eng, lambda e: e.tensor_tensor(out, in0, in1, op), r, w)

    def ts(self, eng, out, in0, s1, s2, op0, op1, r, w):
        if op1 is None:
            self.op(eng, lambda e: e.tensor_scalar(out, in0, s1, None, op0), r, w)
        else:
            self.op(eng, lambda e: e.tensor_scalar(out, in0, s1, s2, op0, op1), r, w)

    def stt(self, out, in0, scalar, in1, op0, op1, r, w, accum_out=None):
        if accum_out is None:
            self.op("dve", lambda e: e.scalar_tensor_tensor(out, in0, scalar, in1, op0, op1), r, w)
        else:
            self.op("dve", lambda e: e.scalar_tensor_tensor(out, in0, scalar, in1, op0, op1, accum_out=accum_out), r, w)

    def cp(self, eng, out, in_, r, w):
        if eng == "act":
            self.op("act", lambda e: e.copy(out, in_), r, w)
        else:
            self.op(eng, lambda e: e.tensor_copy(out, in_), r, w)

    def memset(self, eng, ap, val, w):
        self.op(eng, lambda e: e.memset(ap, val), (), w)


class Ring:
    def __init__(self, alloc, name, shape, dt, n):
        self.t = [alloc(name + str(i), shape, dt) for i in range(n)]
        self.names = [name + str(i) for i in range(n)]
        self.i = -1

    def next(self):
        self.i = (self.i + 1) % len(self.t)
        return self.t[self.i], self.names[self.i]


def host_consts(cfg):
    D, L, C = cfg["D"], cfg["L"], cfg["C"]
    k = {}
    k["ident_f"] = np.eye(128, dtype=np.float32)
    k["ident_b"] = np.eye(128).astype(ml_dtypes.bfloat16)
    P = np.zeros((128, 128), np.float32)
    for j in range(32):
        P[32 + j, j] = -1.0
        P[j, 32 + j] = 1.0
        P[96 + j, 64 + j] = -1.0
        P[64 + j, 96 + j] = 1.0
    k["prot"] = P
    GW = 64
    rows = L // GW
    row = np.repeat(np.arange(rows), GW).astype(np.float32)
    col = np.tile(np.arange(GW), rows).astype(np.float32)
    inv = (10000.0 ** (-np.arange(32, dtype=np.float32) / 32)).astype(np.float32)
    ar, ac = row[:, None] * inv, col[:, None] * inv
    ang = np.concatenate([ar, ar, ac, ac], axis=-1)
    k["cosT"] = np.ascontiguousarray(np.cos(ang).T.astype(np.float32))
    k["sinT"] = np.ascontiguousarray(np.sin(ang).T.astype(np.float32))
    q = np.arange(128)[:, None]
    kk = np.arange(384)[None, :]
    masks = np.zeros((128, 3, 384), np.float32)
    for v, off in enumerate((0, 128, 256)):
        masks[:, v, :] = np.where(np.abs(kk - (q + off)) <= 128, 0.0, NEG)
    k["masks"] = masks
    io = np.tile(np.arange(16, dtype=np.float32), 128)[None, :].repeat(128, 0)
    k["iota16"] = np.ascontiguousarray(io)
    for n, tag in ((L, "L"), (C, "C")):
        t = np.arange(n, dtype=np.float64)
        f = np.arange(n, dtype=np.float64)
        th = 2.0 * np.pi * np.outer(t, f) / (2 * n)
        Fre = np.cos(th)
        Fim = -np.sin(th)
        Fim[:, 0] = np.cos(np.pi * t)
        Fimb = np.sin(th)
        Fimb[:, 0] = np.cos(np.pi * t)
        Gre = (2.0 / (2 * n)) * np.cos(th.T)
        Gre[0, :] = 1.0 / (2 * n)
        Gim = -(2.0 / (2 * n)) * np.sin(th.T)
        Gim[0, :] = np.cos(np.pi * t) / (2 * n)
        nt = n // 128

        def fwd_layout(M):
            return np.ascontiguousarray(M.reshape(nt, 128, nt, 128).transpose(2, 1, 0, 3)).astype(ml_dtypes.bfloat16)

        def inv_layout(M):
            return np.ascontiguousarray(M.reshape(nt, 128, nt, 128).transpose(2, 1, 0, 3)).astype(ml_dtypes.bfloat16)
        k["Fre" + tag] = fwd_layout(Fre)
        k["Fim" + tag] = fwd_layout(Fim)
        k["Fimb" + tag] = fwd_layout(Fimb)
        k["Gre" + tag] = inv_layout(Gre)
        k["Gim" + tag] = inv_layout(Gim)
        tt = np.arange(n, dtype=np.float32)
        t01 = tt / n
        bands = np.arange(1, 17, dtype=np.float32)
        a = 2.0 * math.pi * tt[:, None] * bands[None, :] / n
        z = np.concatenate([t01[:, None], np.cos(a), np.sin(a)], axis=-1).astype(np.float32)
        k["zT" + tag] = np.ascontiguousarray(z.T)
        k["nt01" + tag] = np.ascontiguousarray((-t01).reshape(nt, 128).T.astype(np.float32))
    return k


def build_program(cfg, stop_after=None):
    D, L, C, DEPTH = cfg["D"], cfg["L"], cfg["C"], cfg["DEPTH"]
    KD = D // 128
    T = L + C
    NLT, NCT, NT = L // 128, C // 128, (L + C) // 128
    NQ = D // 128
    GRP = NQ // NKV
    QKVW = (NQ + 2 * NKV) * 128
    NA, NH = (DEPTH + 1) // 2, DEPTH // 2
    CB = D // 512
    last_attn = max(i for i in range(DEPTH) if i % 2 == 0)
    TP = T + 4
    LOFF, COFF = 1, L + 3

    nc = bass.Bass("TRN2", target_bir_lowering=False)
    inp = {}

    def ein(name, shape, dt=F32):
        inp[name] = nc.dram_tensor(name, list(shape), dt, kind="ExternalInput").ap()
        return inp[name]

    x_in = ein("x", [L, D])
    ctx_in = ein("ctx", [C, D])
    cT_in = ein("cT", [128, KD, 2])
    ada_w = ein("ada_w", [DEPTH, D, 6 * D])
    ada_b = ein("ada_b", [DEPTH, 6 * D])
    nmg = ein("norm_mix_g", [DEPTH, D])
    nfg = ein("norm_ffn_g", [DEPTH, D])
    fing = ein("final_g", [1, D])
    w_qkv = ein("attn_w_qkv", [NA, D, QKVW])
    w_o = ein("attn_w_o", [NA, D, D])
    sink_in = ein("attn_sink", [NA, NQ])
    hy_w_in = ein("hy_w_in", [NH, D, 3 * D])
    hy_cw = ein("hy_conv_w", [NH, 3, 3 * D])
    hy_cb = ein("hy_conv_b", [NH, 3 * D])
    hy_w1 = ein("hy_f_w1", [NH, HYE, HYH])
    hy_b1 = ein("hy_f_b1", [NH, HYH, 1])
    hy_w2 = ein("hy_f_w2", [NH, 2, HYH, HYH])
    hy_b2 = ein("hy_f_b2", [NH, 2, HYH, 1])
    hy_w3 = ein("hy_f_w3", [NH, HYH, 4 * D])
    hy_dec = ein("hy_decay", [NH, 2, D])
    hy_fb = ein("hy_fbias", [NH, 2, D])
    hy_wout = ein("hy_w_out", [NH, D, D])
    p_wq = ein("peer_w_q", [DEPTH, D, DQ])
    p_keysT = ein("peer_keysT", [DEPTH, 128, 16, 128])
    p_u = ein("peer_u", [DEPTH * NE, D])
    p_v = ein("peer_v", [DEPTH * NE, D])
    cst = {}
    cst["ident_f"] = ein("ident_f", [128, 128])
    cst["ident_b"] = ein("ident_b", [128, 128], BF16)
    cst["prot"] = ein("prot", [128, 128])
    cst["cosT"] = ein("cosT", [128, L])
    cst["sinT"] = ein("sinT", [128, L])
    cst["masks"] = ein("masks", [128, 3, 384])
    cst["iota16"] = ein("iota16", [128, 2048])
    for n, tag in ((L, "L"), (C, "C")):
        nt = n // 128
        for nm in ("Fre", "Fim", "Fimb", "Gre", "Gim"):
            cst[nm + tag] = ein(nm + tag, [nt, 128, nt, 128], BF16)
        cst["zT" + tag] = ein("zT" + tag, [HYE, n])
        cst["nt01" + tag] = ein("nt01" + tag, [128, nt])

    out_d = nc.dram_tensor("out", [L, D], F32, kind="ExternalOutput").ap()

    def scratch(name, shape, dt=F32):
        return nc.dram_tensor(name, list(shape), dt, kind="Internal").ap()

    hbuf = scratch("hbuf", [T, D])
    moddr = scratch("moddr", [DEPTH, 2, 6 * D])
    MT = scratch("MT", [D, T], BF16)
    XN = scratch("XN", [T, D])
    QT = scratch("QT", [16, 128, T], BF16)
    U_d = scratch("U_d", [T, 3 * D])
    UVs = [scratch("UV%d" % l_, [NE, 2 * D], BF16) for l_ in range(DEPTH)]
    spec = {}
    for tag, n in (("L", L), ("C", C)):
        for o in range(2):
            spec[(tag, o, "re")] = scratch("specre%s%d" % (tag, o), [n, D])
            spec[(tag, o, "im")] = scratch("specim%s%d" % (tag, o), [n, D])
            spec[(tag, o, "d0")] = scratch("specd0%s%d" % (tag, o), [128, D])

    with ExitStack() as top:
        S = Sched(nc, top)
        ps = top.enter_context(nc.psum_tensor("ps", [128, 8, 512], F32))
        psb = ps[:, :, :].bitcast(BF16)

        uid = [0]

        def PB(b):
            return "psb%d" % b

        def phase():
            st = ExitStack()

            def alloc(name, shape, dt):
                uid[0] += 1
                return st.enter_context(nc.sbuf_tensor("%s_%d" % (name, uid[0]), list(shape), dt))
            return st, alloc

        for t in range(NLT):
            S.dma(hbuf[t * 128:(t + 1) * 128, :], x_in[t * 128:(t + 1) * 128, :], [], ["hbuf"])
        for t in range(NCT):
            S.dma(hbuf[L + t * 128:L + (t + 1) * 128, :], ctx_in[t * 128:(t + 1) * 128, :], [], ["hbuf"])
        st, alloc = phase()
        with st:
            cT = alloc("cT", [128, KD, 2], F32)
            cS = alloc("cS", [128, KD, 2], F32)
            S.dma(cT[:], cT_in[:, :, :], [], ["cT"])
            S.act(cS[:], cT[:], AF.Silu, ["cT"], ["cS"])
            NB6 = 6 * D // 512
            GB = 8
            wr = Ring(alloc, "adaw", [128, GB * 512], F32, 2)
            modsb = alloc("modsb", [2, 6 * D], F32)
            biasb = alloc("biasb", [2, 6 * D], F32)
            for i in range(DEPTH):
                S.dma(biasb[:], ada_b[i:i + 1, :].to_broadcast([2, 6 * D]), [], ["biasb"])
                for g0 in range(0, NB6, GB):
                    nb = min(GB, NB6 - g0)
                    for k in range(KD):
                        wt, wn = wr.next()
                        S.dma(wt[:, 0:nb * 512], ada_w[i, k * 128:(k + 1) * 128, g0 * 512:(g0 + nb) * 512], [], [wn])
                        for b in range(nb):
                            S.mm(ps[0:2, b, :], cS[:, k, :], wt[:, b * 512:(b + 1) * 512], k == 0, k == KD - 1, [wn, "cS"], [PB(b)])
                    for b in range(nb):
                        c0 = (g0 + b) * 512
                        S.tt("dve", modsb[:, c0:c0 + 512], ps[0:2, b, :], biasb[:, c0:c0 + 512], ALU.add, [PB(b), "biasb"], ["modsb"])
                S.dma(moddr[i, :, :], modsb[:], ["modsb"], ["moddr"])
        S.barrier()

        def load_mod(alloc, name, layer, row, idx, gvec=None):
            t = alloc(name, [128, D], F32)
            S.dma(t[:], moddr[layer, row:row + 1, idx * D:(idx + 1) * D].to_broadcast([128, D]), ["moddr"], [name])
            if gvec is not None:
                gt = alloc(name + "_g", [128, D], F32)
                S.dma(gt[:], gvec.to_broadcast([128, D]), [], [name + "_g"])
                S.stt(t[:], t[:], 1.0, gt[:], ALU.add, ALU.mult, [name, name + "_g"], [name])
            return t

        def modulate_phase(layer, which, rows, aT, xn_store):
            st, alloc = phase()
            with st:
                ident = alloc("ident", [128, 128], F32)
                S.dma(ident[:], cst["ident_f"][:, :], [], ["ident"])
                gvec = (nmg if which == 1 else nfg)[layer:layer + 1, :]
                sc_i, sh_i = (1, 0) if which == 1 else (4, 3)
                need_rows = sorted(set(r for r, _, _ in rows))
                Gt, St = {}, {}
                for r in need_rows:
                    Gt[r] = load_mod(alloc, "G%d" % r, layer, r, sc_i, gvec)
                    St[r] = load_mod(alloc, "S%d" % r, layer, r, sh_i)
                hr = Ring(alloc, "hT", [128, D], F32, 2)
                ar = Ring(alloc, "aT_", [128, D], F32, 2)
                junk = alloc("junk", [128, D], BF16)
                sr = Ring(alloc, "ss", [128, 2], F32, 2)
                pb = 0
                for (r, ti, coff) in rows:
                    ht, hn = hr.next()
                    at, an = ar.next()
                    ss, sn = sr.next()
                    S.dma(ht[:], hbuf[ti * 128:(ti + 1) * 128, :], ["hbuf"], [hn])
                    S.act(junk[:], ht[:], AF.Square, [hn], ["junk", sn], accum_out=ss[:, 0:1])
                    S.ts("dve", ss[:, 1:2], ss[:, 0:1], 1.0 / D, RMS_EPS, ALU.mult, ALU.add, [sn], [sn])
                    S.act(ss[:, 1:2], ss[:, 1:2], AF.Sqrt, [sn], [sn])
                    S.op("dve", lambda e, ss=ss: e.reciprocal(ss[:, 1:2], ss[:, 1:2]), [sn], [sn])
                    S.stt(at[:], ht[:], ss[:, 1:2], Gt[r][:], ALU.mult, ALU.mult, [hn, sn, "G%d" % r], [an])
                    S.tt("pool", at[:], at[:], St[r][:], ALU.add, [an, "S%d" % r], [an])
                    if xn_store:
                        S.dma(XN[ti * 128:(ti + 1) * 128, :], at[:], [an], ["XN"], eng="pool")
                    for k0 in range(0, KD, 4):
                        b = pb % 2
                        pb += 1
                        for kk in range(4):
                            k = k0 + kk
                            S.tr(ps[:, b, kk * 128:(kk + 1) * 128], at[:, k * 128:(k + 1) * 128], ident[:], [an, "ident"], [PB(b)])
                        S.cp("act" if (k0 // 4) % 2 == 0 else "dve", aT[:, k0:k0 + 4, coff:coff + 128],
                             ps[:, b, :].rearrange("p (k t) -> p k t", k=4), [PB(b)], ["aT"])

        def outproj_phase(layer, wmat, rows):
            st, alloc = phase()
            with st:
                wob = alloc("wob", [128, KD, D], BF16)
                wst = Ring(alloc, "wst", [128, D], F32, 2)
                for k in range(KD):
                    wt, wn = wst.next()
                    S.dma(wt[:], wmat[k * 128:(k + 1) * 128, :], [], [wn])
                    S.cp("pool" if k % 2 else "act", wob[:, k, :], wt[:], [wn], ["wob"])
                need_rows = sorted(set(r for r, _ in rows))
                Gate = {r: load_mod(alloc, "gate%d" % r, layer, r, 2) for r in need_rows}
                mr = Ring(alloc, "mT", [128, KD, 128], BF16, 2)
                hr = Ring(alloc, "hO", [128, D], F32, 2)
                tmp = Ring(alloc, "tmpO", [128, 512], F32, 2)
                NBK = D // 512
                pbi = 0
                for (r, ti) in rows:
                    mt, mn = mr.next()
                    ht, hn = hr.next()
                    S.dma_k(mt[:], MT[:, ti * 128:(ti + 1) * 128].rearrange("(k p) t -> p k t", p=128), ["MT"], [mn])
                    S.dma(ht[:], hbuf[ti * 128:(ti + 1) * 128, :], ["hbuf"], [hn])
                    for nb in range(NBK):
                        b = pbi % 4
                        pbi += 1
                        for k in range(KD):
                            S.mm(ps[:, b, :], mt[:, k, :], wob[:, k, nb * 512:(nb + 1) * 512], k == 0, k == KD - 1, [mn, "wob"], [PB(b)])
                        tt_, tn = tmp.next()
                        S.tt("dve", tt_[:], ps[:, b, :], Gate[r][:, nb * 512:(nb + 1) * 512], ALU.mult, [PB(b), "gate%d" % r], [tn])
                        S.tt("pool", ht[:, nb * 512:(nb + 1) * 512], ht[:, nb * 512:(nb + 1) * 512], tt_[:], ALU.add, [hn, tn], [hn])
                    S.dma(hbuf[ti * 128:(ti + 1) * 128, :], ht[:], [hn], ["hbuf"], eng="pool")
            S.barrier()

        def attention_phase(layer, j, ctx_q):
            Tq = L + (C if ctx_q else 0)
            st, alloc = phase()
            with st:
                aT = alloc("aT", [128, KD, TP], BF16)
                rows = [(0, t, LOFF + t * 128) for t in range(NLT)] + [(1, NLT + t, COFF + t * 128) for t in range(NCT)]
                modulate_phase(layer, 1, rows, aT, False)
                S.barrier()
                st2, alloc2 = phase()
                with st2:
                    alloc = alloc2
                    identb = alloc("identb", [128, 128], BF16)
                    prot = alloc("prot", [128, 128], F32)
                    cosT = alloc("cosT", [128, L], F32)
                    sinT = alloc("sinT", [128, L], F32)
                    masks = alloc("masks", [128, 3, 384], F32)
                    sinkb = alloc("sinkb", [128, NQ], F32)
                    nsinkb = alloc("nsinkb", [128, NQ], F32)
                    S.dma(identb[:], cst["ident_b"][:, :], [], ["identb"])
                    S.dma(prot[:], cst["prot"][:, :], [], ["prot"])
                    S.dma(cosT[:], cst["cosT"][:, :], [], ["cosT"])
                    S.dma(sinT[:], cst["sinT"][:, :], [], ["sinT"])
                    S.dma(masks[:], cst["masks"][:, :, :], [], ["masks"])
                    S.dma(sinkb[:], sink_in[j:j + 1, :].to_broadcast([128, NQ]), [], ["sinkb"])
                    S.ts("dve", nsinkb[:], sinkb[:], -1.0, None, ALU.mult, None, ["sinkb"], ["nsinkb"])
                    wf = Ring(alloc, "wf", [128, KD, 128], F32, 1)
                    wb = Ring(alloc, "wb", [128, KD, 128], BF16, 2)
                    kT = alloc("kT", [128, T], BF16)
                    vtok = alloc("vtok", [128, NT, 128], BF16)
                    qT = alloc("qT", [128, T], BF16)
                    oT = Ring(alloc, "oT", [128, T], BF16, 2)
                    qf = Ring(alloc, "qf", [128, 512], F32, 2)
                    t1 = Ring(alloc, "t1", [128, 512], F32, 2)
                    t2 = Ring(alloc, "t2", [128, 512], F32, 2)
                    sb_s = Ring(alloc, "s", [128, 640], F32, 2)
                    sb_p = Ring(alloc, "p", [128, 640], BF16, 2)
                    sb_pn = Ring(alloc, "pn", [128, 640], BF16, 2)
                    sb_pT = Ring(alloc, "pT", [128, 5, 128], BF16, 2)
                    stat = Ring(alloc, "stat", [128, 8], F32, 4)
                    scale = 128 ** -0.5

                    def load_w(c0):
                        wft, wfn = wf.next()
                        wbt, wbn = wb.next()
                        S.dma_k(wft[:], w_qkv[j, :, c0:c0 + 128].rearrange("(k p) c -> p k c", p=128), [], [wfn])
                        S.cp("pool", wbt[:], wft[:], [wfn], [wbn])
                        return wbt, wbn

                    def proj_fm(wbt, wbn, dst, dname, rope, ranges):
                        for (acol, dcol, n, pos0) in ranges:
                            for c in range(0, n, 512):
                                w = min(512, n - c)
                                b = 0 if (c // 512) % 2 == 0 else 1
                                for k in range(KD):
                                    S.mm(ps[:, b, 0:w], wbt[:, k, :], aT[:, k, acol + c:acol + c + w], k == 0, k == KD - 1, [wbn, "aT"], [PB(b)])
                                if not rope:
                                    S.cp("act", dst[:, dcol + c:dcol + c + w], ps[:, b, 0:w], [PB(b)], [dname])
                                else:
                                    qft, qfn = qf.next()
                                    t1t, t1n = t1.next()
                                    t2t, t2n = t2.next()
                                    S.cp("act", qft[:, 0:w], ps[:, b, 0:w], [PB(b)], [qfn])
                                    S.mm(ps[:, 2, 0:w], prot[:], qft[:, 0:w], True, True, ["prot", qfn], [PB(2)])
                                    S.tt("dve", t1t[:, 0:w], qft[:, 0:w], cosT[:, pos0 + c:pos0 + c + w], ALU.mult, [qfn, "cosT"], [t1n])
                                    S.tt("dve", t2t[:, 0:w], ps[:, 2, 0:w], sinT[:, pos0 + c:pos0 + c + w], ALU.mult, [PB(2), "sinT"], [t2n])
                                    S.tt("pool", dst[:, dcol + c:dcol + c + w], t1t[:, 0:w], t2t[:, 0:w], ALU.add, [t1n, t2n], [dname])

                    for g in range(NKV):
                        wbt, wbn = load_w((NQ + g) * 128)
                        proj_fm(wbt, wbn, kT, "kT", True, [(LOFF, 0, L, 0)])
                        proj_fm(wbt, wbn, kT, "kT", False, [(COFF, L, C, 0)])
                        wbt, wbn = load_w((NQ + NKV + g) * 128)
                        for ti in range(NT):
                            acol = LOFF + ti * 128 if ti < NLT else COFF + (ti - NLT) * 128
                            for k in range(KD):
                                S.mm(ps[:, 7, 0:128], aT[:, k, acol:acol + 128], wbt[:, k, :], k == 0, k == KD - 1, ["aT", wbn], [PB(7)])
                            S.cp("act", vtok[:, ti, :], ps[:, 7, 0:128], [PB(7)], ["vtok"])
                        for hq in range(GRP):
                            h = g * GRP + hq
                            wbt, wbn = load_w(h * 128)
                            proj_fm(wbt, wbn, qT, "qT", True, [(LOFF, 0, L, 0)])
                            if ctx_q:
                                proj_fm(wbt, wbn, qT, "qT", False, [(COFF, L, C, 0)])
                            ot, on = oT.next()
                            nqb = Tq // 128
                            for qb in range(nqb):
                                is_lat = qb < NLT
                                s_t, s_n = sb_s.next()
                                p_t, p_n = sb_p.next()
                                pn_t, pn_n = sb_pn.next()
                                pT_t, pT_n = sb_pT.next()
                                stt_, stn = stat.next()
                                qsl = qT[:, qb * 128:(qb + 1) * 128]
                                if is_lat:
                                    if qb == 0:
                                        k0, mv = 0, 0
                                    elif qb == NLT - 1:
                                        k0, mv = (NLT - 3) * 128, 2
                                    else:
                                        k0, mv = (qb - 1) * 128, 1
                                    S.mm(ps[:, 3, 0:384], qsl, kT[:, k0:k0 + 384], True, True, ["qT", "kT"], [PB(3)])
                                    S.stt(s_t[:, 0:384], ps[:, 3, 0:384], scale, masks[:, mv, :], ALU.mult, ALU.add, [PB(3), "masks"], [s_n])
                                    lo = 0
                                else:
                                    lo = 384
                                S.mm(ps[:, 4, 0:C], qsl, kT[:, L:L + C], True, True, ["qT", "kT"], [PB(4)])
                                S.act(s_t[:, 384:384 + C], ps[:, 4, 0:C], AF.Copy, [PB(4)], [s_n], scale=scale)
                                hi = 384 + C
                                S.op("dve", lambda e, o=stt_[:, 0:1], i=s_t[:, lo:hi]: e.tensor_reduce(o, i, AX.X, ALU.max, negate=True), [s_n], [stn])
                                S.tt("dve", stt_[:, 1:2], stt_[:, 0:1], nsinkb[:, h:h + 1], ALU.min, [stn, "nsinkb"], [stn])
                                S.act(p_t[:, lo:hi], s_t[:, lo:hi], AF.Exp, [s_n, stn], [p_n, stn], bias=stt_[:, 1:2], accum_out=stt_[:, 2:3])
                                S.act(stt_[:, 3:4], sinkb[:, h:h + 1], AF.Exp, ["sinkb", stn], [stn], bias=stt_[:, 1:2])
                                S.tt("dve", stt_[:, 4:5], stt_[:, 2:3], stt_[:, 3:4], ALU.add, [stn], [stn])
                                S.op("dve", lambda e, o=stt_[:, 5:6], i=stt_[:, 4:5]: e.reciprocal(o, i), [stn], [stn])
                                S.ts("dve", pn_t[:, lo:hi], p_t[:, lo:hi], stt_[:, 5:6], None, ALU.mult, None, [p_n, stn], [pn_n])
                                nch = (hi - lo) // 128
                                c_lo = lo // 128
                                for cc in range(nch):
                                    c = c_lo + cc
                                    S.tr(psb[:, 5, c * 128:(c + 1) * 128], pn_t[:, c * 128:(c + 1) * 128], identb[:], [pn_n, "identb"], [PB(5)])
                                S.cp("act", pT_t[:, c_lo:c_lo + nch, :], psb[:, 5, c_lo * 128:(c_lo + nch) * 128].rearrange("p (c t) -> p c t", t=128), [PB(5)], [pT_n])
                                for cc in range(nch):
                                    c = c_lo + cc
                                    if c < 3:
                                        vt = k0 // 128 + c
                                    else:
                                        vt = NLT + (c - 3)
                                    S.mm(ps[:, 6, 0:128], vtok[:, vt, :], pT_t[:, c, :], cc == 0, cc == nch - 1, ["vtok", pT_n], [PB(6)])
                                S.cp("dve", ot[:, qb * 128:(qb + 1) * 128], ps[:, 6, 0:128], [PB(6)], [on])
                            S.dma(MT[h * 128:(h + 1) * 128, 0:Tq], ot[:, 0:Tq], [on], ["MT"], eng="pool")
            S.barrier()

        def hyena_spectra(j, tag, n):
            nt = n // 128
            st, alloc = phase()
            with st:
                zT = alloc("zT", [HYE, n], F32)
                w1 = alloc("w1", [HYE, HYH], F32)
                w2 = alloc("w2", [HYH, 2, HYH], F32)
                b1 = alloc("b1", [HYH, 1], F32)
                b2 = alloc("b2", [HYH, 2], F32)
                hA = alloc("hA", [HYH, n], F32)
                hB = alloc("hB", [HYH, n], F32)
                nt01 = alloc("nt01", [128, nt], F32)
                S.dma(zT[:], cst["zT" + tag][:, :], [], ["zT"])
                S.dma(w1[:], hy_w1[j, :, :], [], ["w1"])
                for i in range(2):
                    S.dma(w2[:, i, :], hy_w2[j, i, :, :], [], ["w2"])
                    S.dma(b2[:, i:i + 1], hy_b2[j, i, :, :], [], ["b2"])
                S.dma(b1[:], hy_b1[j, :, :], [], ["b1"])
                S.dma(nt01[:], cst["nt01" + tag][:, :], [], ["nt01"])
                pre = Ring(alloc, "pre", [HYH, 512], F32, 2)
                TWO_PI = 2.0 * math.pi

                def sin_layer(lhsT, lname, bias_ap, bname, src, sname, dst, dname, kdim):
                    for c in range(0, n, 512):
                        w = min(512, n - c)
                        b = (c // 512) % 2
                        S.mm(ps[0:HYH, b, 0:w], lhsT, src[0:kdim, c:c + w], True, True, [lname, sname], [PB(b)])
                        pt, pn = pre.next()
                        S.act(pt[:, 0:w], ps[0:HYH, b, 0:w], AF.Identity, [PB(b), bname], [pn], bias=bias_ap)
                        for _ in range(2):
                            S.ts("dve", dst[:, c:c + w], pt[:, 0:w], math.pi, -TWO_PI, ALU.is_gt, ALU.mult, [pn], [dname])
                            S.tt("dve", pt[:, 0:w], pt[:, 0:w], dst[:, c:c + w], ALU.add, [pn, dname], [pn])
                            S.ts("dve", dst[:, c:c + w], pt[:, 0:w], -math.pi, TWO_PI, ALU.is_lt, ALU.mult, [pn], [dname])
                            S.tt("dve", pt[:, 0:w], pt[:, 0:w], dst[:, c:c + w], ALU.add, [pn, dname], [pn])
                        S.act(dst[:, c:c + w], pt[:, 0:w], AF.Sin, [pn], [dname])
                sin_layer(w1[:], "w1", b1[:, 0:1], "b1", zT, "zT", hA, "hA", HYE)
                sin_layer(w2[:, 0, :], "w2", b2[:, 0:1], "b2", hA, "hA", hB, "hB", HYH)
                sin_layer(w2[:, 1, :], "w2", b2[:, 1:2], "b2", hB, "hB", hA, "hA", HYH)
                w3f = Ring(alloc, "w3f", [HYH, 512], F32, 2)
                w3b = Ring(alloc, "w3b", [HYH, 512], F32, 2)
                absd = Ring(alloc, "absd", [128, 512], F32, 2)
                wnd = Ring(alloc, "wnd", [128, 512], F32, 2)
                ff = Ring(alloc, "ff", [128, 512], F32, 2)
                fbk = Ring(alloc, "fbk", [128, 512], F32, 2)
                FS = alloc("FS", [128, nt, 512], BF16)
                FF = alloc("FF", [128, nt, 512], BF16)
                FB = alloc("FB", [128, nt, 512], BF16)
                fre = Ring(alloc, "fre", [128, nt, 128], BF16, 2)
                fim = Ring(alloc, "fim", [128, nt, 128], BF16, 2)
                fimb = Ring(alloc, "fimb", [128, nt, 128], BF16, 2)
                ore = Ring(alloc, "ore", [128, 512], F32, 2)
                oim = Ring(alloc, "oim", [128, 512], F32, 2)
                od0 = alloc("od0", [128, 512], F32)
                for o in range(2):
                    for cb in range(CB):
                        c0 = cb * 512
                        wft, wfn = w3f.next()
                        wbt, wbn = w3b.next()
                        adt, adn = absd.next()
                        S.dma(wft[:], hy_w3[j, :, o * 2 * D + c0:o * 2 * D + c0 + 512], [], [wfn])
                        S.dma(wbt[:], hy_w3[j, :, o * 2 * D + D + c0:o * 2 * D + D + c0 + 512], [], [wbn])
                        S.dma(adt[:], hy_dec[j, o:o + 1, c0:c0 + 512].to_broadcast([128, 512]), [], [adn])
                        S.act(adt[:], adt[:], AF.Abs, [adn], [adn])
                        for lt in range(nt):
                            wt, wn = wnd.next()
                            fft, ffn = ff.next()
                            fbt, fbn = fbk.next()
                            S.mm(ps[:, 0, :], hA[:, lt * 128:(lt + 1) * 128], wft[:], True, True, ["hA", wfn], [PB(0)])
                            S.mm(ps[:, 1, :], hA[:, lt * 128:(lt + 1) * 128], wbt[:], True, True, ["hA", wbn], [PB(1)])
                            S.act(wt[:], adt[:], AF.Exp, [adn, "nt01"], [wn], scale=nt01[:, lt:lt + 1])
                            S.stt(fft[:], wt[:], 0.05, ps[:, 0, :], ALU.add, ALU.mult, [wn, PB(0)], [ffn])
                            S.stt(fbt[:], wt[:], 0.05, ps[:, 1, :], ALU.add, ALU.mult, [wn, PB(1)], [fbn])
                            if lt == 0:
                                S.memset("dve", fbt[0:1, :], 0.0, [fbn])
                            S.tt("pool", FS[:, lt, :], fft[:], fbt[:], ALU.add, [ffn, fbn], ["FS"])
                            S.cp("act", FF[:, lt, :], fft[:], [ffn], ["FF"])
                            S.cp("pool", FB[:, lt, :], fbt[:], [fbn], ["FB"])
                        for kf in range(nt):
                            fr_t, fr_n = fre.next()
                            fi_t, fi_n = fim.next()
                            fb_t, fb_n = fimb.next()
                            S.dma(fr_t[:], cst["Fre" + tag][kf, :, :, :], [], [fr_n])
                            S.dma(fi_t[:], cst["Fim" + tag][kf, :, :, :], [], [fi_n])
                            S.dma(fb_t[:], cst["Fimb" + tag][kf, :, :, :], [], [fb_n])
                            for lt in range(nt):
                                S.mm(ps[:, 2, :], fr_t[:, lt, :], FS[:, lt, :], lt == 0, lt == nt - 1, [fr_n, "FS"], [PB(2)])
                            for lt in range(nt):
                                S.mm(ps[:, 3, :], fi_t[:, lt, :], FF[:, lt, :], lt == 0, False, [fi_n, "FF"], [PB(3)])
                            for lt in range(nt):
                                S.mm(ps[:, 3, :], fb_t[:, lt, :], FB[:, lt, :], False, lt == nt - 1, [fb_n, "FB"], [PB(3)])
                            ot, on = ore.next()
                            it, in_ = oim.next()
                            S.cp("act", ot[:], ps[:, 2, :], [PB(2)], [on])
                            S.cp("dve", it[:], ps[:, 3, :], [PB(3)], [in_])
                            if kf == 0:
                                S.cp("pool", od0[:], ot[:], [on], ["od0"])
                                S.cp("pool", od0[0:1, :], it[0:1, :], [in_, "od0"], ["od0"])
                                S.memset("pool", it[0:1, :], 0.0, [in_])
                                S.dma(spec[(tag, o, "d0")][:, c0:c0 + 512], od0[:], ["od0"], ["spec"])
                            S.dma(spec[(tag, o, "re")][kf * 128:(kf + 1) * 128, c0:c0 + 512], ot[:], [on], ["spec"])
                            S.dma(spec[(tag, o, "im")][kf * 128:(kf + 1) * 128, c0:c0 + 512], it[:], [in_], ["spec"])
            S.barrier()

        def hyena_inproj(layer, j, with_ctx):
            st, alloc = phase()
            with st:
                aT = alloc("aT", [128, KD, TP], BF16)
                S.memset("pool", aT[:, :, 0:1], 0.0, ["aT"])
                S.memset("pool", aT[:, :, L + 1:L + 3], 0.0, ["aT"])
                S.memset("pool", aT[:, :, TP - 1:TP], 0.0, ["aT"])
                rows = [(0, t, LOFF + t * 128) for t in range(NLT)]
                if with_ctx:
                    rows += [(1, NLT + t, COFF + t * 128) for t in range(NCT)]
                modulate_phase(layer, 1, rows, aT, False)
                S.barrier()
                st2, alloc = phase()
                with st2:
                    CW = 256
                    wf = Ring(alloc, "wf", [128, KD, CW], F32, 1)
                    wj = [Ring(alloc, "wj%d" % jj, [128, KD, CW], BF16, 2) for jj in range(3)]
                    cwb = Ring(alloc, "cwb", [128, 3, CW], F32, 2)
                    cbb = Ring(alloc, "cbb", [128, CW], F32, 2)
                    ost = Ring(alloc, "ost", [128, CW], F32, 3)
                    for cb in range(3 * D // CW):
                        c0 = cb * CW
                        wft, wfn = wf.next()
                        cwt, cwn = cwb.next()
                        cbt, cbn = cbb.next()
                        S.dma_k(wft[:], hy_w_in[j, :, c0:c0 + CW].rearrange("(k p) c -> p k c", p=128), [], [wfn])
                        S.dma(cwt[:], hy_cw[j:j + 1, :, c0:c0 + CW].to_broadcast([128, 3, CW]), [], [cwn])
                        S.dma(cbt[:], hy_cb[j:j + 1, c0:c0 + CW].to_broadcast([128, CW]), [], [cbn])
                        wjs = []
                        for jj in range(3):
                            wt, wn = wj[jj].next()
                            S.tt("pool" if jj == 1 else "dve", wt[:], wft[:], cwt[:, jj:jj + 1, :].to_broadcast([128, KD, CW]), ALU.mult, [wfn, cwn], [wn])
                            wjs.append((wt, wn))
                        for (r, ti, coff) in rows:
                            b = ti % 4
                            n_mm = 3 * KD
                            i_mm = 0
                            for jj in range(3):
                                wt, wn = wjs[jj]
                                for k in range(KD):
                                    S.mm(ps[:, b, 0:CW], aT[:, k, coff + jj - 1:coff + jj - 1 + 128], wt[:, k, :], i_mm == 0, i_mm == n_mm - 1, ["aT", wn], [PB(b)])
                                    i_mm += 1
                            ot, on = ost.next()
                            S.tt("dve", ot[:], ps[:, b, 0:CW], cbt[:], ALU.add, [PB(b), cbn], [on])
                            S.dma(U_d[ti * 128:(ti + 1) * 128, c0:c0 + CW], ot[:], [on], ["U_d"], eng="pool")
            S.barrier()

        def hyena_conv(j, tag, n, toff):
            nt = n // 128
            st, alloc = phase()
            with st:
                identb = alloc("identb", [128, 128], BF16)
                S.dma(identb[:], cst["ident_b"][:, :], [], ["identb"])
                vbuf = alloc("vbuf", [128, nt, 512], F32)
                zsb = alloc("zsb", [128, nt, 512], BF16)
                Yre = alloc("Yre", [128, nt, 512], BF16)
                Yim = alloc("Yim", [128, nt, 512], BF16)
                fre = Ring(alloc, "fre", [128, nt, 128], BF16, 2)
                fim = Ring(alloc, "fim", [128, nt, 128], BF16, 2)
                sA = Ring(alloc, "sA", [128, 512], F32, 2)
                sB = Ring(alloc, "sB", [128, 512], F32, 2)
                sD = alloc("sD", [128, 512], F32)
                xt = Ring(alloc, "xt", [128, 512], F32, 2)
                fbb = alloc("fbb", [128, 2, 512], F32)
                e1 = Ring(alloc, "e1", [128, 512], F32, 2)
                e2 = Ring(alloc, "e2", [128, 512], F32, 2)
                e3 = Ring(alloc, "e3", [128, 512], F32, 2)
                e4 = Ring(alloc, "e4", [128, 512], F32, 2)
                yo = Ring(alloc, "yo", [128, 512], BF16, 2)
                yT = Ring(alloc, "yT", [128, 4, 128], BF16, 2)
                for cb in range(CB):
                    c0 = cb * 512
                    S.dma_k(vbuf[:], U_d[toff:toff + n, c0:c0 + 512].rearrange("(t p) c -> p t c", p=128), ["U_d"], ["vbuf"])
                    S.dma(fbb[:], hy_fb[j:j + 1, :, c0:c0 + 512].to_broadcast([128, 2, 512]), [], ["fbb"])
                    for o in range(2):
                        S.cp("act", zsb[:], vbuf[:], ["vbuf"], ["zsb"])
                        for kf in range(nt):
                            fr_t, fr_n = fre.next()
                            fi_t, fi_n = fim.next()
                            S.dma(fr_t[:], cst["Fre" + tag][kf, :, :, :], [], [fr_n])
                            S.dma(fi_t[:], cst["Fim" + tag][kf, :, :, :], [], [fi_n])
                            At, An = sA.next()
                            Bt, Bn = sB.next()
                            S.dma(At[:], spec[(tag, o, "re")][kf * 128:(kf + 1) * 128, c0:c0 + 512], ["spec"], [An])
                            S.dma(Bt[:], spec[(tag, o, "im")][kf * 128:(kf + 1) * 128, c0:c0 + 512], ["spec"], [Bn])
                            Dt, Dn = At, An
                            if kf == 0:
                                S.dma(sD[:], spec[(tag, o, "d0")][:, c0:c0 + 512], ["spec"], ["sD"])
                                Dt, Dn = sD, "sD"
                            br, bi = (0, 2) if kf % 2 == 0 else (1, 3)
                            for lt in range(nt):
                                S.mm(ps[:, br, :], fr_t[:, lt, :], zsb[:, lt, :], lt == 0, lt == nt - 1, [fr_n, "zsb"], [PB(br)])
                            for lt in range(nt):
                                S.mm(ps[:, bi, :], fi_t[:, lt, :], zsb[:, lt, :], lt == 0, lt == nt - 1, [fi_n, "zsb"], [PB(bi)])
                            a1, n1 = e1.next()
                            a2, n2 = e2.next()
                            a3, n3 = e3.next()
                            a4, n4 = e4.next()
                            S.tt("dve", a1[:], ps[:, br, :], At[:], ALU.mult, [PB(br), An], [n1])
                            S.tt("dve", a2[:], ps[:, bi, :], Bt[:], ALU.mult, [PB(bi), Bn], [n2])
                            S.tt("pool", Yre[:, kf, :], a1[:], a2[:], ALU.subtract, [n1, n2], ["Yre"])
                            S.tt("dve", a3[:], ps[:, br, :], Bt[:], ALU.mult, [PB(br), Bn], [n3])
                            S.tt("dve", a4[:], ps[:, bi, :], Dt[:], ALU.mult, [PB(bi), Dn], [n4])
                            S.tt("pool", Yim[:, kf, :], a3[:], a4[:], ALU.add, [n3, n4], ["Yim"])
                        for tt_i in range(nt):
                            gr_t, gr_n = fre.next()
                            gi_t, gi_n = fim.next()
                            S.dma(gr_t[:], cst["Gre" + tag][tt_i, :, :, :], [], [gr_n])
                            S.dma(gi_t[:], cst["Gim" + tag][tt_i, :, :, :], [], [gi_n])
                            x_t, x_n = xt.next()
                            S.dma(x_t[:], U_d[toff + tt_i * 128:toff + (tt_i + 1) * 128, (o + 1) * D + c0:(o + 1) * D + c0 + 512], ["U_d"], [x_n])
                            b = 4 + tt_i % 2
                            for kf in range(nt):
                                S.mm(ps[:, b, :], gr_t[:, kf, :], Yre[:, kf, :], kf == 0, False, [gr_n, "Yre"], [PB(b)])
                            for kf in range(nt):
                                S.mm(ps[:, b, :], gi_t[:, kf, :], Yim[:, kf, :], False, kf == nt - 1, [gi_n, "Yim"], [PB(b)])
                            a1, n1 = e1.next()
                            a2, n2 = e2.next()
                            S.tt("pool", a1[:], vbuf[:, tt_i, :], fbb[:, o, :], ALU.mult, ["vbuf", "fbb"], [n1])
                            S.tt("dve", a2[:], ps[:, b, :], a1[:], ALU.add, [PB(b), n1], [n2])
                            if o == 0:
                                S.tt("dve", vbuf[:, tt_i, :], a2[:], x_t[:], ALU.mult, [n2, x_n], ["vbuf"])
                            else:
                                y_t, y_n = yo.next()
                                yT_t, yT_n = yT.next()
                                S.tt("dve", y_t[:], a2[:], x_t[:], ALU.mult, [n2, x_n], [y_n])
                                for q4 in range(4):
                                    S.tr(psb[:, 6, q4 * 128:(q4 + 1) * 128], y_t[:, q4 * 128:(q4 + 1) * 128], identb[:], [y_n, "identb"], [PB(6)])
                                S.cp("act", yT_t[:], psb[:, 6, 0:512].rearrange("p (c t) -> p c t", t=128), [PB(6)], [yT_n])
                                S.dma(MT[c0:c0 + 512, toff + tt_i * 128:toff + (tt_i + 1) * 128].rearrange("(c p) t -> p c t", p=128), yT_t[:], [yT_n], ["MT"], eng="pool")
            S.barrier()

        def peer_phase(layer, with_ctx):
            rows = [(0, t) for t in range(NLT)] + ([(1, NLT + t) for t in range(NCT)] if with_ctx else [])
            st, alloc = phase()
            with st:
                xnT = alloc("aT", [128, KD, T], BF16)
                modulate_phase(layer, 2, [(r, ti, ti * 128) for (r, ti) in rows], xnT, True)
                S.barrier()
                st2, alloc = phase()
                with st2:
                    wf = Ring(alloc, "wf", [128, KD, 128], F32, 2)
                    wb = Ring(alloc, "wb", [128, KD, 128], BF16, 2)
                    qo = Ring(alloc, "qo", [128, T], BF16, 2)
                    ntok = len(rows) * 128
                    for c in range(16):
                        wft, wfn = wf.next()
                        wbt, wbn = wb.next()
                        S.dma_k(wft[:], p_wq[layer, :, c * 128:(c + 1) * 128].rearrange("(k p) c -> p k c", p=128), [], [wfn])
                        S.cp("pool", wbt[:], wft[:], [wfn], [wbn])
                        qt, qn = qo.next()
                        for t0 in range(0, ntok, 512):
                            w = min(512, ntok - t0)
                            b = (t0 // 512) % 4
                            for k in range(KD):
                                S.mm(ps[:, b, 0:w], wbt[:, k, :], xnT[:, k, t0:t0 + w], k == 0, k == KD - 1, [wbn, "aT"], [PB(b)])
                            S.cp("act" if (t0 // 512) % 2 == 0 else "dve", qt[:, t0:t0 + w], ps[:, b, 0:w], [PB(b)], [qn])
                        S.dma(QT[c, :, 0:ntok], qt[:, 0:ntok], [qn], ["QT"], eng="pool")
            S.barrier()
            st, alloc = phase()
            with st:
                keyf = alloc("keyf", [128, 16, 128], F32)
                keyb = alloc("keyb", [128, 16, 128], BF16)
                iota = alloc("iota", [128, 2048], F32)
                identb = alloc("identb", [128, 128], BF16)
                S.dma(keyf[:], p_keysT[layer, :, :, :], [], ["keyf"])
                S.cp("act", keyb[:], keyf[:], ["keyf"], ["keyb"])
                S.dma(iota[:], cst["iota16"][:, :], [], ["iota"])
                S.dma(identb[:], cst["ident_b"][:, :], [], ["identb"])
                Gate = {r: load_mod(alloc, "gate%d" % r, layer, r, 5) for r in sorted(set(r for r, _ in rows))}
                qtt = Ring(alloc, "qtt", [128, 16, 128], BF16, 2)
                xnr = Ring(alloc, "xn", [128, D], F32, 2)
                hr = Ring(alloc, "hP", [128, D], F32, 1)
                uvr = Ring(alloc, "uv", [128, 2 * D], BF16, 6)
                dgr = Ring(alloc, "dg", [128, 128], BF16, 4)
                tmpr = Ring(alloc, "tmpP", [128, 512], F32, 2)
                sc = alloc("sc", [128, 16, 128], F32)
                sc2 = alloc("sc2", [128, 16, 128], F32)
                m16 = alloc("m16", [128, 16, 16], F32)
                i16 = alloc("i16", [128, 16, 16], U32)
                i16f = alloc("i16f", [128, 16, 16], F32)
                cand = alloc("cand", [128, PH, 256], F32)
                cand2 = alloc("cand2", [128, PH, 256], F32)
                tops = alloc("tops", [128, PH, 16], F32)
                posu = alloc("posu", [128, PH, 16], U32)
                pa = alloc("pa", [128, PH, 16], U32)
                pbq = alloc("pbq", [128, PH, 16], U32)
                paf = alloc("paf", [128, PH, 16], F32)
                pbf = alloc("pbf", [128, PH, 16], F32)
                oh = alloc("oh", [128, PH, 16, 16], F32)
                oh2 = alloc("oh2", [128, PH, 16, 16], F32)
                i1s = alloc("i1s", [128, PH, 16], F32)
                i2s = alloc("i2s", [128, PH, 16], F32)
                idf = alloc("idf", [128, 128], F32)
                idir = Ring(alloc, "idi", [128, 128], I32, 2)
                gsmr = Ring(alloc, "gsm", [128, PH, 16], F32, 2)
                gst = alloc("gst", [128, PH, 2], F32)
                aprer = Ring(alloc, "apre", [128, 128], F32, 2)
                glr = Ring(alloc, "gl", [128, 128], F32, 2)
                base = 0
                NBK = D // 512

                def routing(ti, idi, idn, gsm, gsn):
                    q_t, q_n = qtt.next()
                    S.dma_k(q_t[:], QT[:, :, ti * 128:(ti + 1) * 128].rearrange("c p t -> p c t"), ["QT"], [q_n])
                    yield
                    for c in range(16):
                        b = c // 4
                        S.mm(ps[:, b, (c % 4) * 128:(c % 4 + 1) * 128], q_t[:, c, :], keyb[:, c, :], True, True, [q_n, "keyb"], [PB(b)])
                        yield
                    for b in range(4):
                        S.cp("act", sc[:, b * 4:(b + 1) * 4, :], ps[:, b, :].rearrange("p (c k) -> p c k", k=128), [PB(b)], ["sc"])
                        yield
                    for c in range(16):
                        S.op("dve", lambda e, c=c: e.max(m16[:, c, 0:8], sc[:, c, :]), ["sc"], ["m16"])
                        yield
                        S.op("dve", lambda e, c=c: e.max_index(i16[:, c, 0:8], m16[:, c, 0:8], sc[:, c, :]), ["sc", "m16"], ["i16"])
                        yield
                        S.op("dve", lambda e, c=c: e.match_replace(sc2[:, c, :], m16[:, c, 0:8], sc[:, c, :], NEG), ["sc", "m16"], ["sc2"])
                        yield
                        S.op("dve", lambda e, c=c: e.max(m16[:, c, 8:16], sc2[:, c, :]), ["sc2"], ["m16"])
                        yield
                        S.op("dve", lambda e, c=c: e.max_index(i16[:, c, 8:16], m16[:, c, 8:16], sc2[:, c, :]), ["sc2", "m16"], ["i16"])
                        yield
                    S.cp("dve", i16f[:], i16[:], ["i16"], ["i16f"])
                    yield
                    m4 = m16[:].rearrange("p (h j) a -> p h j a", j=2)
                    i4 = i16f[:].rearrange("p (h j) a -> p h j a", j=2)
                    for h in range(PH):
                        S.tt("dve", cand[:, h, :].rearrange("p (a b) -> p a b", b=16),
                             m4[:, h, 0, :].unsqueeze(2).to_broadcast([128, 16, 16]),
                             m4[:, h, 1, :].unsqueeze(1).to_broadcast([128, 16, 16]), ALU.add, ["m16"], ["cand"])
                        yield
                    for h in range(PH):
                        S.op("dve", lambda e, h=h: e.max(tops[:, h, 0:8], cand[:, h, :]), ["cand"], ["tops"])
                        yield
                        S.op("dve", lambda e, h=h: e.max_index(posu[:, h, 0:8], tops[:, h, 0:8], cand[:, h, :]), ["cand", "tops"], ["posu"])
                        yield
                        S.op("dve", lambda e, h=h: e.match_replace(cand2[:, h, :], tops[:, h, 0:8], cand[:, h, :], NEG), ["cand", "tops"], ["cand2"])
                        yield
                        S.op("dve", lambda e, h=h: e.max(tops[:, h, 8:16], cand2[:, h, :]), ["cand2"], ["tops"])
                        yield
                        S.op("dve", lambda e, h=h: e.max_index(posu[:, h, 8:16], tops[:, h, 8:16], cand2[:, h, :]), ["cand2", "tops"], ["posu"])
                        yield
                    S.op("dve", lambda e: e.tensor_single_scalar(pa[:], posu[:], 4, ALU.logical_shift_right), ["posu"], ["pa"])
                    yield
                    S.op("dve", lambda e: e.tensor_single_scalar(pbq[:], posu[:], 15, ALU.bitwise_and), ["posu"], ["pbq"])
                    yield
                    S.cp("dve", paf[:], pa[:], ["pa"], ["paf"])
                    yield
                    S.cp("dve", pbf[:], pbq[:], ["pbq"], ["pbf"])
                    yield
                    iota4 = iota[:].rearrange("p (h k a) -> p h k a", h=PH, k=16)
                    for (pf, pfn, jj, dst, dn) in ((paf, "paf", 0, i1s, "i1s"), (pbf, "pbf", 1, i2s, "i2s")):
                        S.tt("dve", oh[:], pf[:].unsqueeze(3).to_broadcast([128, PH, 16, 16]), iota4, ALU.is_equal, [pfn, "iota"], ["oh"])
                        yield
                        S.tt("dve", oh2[:], oh[:], i4[:, :, jj, :].unsqueeze(2).to_broadcast([128, PH, 16, 16]), ALU.mult, ["oh", "i16f"], ["oh2"])
                        yield
                        S.op("dve", lambda e, dst=dst: e.tensor_reduce(dst[:], oh2[:], AX.X, ALU.add), ["oh2"], [dn])
                        yield
                    S.stt(idf[:], i1s[:].rearrange("p h k -> p (h k)"), 128.0, i2s[:].rearrange("p h k -> p (h k)"), ALU.mult, ALU.add, ["i1s", "i2s"], ["idf"])
                    yield
                    S.ts("dve", idf[:], idf[:], float(base), None, ALU.add, None, ["idf"], ["idf"])
                    yield
                    S.cp("dve", idi[:], idf[:], ["idf"], [idn])
                    yield
                    S.tt("dve", gsm[:], tops[:], tops[:, :, 0:1].to_broadcast([128, PH, 16]), ALU.subtract, ["tops"], [gsn])
                    yield
                    S.act(gsm[:], gsm[:], AF.Exp, [gsn], [gsn])
                    yield
                    S.op("dve", lambda e: e.tensor_reduce(gst[:, :, 0], gsm[:], AX.X, ALU.add), [gsn], ["gst"])
                    yield
                    S.op("dve", lambda e: e.reciprocal(gst[:, :, 1], gst[:, :, 0]), ["gst"], ["gst"])
                    yield
                    S.tt("dve", gsm[:], gsm[:], gst[:, :, 1:2].to_broadcast([128, PH, 16]), ALU.mult, [gsn, "gst"], [gsn])
                    yield

                def start_routing(ti):
                    idi, idn = idir.next()
                    gsm, gsn = gsmr.next()
                    return routing(ti, idi, idn, gsm, gsn), (idi, idn, gsm, gsn)

                def drain(gen, n=None):
                    k = 0
                    for _ in gen:
                        k += 1
                        if n is not None and k >= n:
                            return

                gen, cur = start_routing(rows[0][1])
                drain(gen)
                for ri, (r, ti) in enumerate(rows):
                    idi, idn, gsm, gsn = cur
                    nxt_gen, nxt = (None, None)
                    if ri + 1 < len(rows):
                        nxt_gen, nxt = start_routing(rows[ri + 1][1])
                    x_t, x_n = xnr.next()
                    S.dma(x_t[:], XN[ti * 128:(ti + 1) * 128, :], ["XN"], [x_n])
                    apre, apn = aprer.next()
                    gl, gln = glr.next()
                    gflat = gsm[:].rearrange("p h k -> p (h k)")
                    for s_ in range(128):
                        uv, uvn = uvr.next()
                        dg, dgn = dgr.next()
                        S.op("pool", lambda e, uv=uv, s_=s_, idi=idi: e.indirect_dma_start(
                            out=uv[:], out_offset=None, in_=UVs[layer][:, :],
                            in_offset=bass.IndirectOffsetOnAxis(ap=idi[:, s_:s_ + 1], axis=0)), [idn], [uvn], dma=True)
                        an = "%s_%d" % (apn, s_ % 8)
                        S.stt(uv[:, 0:D], uv[:, 0:D], 1.0, x_t[:], ALU.mult, ALU.mult, [uvn, x_n], [uvn + "u", an], accum_out=apre[:, s_:s_ + 1])
                        S.act(gl[:, s_:s_ + 1], apre[:, s_:s_ + 1], AF.Gelu, [an], [an + "g"])
                        S.act(gl[:, s_:s_ + 1], gl[:, s_:s_ + 1], AF.Copy, [an + "g", gsn], [an + "g"], scale=gflat[:, s_:s_ + 1])
                        S.act(dg[:], identb[:], AF.Copy, ["identb", an + "g"], [dgn], scale=gl[:, s_:s_ + 1])
                        for nb in range(NBK):
                            S.mm(ps[:, 4 + nb, :], dg[:], uv[:, D + nb * 512:D + (nb + 1) * 512], s_ == 0, s_ == 127, [dgn, uvn], [PB(4 + nb)])
                        if nxt_gen is not None:
                            drain(nxt_gen, 2)
                    if nxt_gen is not None:
                        drain(nxt_gen)
                    h_t, h_n = hr.next()
                    S.dma(h_t[:], hbuf[ti * 128:(ti + 1) * 128, :], ["hbuf"], [h_n])
                    for nb in range(NBK):
                        tt_, tn = tmpr.next()
                        S.tt("dve", tt_[:], ps[:, 4 + nb, :], Gate[r][:, nb * 512:(nb + 1) * 512], ALU.mult, [PB(4 + nb), "gate%d" % r], [tn])
                        S.tt("dve", h_t[:, nb * 512:(nb + 1) * 512], h_t[:, nb * 512:(nb + 1) * 512], tt_[:], ALU.add, [h_n, tn], [h_n])
                    S.dma(hbuf[ti * 128:(ti + 1) * 128, :], h_t[:], [h_n], ["hbuf"], eng="act")
                    cur = nxt
            S.barrier()

        def convert_tables():
            st, alloc = phase()
            with st:
                fu = Ring(alloc, "cvu", [128, D], F32, 3)
                fv = Ring(alloc, "cvv", [128, D], F32, 3)
                ob = Ring(alloc, "cvo", [128, 2 * D], BF16, 3)
                engs = ["act", "dve", "pool"]
                prev = None
                for ch in range(DEPTH * NE // 128):
                    r0 = ch * 128
                    ut, un = fu.next()
                    vt, vn = fv.next()
                    ot, on = ob.next()
                    S.dma(ut[:], p_u[r0:r0 + 128, :], [], [un])
                    S.dma(vt[:], p_v[r0:r0 + 128, :], [], [vn])
                    if prev is not None:
                        S.dma(UVs[prev[0] // NE][prev[0] % NE:prev[0] % NE + 128, :], prev[1][:], [prev[2]], ["UV"])
                    S.cp(engs[ch % 3], ot[:, 0:D], ut[:], [un], [on])
                    S.cp(engs[(ch + 1) % 3], ot[:, D:2 * D], vt[:], [vn], [on])
                    prev = (r0, ot, on)
                S.dma(UVs[prev[0] // NE][prev[0] % NE:prev[0] % NE + 128, :], prev[1][:], [prev[2]], ["UV"])
            S.barrier()

        convert_tables()
        for i in range(DEPTH):
            is_attn = (i % 2) == 0
            j = i // 2
            ctx_update = i < last_attn
            lat_rows = [(0, t) for t in range(NLT)]
            ctx_rows = [(1, NLT + t) for t in range(NCT)]
            if is_attn:
                attention_phase(i, j, ctx_update)
                outproj_phase(i, w_o[j], lat_rows + (ctx_rows if ctx_update else []))
            else:
                hyena_spectra(j, "L", L)
                hyena_inproj(i, j, ctx_update)
                hyena_conv(j, "L", L, 0)
                if ctx_update:
                    hyena_spectra(j, "C", C)
                    hyena_conv(j, "C", C, L)
                outproj_phase(i, hy_wout[j], lat_rows + (ctx_rows if ctx_update else []))
            if stop_after == "mix%d" % i:
                break
            peer_phase(i, ctx_update)
            if stop_after == "ffn%d" % i:
                break

        st, alloc = phase()
        with st:
            if stop_after is None:
                gt = alloc("fg", [128, D], F32)
                S.dma(gt[:], fing[0:1, :].to_broadcast([128, D]), [], ["fg"])
            hr = Ring(alloc, "hF", [128, D], F32, 2)
            orr = Ring(alloc, "oF", [128, D], F32, 2)
            junk = alloc("junkF", [128, D], BF16)
            sr = Ring(alloc, "ssF", [128, 2], F32, 2)
            for t in range(NLT):
                ht, hn = hr.next()
                S.dma(ht[:], hbuf[t * 128:(t + 1) * 128, :], ["hbuf"], [hn])
                if stop_after is None:
                    ot, on = orr.next()
                    ss, sn = sr.next()
                    S.act(junk[:], ht[:], AF.Square, [hn], ["junkF", sn], accum_out=ss[:, 0:1])
                    S.ts("dve", ss[:, 1:2], ss[:, 0:1], 1.0 / D, RMS_EPS, ALU.mult, ALU.add, [sn], [sn])
                    S.act(ss[:, 1:2], ss[:, 1:2], AF.Sqrt, [sn], [sn])
                    S.op("dve", lambda e, ss=ss: e.reciprocal(ss[:, 1:2], ss[:, 1:2]), [sn], [sn])
                    S.stt(ot[:], ht[:], ss[:, 1:2], gt[:], ALU.mult, ALU.mult, [hn, sn, "fg"], [on])
                    S.dma(out_d[t * 128:(t + 1) * 128, :], ot[:], [on], ["out"], eng="pool")
                else:
                    S.dma(out_d[t * 128:(t + 1) * 128, :], ht[:], [hn], ["out"], eng="pool")
        S.barrier()
        S.emit()
        nops = S.nops
    return nc, nops


def make_in_maps(inputs, cfg, cores):
    D, L, C = cfg["D"], cfg["L"], cfg["C"]
    KD = D // 128
    f = lambda a: np.ascontiguousarray(np.asarray(a, dtype=np.float32))
    shared = {}
    for k in ("ada_w", "ada_b", "norm_mix_g", "norm_ffn_g", "attn_w_qkv", "attn_w_o", "attn_sink", "hy_w_in",
              "hy_conv_w", "hy_conv_b", "hy_f_w1", "hy_f_w2", "hy_f_w3", "hy_decay", "hy_fbias", "hy_w_out", "peer_w_q"):
        shared[k] = f(inputs[k])
    shared["final_g"] = f(inputs["final_g"]).reshape(1, D)
    shared["hy_f_b1"] = f(inputs["hy_f_b1"])[..., None]
    shared["hy_f_b2"] = f(inputs["hy_f_b2"])[..., None]
    k1 = f(inputs["peer_keys1"])
    k2 = f(inputs["peer_keys2"])
    ks = np.stack([k1, k2], axis=2)
    shared["peer_keysT"] = np.ascontiguousarray(ks.transpose(0, 4, 1, 2, 3).reshape(ks.shape[0], 128, 16, 128))
    shared["peer_u"] = f(inputs["peer_u"]).reshape(-1, D)
    shared["peer_v"] = f(inputs["peer_v"]).reshape(-1, D)
    shared.update(host_consts(cfg))
    c_ctx = f(inputs["c_ctx"])
    maps = []
    for b in cores:
        m = dict(shared)
        m["x"] = f(inputs["x"][b])
        m["ctx"] = f(inputs["ctx"][b])
        cb = f(inputs["c"][b])
        cT = np.stack([cb.reshape(KD, 128).T, c_ctx.reshape(KD, 128).T], axis=-1)
        m["cT"] = np.ascontiguousarray(cT)
        maps.append(m)
    return maps


_NC_CACHE = {}


def kernel(**inputs):
    cfg = CFG
    key = (cfg["D"], cfg["L"], cfg["C"])
    if key not in _NC_CACHE:
        _NC_CACHE[key] = build_program(cfg)[0]
    nc = _NC_CACHE[key]
    B = inputs["x"].shape[0]
    maps = make_in_maps(inputs, cfg, list(range(B)))
    res = run_bass_kernel_spmd(nc, maps, core_ids=list(range(B)))
    return np.stack([np.asarray(r["out"], dtype=np.float32) for r in res.results], axis=0)
```

```python
import math
from contextlib import ExitStack
import numpy as np
import ml_dtypes
import concourse.bass as bass
import concourse.mybir as mybir
from concourse.bass_utils import run_bass_kernel_spmd

F32 = mybir.dt.float32
BF16 = mybir.dt.bfloat16
I32 = mybir.dt.int32
U32 = mybir.dt.uint32
ALU = mybir.AluOpType
AF = mybir.ActivationFunctionType
AX = mybir.AxisListType

ENGS = ("pe", "act", "dve", "pool", "sp")
N_DMA_SEMS = 56
EPOCH_N = 12000

CFG = dict(D=2048, L=2048, C=256, DEPTH=4, NCORES=8)
NEG = -1e30
RMS_EPS = 1e-6
NKV = 4
PH, PK, TOPK, DQ = 8, 128, 16, 2048
NE = PK * PK
HYH = 64
HYE = 33


class Sched:
    def __init__(self, nc, stack):
        self.nc = nc
        self.stack = stack
        self.epoch = {e: 0 for e in ENGS}
        self.ops = {e: [] for e in ENGS}
        self.cnt = {e: 0 for e in ENGS}
        self.seen = {e: {} for e in ENGS}
        self.res = {}
        self.sems = {}
        for e in ENGS:
            self.sems[(e, 0)] = stack.enter_context(nc.semaphore("s_" + e))
        self.dma_cnt = [0] * N_DMA_SEMS
        self.dma_rr = 0
        for j in range(N_DMA_SEMS):
            self.sems[("d", j)] = stack.enter_context(nc.semaphore("s_d%d" % j))
        self.nops = 0

    def _need(self, eng, tok, waits):
        if tok is None:
            return
        k, v = tok
        if k[0] == eng and eng == "pe":
            return
        if self.seen[eng].get(k, 0) >= v:
            return
        self.seen[eng][k] = v
        waits.append((k, v))

    def op(self, eng, fn, reads=(), writes=(), dma=False):
        waits = []
        for r in reads:
            st = self.res.get(r)
            if st is not None:
                self._need(eng, st[0], waits)
        for w in writes:
            st = self.res.get(w)
            if st is not None:
                self._need(eng, st[0], waits)
                for t in st[1].items():
                    self._need(eng, t, waits)
        if dma:
            j = self.dma_rr
            self.dma_rr = (self.dma_rr + 1) % N_DMA_SEMS
            if self.dma_cnt[j] > 0:
                self._need(eng, (("d", j), 16 * self.dma_cnt[j]), waits)
            self.dma_cnt[j] += 1
            tok = (("d", j), 16 * self.dma_cnt[j])
            inc = 16
        else:
            self.cnt[eng] += 1
            tok = ((eng, self.epoch[eng]), self.cnt[eng])
            inc = 1
        k, v = tok
        for r in reads:
            st = self.res.setdefault(r, [None, {}])
            if st[1].get(k, 0) < v:
                st[1][k] = v
        for w in writes:
            self.res[w] = [tok, {}]
        self.ops[eng].append((waits, fn, tok[0], inc))
        self.nops += 1
        return tok

    def barrier(self):
        toks = [((e, self.epoch[e]), self.cnt[e]) for e in ENGS if self.cnt[e] > 0]
        toks += [(("d", j), 16 * self.dma_cnt[j]) for j in range(N_DMA_SEMS) if self.dma_cnt[j] > 0]
        for e in ENGS:
            waits = []
            for t in toks:
                self._need(e, t, waits)
            self.ops[e].append((waits, None, None, 0))
        self.res = {}
        for e in ENGS:
            if self.cnt[e] > EPOCH_N:
                self.epoch[e] += 1
                self.cnt[e] = 0
                self.sems[(e, self.epoch[e])] = self.stack.enter_context(
                    self.nc.semaphore("s_%s_%d" % (e, self.epoch[e])))

    def emit(self):
        nc = self.nc
        sems = self.sems
        with nc.Block() as block:
            def mk(engname):
                lst = self.ops[engname]

                def body(e):
                    for waits, fn, semkey, inc in lst:
                        for k, v in waits:
                            e.wait_ge(sems[k], v)
                        if fn is not None:
                            fn(e).then_inc(sems[semkey], inc)
                return body
            block.tensor(mk("pe"))
            block.scalar(mk("act"))
            block.vector(mk("dve"))
            block.gpsimd(mk("pool"))
            block.sync(mk("sp"))

    def dma(self, out, in_, r, w, eng="sp"):
        self.op(eng, lambda e: e.dma_start(out=out, in_=in_), r, w, dma=True)

    def dma_k(self, out, in_, r, w, eng="sp", g=4):
        K_ = out.shape[1]
        for k0 in range(0, K_, g):
            k1 = min(K_, k0 + g)
            self.dma(out[:, k0:k1, :], in_[:, k0:k1, :], r, w, eng)

    def mm(self, out, lhsT, rhs, start, stop, r, w):
        self.op("pe", lambda e: e.matmul(out, lhsT, rhs, start=start, stop=stop), r, w)

    def tr(self, out, in_, ident, r, w):
        self.op("pe", lambda e: e.transpose(out, in_, ident), r, w)

    def act(self, out, in_, func, r, w, bias=None, scale=None, accum_out=None):
        kw = {}
        if bias is not None:
            kw["bias"] = bias
        if scale is not None:
            kw["scale"] = scale
        if accum_out is not None:
            kw["accum_out"] = accum_out
        self.op("act", lambda e: e.activation(out, in_, func, **kw), r, w)

    def tt(self, eng, out, in0, in1, op, r, w):
        self.op(eng, lambda e: e.tensor_tensor(out, in0, in1, op), r, w)

    def ts(self, eng, out, in0, s1, s2, op0, op1, r, w):
        if op1 is None:
            self.op(eng, lambda e: e.tensor_scalar(out, in0, s1, None, op0), r, w)
        else:
            self.op(eng, lambda e: e.tensor_scalar(out, in0, s1, s2, op0, op1), r, w)

    def stt(self, out, in0, scalar, in1, op0, op1, r, w, accum_out=None):
        if accum_out is None:
            self.op("dve", lambda e: e.scalar_tensor_tensor(out, in0, scalar, in1, op0, op1), r, w)
        else:
            self.op("dve", lambda e: e.scalar_tensor_tensor(out, in0, scalar, in1, op0, op1, accum_out=accum_out), r, w)

    def cp(self, eng, out, in_, r, w):
        if eng == "act":
            self.op("act", lambda e: e.copy(out, in_), r, w)
        else:
            self.op(eng, lambda e: e.tensor_copy(out, in_), r, w)

    def memset(self, eng, ap, val, w):
        self.op(eng, lambda e: e.memset(ap, val), (), w)


class Ring:
    def __init__(self, alloc, name, shape, dt, n):
        self.t = [alloc(name + str(i), shape, dt) for i in range(n)]
        self.names = [name + str(i) for i in range(n)]
        self.i = -1

    def next(self):
        self.i = (self.i + 1) % len(self.t)
        return self.t[self.i], self.names[self.i]


def host_consts(cfg):
    D, L, C = cfg["D"], cfg["L"], cfg["C"]
    k = {}
    k["ident_f"] = np.eye(128, dtype=np.float32)
    k["ident_b"] = np.eye(128).astype(ml_dtypes.bfloat16)
    P = np.zeros((128, 128), np.float32)
    for j in range(32):
        P[32 + j, j] = -1.0
        P[j, 32 + j] = 1.0
        P[96 + j, 64 + j] = -1.0
        P[64 + j, 96 + j] = 1.0
    k["prot"] = P
    GW = 64
    rows = L // GW
    row = np.repeat(np.arange(rows), GW).astype(np.float32)
    col = np.tile(np.arange(GW), rows).astype(np.float32)
    inv = (10000.0 ** (-np.arange(32, dtype=np.float32) / 32)).astype(np.float32)
    ar, ac = row[:, None] * inv, col[:, None] * inv
    ang = np.concatenate([ar, ar, ac, ac], axis=-1)
    k["cosT"] = np.ascontiguousarray(np.cos(ang).T.astype(np.float32))
    k["sinT"] = np.ascontiguousarray(np.sin(ang).T.astype(np.float32))
    q = np.arange(128)[:, None]
    kk = np.arange(384)[None, :]
    masks = np.zeros((128, 3, 384), np.float32)
    for v, off in enumerate((0, 128, 256)):
        masks[:, v, :] = np.where(np.abs(kk - (q + off)) <= 128, 0.0, NEG)
    k["masks"] = masks
    io = np.tile(np.arange(16, dtype=np.float32), 128)[None, :].repeat(128, 0)
    k["iota16"] = np.ascontiguousarray(io)
    for n, tag in ((L, "L"), (C, "C")):
        t = np.arange(n, dtype=np.float64)
        f = np.arange(n, dtype=np.float64)
        th = 2.0 * np.pi * np.outer(t, f) / (2 * n)
        Fre = np.cos(th)
        Fim = -np.sin(th)
        Fim[:, 0] = np.cos(np.pi * t)
        Fimb = np.sin(th)
        Fimb[:, 0] = np.cos(np.pi * t)
        Gre = (2.0 / (2 * n)) * np.cos(th.T)
        Gre[0, :] = 1.0 / (2 * n)
        Gim = -(2.0 / (2 * n)) * np.sin(th.T)
        Gim[0, :] = np.cos(np.pi * t) / (2 * n)
        nt = n // 128

        def fwd_layout(M):
            return np.ascontiguousarray(M.reshape(nt, 128, nt, 128).transpose(2, 1, 0, 3)).astype(ml_dtypes.bfloat16)

        def inv_layout(M):
            return np.ascontiguousarray(M.reshape(nt, 128, nt, 128).transpose(2, 1, 0, 3)).astype(ml_dtypes.bfloat16)
        k["Fre" + tag] = fwd_layout(Fre)
        k["Fim" + tag] = fwd_layout(Fim)
        k["Fimb" + tag] = fwd_layout(Fimb)
        k["Gre" + tag] = inv_layout(Gre)
        k["Gim" + tag] = inv_layout(Gim)
        tt = np.arange(n, dtype=np.float32)
        t01 = tt / n
        bands = np.arange(1, 17, dtype=np.float32)
        a = 2.0 * math.pi * tt[:, None] * bands[None, :] / n
        z = np.concatenate([t01[:, None], np.cos(a), np.sin(a)], axis=-1).astype(np.float32)
        k["zT" + tag] = np.ascontiguousarray(z.T)
        k["nt01" + tag] = np.ascontiguousarray((-t01).reshape(nt, 128).T.astype(np.float32))
    return k


def build_program(cfg, stop_after=None):
    D, L, C, DEPTH = cfg["D"], cfg["L"], cfg["C"], cfg["DEPTH"]
    KD = D // 128
    T = L + C
    NLT, NCT, NT = L // 128, C // 128, (L + C) // 128
    NQ = D // 128
    GRP = NQ // NKV
    QKVW = (NQ + 2 * NKV) * 128
    NA, NH = (DEPTH + 1) // 2, DEPTH // 2
    CB = D // 512
    last_attn = max(i for i in range(DEPTH) if i % 2 == 0)
    TP = T + 4
    LOFF, COFF = 1, L + 3

    nc = bass.Bass("TRN2", target_bir_lowering=False)
    inp = {}

    def ein(name, shape, dt=F32):
        inp[name] = nc.dram_tensor(name, list(shape), dt, kind="ExternalInput").ap()
        return inp[name]

    x_in = ein("x", [L, D])
    ctx_in = ein("ctx", [C, D])
    cT_in = ein("cT", [128, KD, 2])
    ada_w = ein("ada_w", [DEPTH, D, 6 * D])
    ada_b = ein("ada_b", [DEPTH, 6 * D])
    nmg = ein("norm_mix_g", [DEPTH, D])
    nfg = ein("norm_ffn_g", [DEPTH, D])
    fing = ein("final_g", [1, D])
    w_qkv = ein("attn_w_qkv", [NA, D, QKVW])
    w_o = ein("attn_w_o", [NA, D, D])
    sink_in = ein("attn_sink", [NA, NQ])
    hy_w_in = ein("hy_w_in", [NH, D, 3 * D])
    hy_cw = ein("hy_conv_w", [NH, 3, 3 * D])
    hy_cb = ein("hy_conv_b", [NH, 3 * D])
    hy_w1 = ein("hy_f_w1", [NH, HYE, HYH])
    hy_b1 = ein("hy_f_b1", [NH, HYH, 1])
    hy_w2 = ein("hy_f_w2", [NH, 2, HYH, HYH])
    hy_b2 = ein("hy_f_b2", [NH, 2, HYH, 1])
    hy_w3 = ein("hy_f_w3", [NH, HYH, 4 * D])
    hy_dec = ein("hy_decay", [NH, 2, D])
    hy_fb = ein("hy_fbias", [NH, 2, D])
    hy_wout = ein("hy_w_out", [NH, D, D])
    p_wq = ein("peer_w_q", [DEPTH, D, DQ])
    p_keysT = ein("peer_keysT", [DEPTH, 128, 16, 128])
    p_u = ein("peer_u", [DEPTH * NE, D])
    p_v = ein("peer_v", [DEPTH * NE, D])
    cst = {}
    cst["ident_f"] = ein("ident_f", [128, 128])
    cst["ident_b"] = ein("ident_b", [128, 128], BF16)
    cst["prot"] = ein("prot", [128, 128])
    cst["cosT"] = ein("cosT", [128, L])
    cst["sinT"] = ein("sinT", [128, L])
    cst["masks"] = ein("masks", [128, 3, 384])
    cst["iota16"] = ein("iota16", [128, 2048])
    for n, tag in ((L, "L"), (C, "C")):
        nt = n // 128
        for nm in ("Fre", "Fim", "Fimb", "Gre", "Gim"):
            cst[nm + tag] = ein(nm + tag, [nt, 128, nt, 128], BF16)
        cst["zT" + tag] = ein("zT" + tag, [HYE, n])
        cst["nt01" + tag] = ein("nt01" + tag, [128, nt])

    out_d = nc.dram_tensor("out", [L, D], F32, kind="ExternalOutput").ap()

    def scratch(name, shape, dt=F32):
        return nc.dram_tensor(name, list(shape), dt, kind="Internal").ap()

    hbuf = scratch("hbuf", [T, D])
    moddr = scratch("moddr", [DEPTH, 2, 6 * D])
    MT = scratch("MT", [D, T], BF16)
    XN = scratch("XN", [T, D])
    QT = scratch("QT", [16, 128, T], BF16)
    U_d = scratch("U_d", [T, 3 * D])
    UVs = [scratch("UV%d" % l_, [NE, 2 * D], BF16) for l_ in range(DEPTH)]
    spec = {}
    for tag, n in (("L", L), ("C", C)):
        for o in range(2):
            spec[(tag, o, "re")] = scratch("specre%s%d" % (tag, o), [n, D])
            spec[(tag, o, "im")] = scratch("specim%s%d" % (tag, o), [n, D])
            spec[(tag, o, "d0")] = scratch("specd0%s%d" % (tag, o), [128, D])

    with ExitStack() as top:
        S = Sched(nc, top)
        ps = top.enter_context(nc.psum_tensor("ps", [128, 8, 512], F32))
        psb = ps[:, :, :].bitcast(BF16)

        uid = [0]

        def PB(b):
            return "psb%d" % b

        def phase():
            st = ExitStack()

            def alloc(name, shape, dt):
                uid[0] += 1
                return st.enter_context(nc.sbuf_tensor("%s_%d" % (name, uid[0]), list(shape), dt))
            return st, alloc

        for t in range(NLT):
            S.dma(hbuf[t * 128:(t + 1) * 128, :], x_in[t * 128:(t + 1) * 128, :], [], ["hbuf"])
        for t in range(NCT):
            S.dma(hbuf[L + t * 128:L + (t + 1) * 128, :], ctx_in[t * 128:(t + 1) * 128, :], [], ["hbuf"])
        st, alloc = phase()
        with st:
            cT = alloc("cT", [128, KD, 2], F32)
            cS = alloc("cS", [128, KD, 2], F32)
            S.dma(cT[:], cT_in[:, :, :], [], ["cT"])
            S.act(cS[:], cT[:], AF.Silu, ["cT"], ["cS"])
            NB6 = 6 * D // 512
            GB = 8
            wr = Ring(alloc, "adaw", [128, GB * 512], F32, 2)
            modsb = alloc("modsb", [2, 6 * D], F32)
            biasb = alloc("biasb", [2, 6 * D], F32)
            for i in range(DEPTH):
                S.dma(biasb[:], ada_b[i:i + 1, :].to_broadcast([2, 6 * D]), [], ["biasb"])
                for g0 in range(0, NB6, GB):
                    nb = min(GB, NB6 - g0)
                    for k in range(KD):
                        wt, wn = wr.next()
                        S.dma(wt[:, 0:nb * 512], ada_w[i, k * 128:(k + 1) * 128, g0 * 512:(g0 + nb) * 512], [], [wn])
                        for b in range(nb):
                            S.mm(ps[0:2, b, :], cS[:, k, :], wt[:, b * 512:(b + 1) * 512], k == 0, k == KD - 1, [wn, "cS"], [PB(b)])
                    for b in range(nb):
                        c0 = (g0 + b) * 512
                        S.tt("dve", modsb[:, c0:c0 + 512], ps[0:2, b, :], biasb[:, c0:c0 + 512], ALU.add, [PB(b), "biasb"], ["modsb"])
                S.dma(moddr[i, :, :], modsb[:], ["modsb"], ["moddr"])
        S.barrier()

        def load_mod(alloc, name, layer, row, idx, gvec=None):
            t = alloc(name, [128, D], F32)
            S.dma(t[:], moddr[layer, row:row + 1, idx * D:(idx + 1) * D].to_broadcast([128, D]), ["moddr"], [name])
            if gvec is not None:
                gt = alloc(name + "_g", [128, D], F32)
                S.dma(gt[:], gvec.to_broadcast([128, D]), [], [name + "_g"])
                S.stt(t[:], t[:], 1.0, gt[:], ALU.add, ALU.mult, [name, name + "_g"], [name])
            return t

        def modulate_phase(layer, which, rows, aT, xn_store):
            st, alloc = phase()
            with st:
                ident = alloc("ident", [128, 128], F32)
                S.dma(ident[:], cst["ident_f"][:, :], [], ["ident"])
                gvec = (nmg if which == 1 else nfg)[layer:layer + 1, :]
                sc_i, sh_i = (1, 0) if which == 1 else (4, 3)
                need_rows = sorted(set(r for r, _, _ in rows))
                Gt, St = {}, {}
                for r in need_rows:
                    Gt[r] = load_mod(alloc, "G%d" % r, layer, r, sc_i, gvec)
                    St[r] = load_mod(alloc, "S%d" % r, layer, r, sh_i)
                hr = Ring(alloc, "hT", [128, D], F32, 2)
                ar = Ring(alloc, "aT_", [128, D], F32, 2)
                junk = alloc("junk", [128, D], BF16)
                sr = Ring(alloc, "ss", [128, 2], F32, 2)
                pb = 0
                for (r, ti, coff) in rows:
                    ht, hn = hr.next()
                    at, an = ar.next()
                    ss, sn = sr.next()
                    S.dma(ht[:], hbuf[ti * 128:(ti + 1) * 128, :], ["hbuf"], [hn])
                    S.act(junk[:], ht[:], AF.Square, [hn], ["junk", sn], accum_out=ss[:, 0:1])
                    S.ts("dve", ss[:, 1:2], ss[:, 0:1], 1.0 / D, RMS_EPS, ALU.mult, ALU.add, [sn], [sn])
                    S.act(ss[:, 1:2], ss[:, 1:2], AF.Sqrt, [sn], [sn])
                    S.op("dve", lambda e, ss=ss: e.reciprocal(ss[:, 1:2], ss[:, 1:2]), [sn], [sn])
                    S.stt(at[:], ht[:], ss[:, 1:2], Gt[r][:], ALU.mult, ALU.mult, [hn, sn, "G%d" % r], [an])
                    S.tt("pool", at[:], at[:], St[r][:], ALU.add, [an, "S%d" % r], [an])
                    if xn_store:
                        S.dma(XN[ti * 128:(ti + 1) * 128, :], at[:], [an], ["XN"], eng="pool")
                    for k0 in range(0, KD, 4):
                        b = pb % 2
                        pb += 1
                        for kk in range(4):
                            k = k0 + kk
                            S.tr(ps[:, b, kk * 128:(kk + 1) * 128], at[:, k * 128:(k + 1) * 128], ident[:], [an, "ident"], [PB(b)])
                        S.cp("act" if (k0 // 4) % 2 == 0 else "dve", aT[:, k0:k0 + 4, coff:coff + 128],
                             ps[:, b, :].rearrange("p (k t) -> p k t", k=4), [PB(b)], ["aT"])

        def outproj_phase(layer, wmat, rows):
            st, alloc = phase()
            with st:
                wob = alloc("wob", [128, KD, D], BF16)
                wst = Ring(alloc, "wst", [128, D], F32, 2)
                for k in range(KD):
                    wt, wn = wst.next()
                    S.dma(wt[:], wmat[k * 128:(k + 1) * 128, :], [], [wn])
                    S.cp("pool" if k % 2 else "act", wob[:, k, :], wt[:], [wn], ["wob"])
                need_rows = sorted(set(r for r, _ in rows))
                Gate = {r: load_mod(alloc, "gate%d" % r, layer, r, 2) for r in need_rows}
                mr = Ring(alloc, "mT", [128, KD, 128], BF16, 2)
                hr = Ring(alloc, "hO", [128, D], F32, 2)
                tmp = Ring(alloc, "tmpO", [128, 512], F32, 2)
                NBK = D // 512
                pbi = 0
                for (r, ti) in rows:
                    mt, mn = mr.next()
                    ht, hn = hr.next()
                    S.dma_k(mt[:], MT[:, ti * 128:(ti + 1) * 128].rearrange("(k p) t -> p k t", p=128), ["MT"], [mn])
                    S.dma(ht[:], hbuf[ti * 128:(ti + 1) * 128, :], ["hbuf"], [hn])
                    for nb in range(NBK):
                        b = pbi % 4
                        pbi += 1
                        for k in range(KD):
                            S.mm(ps[:, b, :], mt[:, k, :], wob[:, k, nb * 512:(nb + 1) * 512], k == 0, k == KD - 1, [mn, "wob"], [PB(b)])
                        tt_, tn = tmp.next()
                        S.tt("dve", tt_[:], ps[:, b, :], Gate[r][:, nb * 512:(nb + 1) * 512], ALU.mult, [PB(b), "gate%d" % r], [tn])
                        S.tt("pool", ht[:, nb * 512:(nb + 1) * 512], ht[:, nb * 512:(nb + 1) * 512], tt_[:], ALU.add, [hn, tn], [hn])
                    S.dma(hbuf[ti * 128:(ti + 1) * 128, :], ht[:], [hn], ["hbuf"], eng="pool")
            S.barrier()

        def attention_phase(layer, j, ctx_q):
            Tq = L + (C if ctx_q else 0)
            st, alloc = phase()
            with st:
                aT = alloc("aT", [128, KD, TP], BF16)
                rows = [(0, t, LOFF + t * 128) for t in range(NLT)] + [(1, NLT + t, COFF + t * 128) for t in range(NCT)]
                modulate_phase(layer, 1, rows, aT, False)
                S.barrier()
                st2, alloc2 = phase()
                with st2:
                    alloc = alloc2
                    identb = alloc("identb", [128, 128], BF16)
                    prot = alloc("prot", [128, 128], F32)
                    cosT = alloc("cosT", [128, L], F32)
                    sinT = alloc("sinT", [128, L], F32)
                    masks = alloc("masks", [128, 3, 384], F32)
                    sinkb = alloc("sinkb", [128, NQ], F32)
                    nsinkb = alloc("nsinkb", [128, NQ], F32)
                    S.dma(identb[:], cst["ident_b"][:, :], [], ["identb"])
                    S.dma(prot[:], cst["prot"][:, :], [], ["prot"])
                    S.dma(cosT[:], cst["cosT"][:, :], [], ["cosT"])
                    S.dma(sinT[:], cst["sinT"][:, :], [], ["sinT"])
                    S.dma(masks[:], cst["masks"][:, :, :], [], ["masks"])
                    S.dma(sinkb[:], sink_in[j:j + 1, :].to_broadcast([128, NQ]), [], ["sinkb"])
                    S.ts("dve", nsinkb[:], sinkb[:], -1.0, None, ALU.mult, None, ["sinkb"], ["nsinkb"])
                    wf = Ring(alloc, "wf", [128, KD, 128], F32, 1)
                    wb = Ring(alloc, "wb", [128, KD, 128], BF16, 2)
                    kT = alloc("kT", [128, T], BF16)
                    vtok = alloc("vtok", [128, NT, 128], BF16)
                    qT = alloc("qT", [128, T], BF16)
                    oT = Ring(alloc, "oT", [128, T], BF16, 2)
                    qf = Ring(alloc, "qf", [128, 512], F32, 2)
                    t1 = Ring(alloc, "t1", [128, 512], F32, 2)
                    t2 = Ring(alloc, "t2", [128, 512], F32, 2)
                    sb_s = Ring(alloc, "s", [128, 640], F32, 4)
                    sb_p = Ring(alloc, "p", [128, 640], BF16, 4)
                    sb_pn = Ring(alloc, "pn", [128, 640], BF16, 4)
                    sb_pT = Ring(alloc, "pT", [128, 5, 128], BF16, 4)
                    stat = Ring(alloc, "stat", [128, 8], F32, 6)
                    scale = 128 ** -0.5

                    def load_w(c0):
                        wft, wfn = wf.next()
                        wbt, wbn = wb.next()
                        S.dma_k(wft[:], w_qkv[j, :, c0:c0 + 128].rearrange("(k p) c -> p k c", p=128), [], [wfn])
                        S.cp("pool", wbt[:], wft[:], [wfn], [wbn])
                        return wbt, wbn

                    def proj_fm(wbt, wbn, dst, dname, rope, ranges):
                        for (acol, dcol, n, pos0) in ranges:
                            for c in range(0, n, 512):
                                w = min(512, n - c)
                                b = 0 if (c // 512) % 2 == 0 else 1
                                for k in range(KD):
                                    S.mm(ps[:, b, 0:w], wbt[:, k, :], aT[:, k, acol + c:acol + c + w], k == 0, k == KD - 1, [wbn, "aT"], [PB(b)])
                                if not rope:
                                    S.cp("act", dst[:, dcol + c:dcol + c + w], ps[:, b, 0:w], [PB(b)], [dname])
                                else:
                                    qft, qfn = qf.next()
                                    t1t, t1n = t1.next()
                                    t2t, t2n = t2.next()
                                    S.cp("act", qft[:, 0:w], ps[:, b, 0:w], [PB(b)], [qfn])
                                    S.mm(ps[:, 2, 0:w], prot[:], qft[:, 0:w], True, True, ["prot", qfn], [PB(2)])
                                    S.tt("dve", t1t[:, 0:w], qft[:, 0:w], cosT[:, pos0 + c:pos0 + c + w], ALU.mult, [qfn, "cosT"], [t1n])
                                    S.tt("dve", t2t[:, 0:w], ps[:, 2, 0:w], sinT[:, pos0 + c:pos0 + c + w], ALU.mult, [PB(2), "sinT"], [t2n])
                                    S.tt("pool", dst[:, dcol + c:dcol + c + w], t1t[:, 0:w], t2t[:, 0:w], ALU.add, [t1n, t2n], [dname])

                    for g in range(NKV):
                        wbt, wbn = load_w((NQ + g) * 128)
                        proj_fm(wbt, wbn, kT, "kT", True, [(LOFF, 0, L, 0)])
                        proj_fm(wbt, wbn, kT, "kT", False, [(COFF, L, C, 0)])
                        wbt, wbn = load_w((NQ + NKV + g) * 128)
                        for ti in range(NT):
                            acol = LOFF + ti * 128 if ti < NLT else COFF + (ti - NLT) * 128
                            for k in range(KD):
                                S.mm(ps[:, 7, 0:128], aT[:, k, acol:acol + 128], wbt[:, k, :], k == 0, k == KD - 1, ["aT", wbn], [PB(7)])
                            S.cp("act", vtok[:, ti, :], ps[:, 7, 0:128], [PB(7)], ["vtok"])
                        for hq in range(GRP):
                            h = g * GRP + hq
                            wbt, wbn = load_w(h * 128)
                            proj_fm(wbt, wbn, qT, "qT", True, [(LOFF, 0, L, 0)])
                            if ctx_q:
                                proj_fm(wbt, wbn, qT, "qT", False, [(COFF, L, C, 0)])
                            ot, on = oT.next()
                            nqb = Tq // 128
                            SLOTS = [dict(win=2, ctx=(3, 0), pt=4, o=5), dict(win=6, ctx=(3, 256), pt=7, o=1)]

                            def unit(qb, sl, h=h, ot=ot, on=on):
                                is_lat = qb < NLT
                                s_t, s_n = sb_s.next()
                                p_t, p_n = sb_p.next()
                                pn_t, pn_n = sb_pn.next()
                                pT_t, pT_n = sb_pT.next()
                                stt_, stn = stat.next()
                                qsl = qT[:, qb * 128:(qb + 1) * 128]
                                bw, (bc, cc0), bp, bo = sl["win"], sl["ctx"], sl["pt"], sl["o"]
                                cn = "psb%d_%d" % (bc, cc0)
                                k0 = 0
                                if is_lat:
                                    if qb == 0:
                                        k0, mv = 0, 0
                                    elif qb == NLT - 1:
                                        k0, mv = (NLT - 3) * 128, 2
                                    else:
                                        k0, mv = (qb - 1) * 128, 1
                                    S.mm(ps[:, bw, 0:384], qsl, kT[:, k0:k0 + 384], True, True, ["qT", "kT"], [PB(bw)])
                                    yield
                                    S.stt(s_t[:, 0:384], ps[:, bw, 0:384], scale, masks[:, mv, :], ALU.mult, ALU.add, [PB(bw), "masks"], [s_n])
                                    yield
                                    lo = 0
                                else:
                                    lo = 384
                                S.mm(ps[:, bc, cc0:cc0 + C], qsl, kT[:, L:L + C], True, True, ["qT", "kT"], [cn])
                                yield
                                S.act(s_t[:, 384:384 + C], ps[:, bc, cc0:cc0 + C], AF.Copy, [cn], [s_n], scale=scale)
                                yield
                                hi = 384 + C
                                S.op("dve", lambda e, o=stt_[:, 0:1], i=s_t[:, lo:hi]: e.tensor_reduce(o, i, AX.X, ALU.max, negate=True), [s_n], [stn])
                                yield
                                S.tt("dve", stt_[:, 1:2], stt_[:, 0:1], nsinkb[:, h:h + 1], ALU.min, [stn, "nsinkb"], [stn])
                                yield
                                S.act(p_t[:, lo:hi], s_t[:, lo:hi], AF.Exp, [s_n, stn], [p_n, stn], bias=stt_[:, 1:2], accum_out=stt_[:, 2:3])
                                yield
                                S.act(stt_[:, 3:4], sinkb[:, h:h + 1], AF.Exp, ["sinkb", stn], [stn], bias=stt_[:, 1:2])
                                yield
                                S.tt("dve", stt_[:, 4:5], stt_[:, 2:3], stt_[:, 3:4], ALU.add, [stn], [stn])
                                yield
                                S.op("dve", lambda e, o=stt_[:, 5:6], i=stt_[:, 4:5]: e.reciprocal(o, i), [stn], [stn])
                                yield
                                S.ts("dve", pn_t[:, lo:hi], p_t[:, lo:hi], stt_[:, 5:6], None, ALU.mult, None, [p_n, stn], [pn_n])
                                yield
                                nch = (hi - lo) // 128
                                c_lo = lo // 128
                                for cc in range(nch):
                                    c = c_lo + cc
                                    S.tr(psb[:, bp, c * 128:(c + 1) * 128], pn_t[:, c * 128:(c + 1) * 128], identb[:], [pn_n, "identb"], [PB(bp)])
                                yield
                                S.cp("act", pT_t[:, c_lo:c_lo + nch, :], psb[:, bp, c_lo * 128:(c_lo + nch) * 128].rearrange("p (c t) -> p c t", t=128), [PB(bp)], [pT_n])
                                yield
                                for cc in range(nch):
                                    c = c_lo + cc
                                    if c < 3:
                                        vt = k0 // 128 + c
                                    else:
                                        vt = NLT + (c - 3)
                                    S.mm(ps[:, bo, 0:128], vtok[:, vt, :], pT_t[:, c, :], cc == 0, cc == nch - 1, ["vtok", pT_n], [PB(bo)])
                                yield
                                S.cp("dve", ot[:, qb * 128:(qb + 1) * 128], ps[:, bo, 0:128], [PB(bo)], [on + "_%d" % (qb % 2)])
                                yield

                            for qb0 in range(0, nqb, 2):
                                gens = [unit(qb0 + u, SLOTS[u]) for u in range(min(2, nqb - qb0))]
                                while gens:
                                    for g_ in list(gens):
                                        try:
                                            next(g_)
                                        except StopIteration:
                                            gens.remove(g_)
                            S.dma(MT[h * 128:(h + 1) * 128, 0:Tq], ot[:, 0:Tq], [on + "_0", on + "_1"], ["MT"], eng="pool")
            S.barrier()

        def hyena_spectra(j, tag, n):
            nt = n // 128
            st, alloc = phase()
            with st:
                zT = alloc("zT", [HYE, n], F32)
                w1 = alloc("w1", [HYE, HYH], F32)
                w2 = alloc("w2", [HYH, 2, HYH], F32)
                b1 = alloc("b1", [HYH, 1], F32)
                b2 = alloc("b2", [HYH, 2], F32)
                hA = alloc("hA", [HYH, n], F32)
                hB = alloc("hB", [HYH, n], F32)
                nt01 = alloc("nt01", [128, nt], F32)
                S.dma(zT[:], cst["zT" + tag][:, :], [], ["zT"])
                S.dma(w1[:], hy_w1[j, :, :], [], ["w1"])
                for i in range(2):
                    S.dma(w2[:, i, :], hy_w2[j, i, :, :], [], ["w2"])
                    S.dma(b2[:, i:i + 1], hy_b2[j, i, :, :], [], ["b2"])
                S.dma(b1[:], hy_b1[j, :, :], [], ["b1"])
                S.dma(nt01[:], cst["nt01" + tag][:, :], [], ["nt01"])
                pre = Ring(alloc, "pre", [HYH, 512], F32, 2)
                TWO_PI = 2.0 * math.pi

                def sin_layer(lhsT, lname, bias_ap, bname, src, sname, dst, dname, kdim):
                    for c in range(0, n, 512):
                        w = min(512, n - c)
                        b = (c // 512) % 2
                        S.mm(ps[0:HYH, b, 0:w], lhsT, src[0:kdim, c:c + w], True, True, [lname, sname], [PB(b)])
                        pt, pn = pre.next()
                        S.act(pt[:, 0:w], ps[0:HYH, b, 0:w], AF.Identity, [PB(b), bname], [pn], bias=bias_ap)
                        for _ in range(2):
                            S.ts("dve", dst[:, c:c + w], pt[:, 0:w], math.pi, -TWO_PI, ALU.is_gt, ALU.mult, [pn], [dname])
                            S.tt("dve", pt[:, 0:w], pt[:, 0:w], dst[:, c:c + w], ALU.add, [pn, dname], [pn])
                            S.ts("dve", dst[:, c:c + w], pt[:, 0:w], -math.pi, TWO_PI, ALU.is_lt, ALU.mult, [pn], [dname])
                            S.tt("dve", pt[:, 0:w], pt[:, 0:w], dst[:, c:c + w], ALU.add, [pn, dname], [pn])
                        S.act(dst[:, c:c + w], pt[:, 0:w], AF.Sin, [pn], [dname])
                sin_layer(w1[:], "w1", b1[:, 0:1], "b1", zT, "zT", hA, "hA", HYE)
                sin_layer(w2[:, 0, :], "w2", b2[:, 0:1], "b2", hA, "hA", hB, "hB", HYH)
                sin_layer(w2[:, 1, :], "w2", b2[:, 1:2], "b2", hB, "hB", hA, "hA", HYH)
                w3f = Ring(alloc, "w3f", [HYH, 512], F32, 2)
                w3b = Ring(alloc, "w3b", [HYH, 512], F32, 2)
                absd = Ring(alloc, "absd", [128, 512], F32, 2)
                wnd = Ring(alloc, "wnd", [128, 512], F32, 2)
                ff = Ring(alloc, "ff", [128, 512], F32, 2)
                fbk = Ring(alloc, "fbk", [128, 512], F32, 2)
                FS = alloc("FS", [128, nt, 512], BF16)
                FF = alloc("FF", [128, nt, 512], BF16)
                FB = alloc("FB", [128, nt, 512], BF16)
                fre = Ring(alloc, "fre", [128, nt, 128], BF16, 2)
                fim = Ring(alloc, "fim", [128, nt, 128], BF16, 2)
                fimb = Ring(alloc, "fimb", [128, nt, 128], BF16, 2)
                ore = Ring(alloc, "ore", [128, 512], F32, 2)
                oim = Ring(alloc, "oim", [128, 512], F32, 2)
                od0 = alloc("od0", [128, 512], F32)
                for o in range(2):
                    for cb in range(CB):
                        c0 = cb * 512
                        wft, wfn = w3f.next()
                        wbt, wbn = w3b.next()
                        adt, adn = absd.next()
                        S.dma(wft[:], hy_w3[j, :, o * 2 * D + c0:o * 2 * D + c0 + 512], [], [wfn])
                        S.dma(wbt[:], hy_w3[j, :, o * 2 * D + D + c0:o * 2 * D + D + c0 + 512], [], [wbn])
                        S.dma(adt[:], hy_dec[j, o:o + 1, c0:c0 + 512].to_broadcast([128, 512]), [], [adn])
                        S.act(adt[:], adt[:], AF.Abs, [adn], [adn])
                        for lt in range(nt):
                            wt, wn = wnd.next()
                            fft, ffn = ff.next()
                            fbt, fbn = fbk.next()
                            S.mm(ps[:, 0, :], hA[:, lt * 128:(lt + 1) * 128], wft[:], True, True, ["hA", wfn], [PB(0)])
                            S.mm(ps[:, 1, :], hA[:, lt * 128:(lt + 1) * 128], wbt[:], True, True, ["hA", wbn], [PB(1)])
                            S.act(wt[:], adt[:], AF.Exp, [adn, "nt01"], [wn], scale=nt01[:, lt:lt + 1])
                            S.stt(fft[:], wt[:], 0.05, ps[:, 0, :], ALU.add, ALU.mult, [wn, PB(0)], [ffn])
                            S.stt(fbt[:], wt[:], 0.05, ps[:, 1, :], ALU.add, ALU.mult, [wn, PB(1)], [fbn])
                            if lt == 0:
                                S.memset("dve", fbt[0:1, :], 0.0, [fbn])
                            S.tt("pool", FS[:, lt, :], fft[:], fbt[:], ALU.add, [ffn, fbn], ["FS"])
                            S.cp("act", FF[:, lt, :], fft[:], [ffn], ["FF"])
                            S.cp("pool", FB[:, lt, :], fbt[:], [fbn], ["FB"])
                        for kf in range(nt):
                            fr_t, fr_n = fre.next()
                            fi_t, fi_n = fim.next()
                            fb_t, fb_n = fimb.next()
                            S.dma(fr_t[:], cst["Fre" + tag][kf, :, :, :], [], [fr_n])
                            S.dma(fi_t[:], cst["Fim" + tag][kf, :, :, :], [], [fi_n])
                            S.dma(fb_t[:], cst["Fimb" + tag][kf, :, :, :], [], [fb_n])
                            for lt in range(nt):
                                S.mm(ps[:, 2, :], fr_t[:, lt, :], FS[:, lt, :], lt == 0, lt == nt - 1, [fr_n, "FS"], [PB(2)])
                            for lt in range(nt):
                                S.mm(ps[:, 3, :], fi_t[:, lt, :], FF[:, lt, :], lt == 0, False, [fi_n, "FF"], [PB(3)])
                            for lt in range(nt):
                                S.mm(ps[:, 3, :], fb_t[:, lt, :], FB[:, lt, :], False, lt == nt - 1, [fb_n, "FB"], [PB(3)])
                            ot, on = ore.next()
                            it, in_ = oim.next()
                            S.cp("act", ot[:], ps[:, 2, :], [PB(2)], [on])
                            S.cp("dve", it[:], ps[:, 3, :], [PB(3)], [in_])
                            if kf == 0:
                                S.cp("pool", od0[:], ot[:], [on], ["od0"])
                                S.cp("pool", od0[0:1, :], it[0:1, :], [in_, "od0"], ["od0"])
                                S.memset("pool", it[0:1, :], 0.0, [in_])
                                S.dma(spec[(tag, o, "d0")][:, c0:c0 + 512], od0[:], ["od0"], ["spec"])
                            S.dma(spec[(tag, o, "re")][kf * 128:(kf + 1) * 128, c0:c0 + 512], ot[:], [on], ["spec"])
                            S.dma(spec[(tag, o, "im")][kf * 128:(kf + 1) * 128, c0:c0 + 512], it[:], [in_], ["spec"])
            S.barrier()

        def hyena_inproj(layer, j, with_ctx):
            st, alloc = phase()
            with st:
                aT = alloc("aT", [128, KD, TP], BF16)
                S.memset("pool", aT[:, :, 0:1], 0.0, ["aT"])
                S.memset("pool", aT[:, :, L + 1:L + 3], 0.0, ["aT"])
                S.memset("pool", aT[:, :, TP - 1:TP], 0.0, ["aT"])
                rows = [(0, t, LOFF + t * 128) for t in range(NLT)]
                if with_ctx:
                    rows += [(1, NLT + t, COFF + t * 128) for t in range(NCT)]
                modulate_phase(layer, 1, rows, aT, False)
                S.barrier()
                st2, alloc = phase()
                with st2:
                    CW = 256
                    wf = Ring(alloc, "wf", [128, KD, CW], F32, 1)
                    wj = [Ring(alloc, "wj%d" % jj, [128, KD, CW], BF16, 2) for jj in range(3)]
                    cwb = Ring(alloc, "cwb", [128, 3, CW], F32, 2)
                    cbb = Ring(alloc, "cbb", [128, CW], F32, 2)
                    ost = Ring(alloc, "ost", [128, CW], F32, 3)
                    for cb in range(3 * D // CW):
                        c0 = cb * CW
                        wft, wfn = wf.next()
                        cwt, cwn = cwb.next()
                        cbt, cbn = cbb.next()
                        S.dma_k(wft[:], hy_w_in[j, :, c0:c0 + CW].rearrange("(k p) c -> p k c", p=128), [], [wfn])
                        S.dma(cwt[:], hy_cw[j:j + 1, :, c0:c0 + CW].to_broadcast([128, 3, CW]), [], [cwn])
                        S.dma(cbt[:], hy_cb[j:j + 1, c0:c0 + CW].to_broadcast([128, CW]), [], [cbn])
                        wjs = []
                        for jj in range(3):
                            wt, wn = wj[jj].next()
                            S.tt("pool" if jj == 1 else "dve", wt[:], wft[:], cwt[:, jj:jj + 1, :].to_broadcast([128, KD, CW]), ALU.mult, [wfn, cwn], [wn])
                            wjs.append((wt, wn))
                        for (r, ti, coff) in rows:
                            b = ti % 4
                            n_mm = 3 * KD
                            i_mm = 0
                            for jj in range(3):
                                wt, wn = wjs[jj]
                                for k in range(KD):
                                    S.mm(ps[:, b, 0:CW], aT[:, k, coff + jj - 1:coff + jj - 1 + 128], wt[:, k, :], i_mm == 0, i_mm == n_mm - 1, ["aT", wn], [PB(b)])
                                    i_mm += 1
                            ot, on = ost.next()
                            S.tt("dve", ot[:], ps[:, b, 0:CW], cbt[:], ALU.add, [PB(b), cbn], [on])
                            S.dma(U_d[ti * 128:(ti + 1) * 128, c0:c0 + CW], ot[:], [on], ["U_d"], eng="pool")
            S.barrier()

        def hyena_conv(j, tag, n, toff):
            nt = n // 128
            st, alloc = phase()
            with st:
                identb = alloc("identb", [128, 128], BF16)
                S.dma(identb[:], cst["ident_b"][:, :], [], ["identb"])
                vbuf = alloc("vbuf", [128, nt, 512], F32)
                zsb = alloc("zsb", [128, nt, 512], BF16)
                Yre = alloc("Yre", [128, nt, 512], BF16)
                Yim = alloc("Yim", [128, nt, 512], BF16)
                fre = Ring(alloc, "fre", [128, nt, 128], BF16, 2)
                fim = Ring(alloc, "fim", [128, nt, 128], BF16, 2)
                sA = Ring(alloc, "sA", [128, 512], F32, 2)
                sB = Ring(alloc, "sB", [128, 512], F32, 2)
                sD = alloc("sD", [128, 512], F32)
                xt = Ring(alloc, "xt", [128, 512], F32, 2)
                fbb = alloc("fbb", [128, 2, 512], F32)
                e1 = Ring(alloc, "e1", [128, 512], F32, 2)
                e2 = Ring(alloc, "e2", [128, 512], F32, 2)
                e3 = Ring(alloc, "e3", [128, 512], F32, 2)
                e4 = Ring(alloc, "e4", [128, 512], F32, 2)
                yo = Ring(alloc, "yo", [128, 512], BF16, 2)
                yT = Ring(alloc, "yT", [128, 4, 128], BF16, 2)
                for cb in range(CB):
                    c0 = cb * 512
                    S.dma_k(vbuf[:], U_d[toff:toff + n, c0:c0 + 512].rearrange("(t p) c -> p t c", p=128), ["U_d"], ["vbuf"])
                    S.dma(fbb[:], hy_fb[j:j + 1, :, c0:c0 + 512].to_broadcast([128, 2, 512]), [], ["fbb"])
                    for o in range(2):
                        S.cp("act", zsb[:], vbuf[:], ["vbuf"], ["zsb"])
                        for kf in range(nt):
                            fr_t, fr_n = fre.next()
                            fi_t, fi_n = fim.next()
                            S.dma(fr_t[:], cst["Fre" + tag][kf, :, :, :], [], [fr_n])
                            S.dma(fi_t[:], cst["Fim" + tag][kf, :, :, :], [], [fi_n])
                            At, An = sA.next()
                            Bt, Bn = sB.next()
                            S.dma(At[:], spec[(tag, o, "re")][kf * 128:(kf + 1) * 128, c0:c0 + 512], ["spec"], [An])
                            S.dma(Bt[:], spec[(tag, o, "im")][kf * 128:(kf + 1) * 128, c0:c0 + 512], ["spec"], [Bn])
                            Dt, Dn = At, An
                            if kf == 0:
                                S.dma(sD[:], spec[(tag, o, "d0")][:, c0:c0 + 512], ["spec"], ["sD"])
                                Dt, Dn = sD, "sD"
                            br, bi = (0, 2) if kf % 2 == 0 else (1, 3)
                            for lt in range(nt):
                                S.mm(ps[:, br, :], fr_t[:, lt, :], zsb[:, lt, :], lt == 0, lt == nt - 1, [fr_n, "zsb"], [PB(br)])
                            for lt in range(nt):
                                S.mm(ps[:, bi, :], fi_t[:, lt, :], zsb[:, lt, :], lt == 0, lt == nt - 1, [fi_n, "zsb"], [PB(bi)])
                            a1, n1 = e1.next()
                            a2, n2 = e2.next()
                            a3, n3 = e3.next()
                            a4, n4 = e4.next()
                            S.tt("dve", a1[:], ps[:, br, :], At[:], ALU.mult, [PB(br), An], [n1])
                            S.tt("dve", a2[:], ps[:, bi, :], Bt[:], ALU.mult, [PB(bi), Bn], [n2])
                            S.tt("pool", Yre[:, kf, :], a1[:], a2[:], ALU.subtract, [n1, n2], ["Yre"])
                            S.tt("dve", a3[:], ps[:, br, :], Bt[:], ALU.mult, [PB(br), Bn], [n3])
                            S.tt("dve", a4[:], ps[:, bi, :], Dt[:], ALU.mult, [PB(bi), Dn], [n4])
                            S.tt("pool", Yim[:, kf, :], a3[:], a4[:], ALU.add, [n3, n4], ["Yim"])
                        for tt_i in range(nt):
                            gr_t, gr_n = fre.next()
                            gi_t, gi_n = fim.next()
                            S.dma(gr_t[:], cst["Gre" + tag][tt_i, :, :, :], [], [gr_n])
                            S.dma(gi_t[:], cst["Gim" + tag][tt_i, :, :, :], [], [gi_n])
                            x_t, x_n = xt.next()
                            S.dma(x_t[:], U_d[toff + tt_i * 128:toff + (tt_i + 1) * 128, (o + 1) * D + c0:(o + 1) * D + c0 + 512], ["U_d"], [x_n])
                            b = 4 + tt_i % 2
                            for kf in range(nt):
                                S.mm(ps[:, b, :], gr_t[:, kf, :], Yre[:, kf, :], kf == 0, False, [gr_n, "Yre"], [PB(b)])
                            for kf in range(nt):
                                S.mm(ps[:, b, :], gi_t[:, kf, :], Yim[:, kf, :], False, kf == nt - 1, [gi_n, "Yim"], [PB(b)])
                            a1, n1 = e1.next()
                            a2, n2 = e2.next()
                            S.tt("pool", a1[:], vbuf[:, tt_i, :], fbb[:, o, :], ALU.mult, ["vbuf", "fbb"], [n1])
                            S.tt("dve", a2[:], ps[:, b, :], a1[:], ALU.add, [PB(b), n1], [n2])
                            if o == 0:
                                S.tt("dve", vbuf[:, tt_i, :], a2[:], x_t[:], ALU.mult, [n2, x_n], ["vbuf"])
                            else:
                                y_t, y_n = yo.next()
                                yT_t, yT_n = yT.next()
                                S.tt("dve", y_t[:], a2[:], x_t[:], ALU.mult, [n2, x_n], [y_n])
                                for q4 in range(4):
                                    S.tr(psb[:, 6, q4 * 128:(q4 + 1) * 128], y_t[:, q4 * 128:(q4 + 1) * 128], identb[:], [y_n, "identb"], [PB(6)])
                                S.cp("act", yT_t[:], psb[:, 6, 0:512].rearrange("p (c t) -> p c t", t=128), [PB(6)], [yT_n])
                                S.dma(MT[c0:c0 + 512, toff + tt_i * 128:toff + (tt_i + 1) * 128].rearrange("(c p) t -> p c t", p=128), yT_t[:], [yT_n], ["MT"], eng="pool")
            S.barrier()

        def peer_phase(layer, with_ctx):
            rows = [(0, t) for t in range(NLT)] + ([(1, NLT + t) for t in range(NCT)] if with_ctx else [])
            st, alloc = phase()
            with st:
                xnT = alloc("aT", [128, KD, T], BF16)
                modulate_phase(layer, 2, [(r, ti, ti * 128) for (r, ti) in rows], xnT, True)
                S.barrier()
                st2, alloc = phase()
                with st2:
                    wf = Ring(alloc, "wf", [128, KD, 128], F32, 2)
                    wb = Ring(alloc, "wb", [128, KD, 128], BF16, 2)
                    qo = Ring(alloc, "qo", [128, T], BF16, 2)
                    ntok = len(rows) * 128
                    for c in range(16):
                        wft, wfn = wf.next()
                        wbt, wbn = wb.next()
                        S.dma_k(wft[:], p_wq[layer, :, c * 128:(c + 1) * 128].rearrange("(k p) c -> p k c", p=128), [], [wfn])
                        S.cp("pool", wbt[:], wft[:], [wfn], [wbn])
                        qt, qn = qo.next()
                        for t0 in range(0, ntok, 512):
                            w = min(512, ntok - t0)
                            b = (t0 // 512) % 4
                            for k in range(KD):
                                S.mm(ps[:, b, 0:w], wbt[:, k, :], xnT[:, k, t0:t0 + w], k == 0, k == KD - 1, [wbn, "aT"], [PB(b)])
                            S.cp("act" if (t0 // 512) % 2 == 0 else "dve", qt[:, t0:t0 + w], ps[:, b, 0:w], [PB(b)], [qn])
                        S.dma(QT[c, :, 0:ntok], qt[:, 0:ntok], [qn], ["QT"], eng="pool")
            S.barrier()
            st, alloc = phase()
            with st:
                keyf = alloc("keyf", [128, 16, 128], F32)
                keyb = alloc("keyb", [128, 16, 128], BF16)
                iota = alloc("iota", [128, 2048], F32)
                identb = alloc("identb", [128, 128], BF16)
                S.dma(keyf[:], p_keysT[layer, :, :, :], [], ["keyf"])
                S.cp("act", keyb[:], keyf[:], ["keyf"], ["keyb"])
                S.dma(iota[:], cst["iota16"][:, :], [], ["iota"])
                S.dma(identb[:], cst["ident_b"][:, :], [], ["identb"])
                Gate = {r: load_mod(alloc, "gate%d" % r, layer, r, 5) for r in sorted(set(r for r, _ in rows))}
                qtt = Ring(alloc, "qtt", [128, 16, 128], BF16, 2)
                xnr = Ring(alloc, "xn", [128, D], F32, 2)
                hr = Ring(alloc, "hP", [128, D], F32, 1)
                uvr = Ring(alloc, "uv", [128, 2 * D], BF16, 6)
                dgr = Ring(alloc, "dg", [128, 128], BF16, 4)
                tmpr = Ring(alloc, "tmpP", [128, 512], F32, 2)
                sc = alloc("sc", [128, 16, 128], F32)
                sc2 = alloc("sc2", [128, 16, 128], F32)
                m16 = alloc("m16", [128, 16, 16], F32)
                i16 = alloc("i16", [128, 16, 16], U32)
                i16f = alloc("i16f", [128, 16, 16], F32)
                cand = alloc("cand", [128, PH, 256], F32)
                cand2 = alloc("cand2", [128, PH, 256], F32)
                tops = alloc("tops", [128, PH, 16], F32)
                posu = alloc("posu", [128, PH, 16], U32)
                pa = alloc("pa", [128, PH, 16], U32)
                pbq = alloc("pbq", [128, PH, 16], U32)
                paf = alloc("paf", [128, PH, 16], F32)
                pbf = alloc("pbf", [128, PH, 16], F32)
                oh = alloc("oh", [128, PH, 16, 16], F32)
                oh2 = alloc("oh2", [128, PH, 16, 16], F32)
                i1s = alloc("i1s", [128, PH, 16], F32)
                i2s = alloc("i2s", [128, PH, 16], F32)
                idf = alloc("idf", [128, 128], F32)
                idir = Ring(alloc, "idi", [128, 128], I32, 2)
                gsmr = Ring(alloc, "gsm", [128, PH, 16], F32, 2)
                gst = alloc("gst", [128, PH, 2], F32)
                aprer = Ring(alloc, "apre", [128, 128], F32, 2)
                glr = Ring(alloc, "gl", [128, 128], F32, 2)
                base = 0
                NBK = D // 512

                def routing(ti, idi, idn, gsm, gsn):
                    q_t, q_n = qtt.next()
                    S.dma_k(q_t[:], QT[:, :, ti * 128:(ti + 1) * 128].rearrange("c p t -> p c t"), ["QT"], [q_n])
                    yield
                    for c in range(16):
                        b = c // 4
                        S.mm(ps[:, b, (c % 4) * 128:(c % 4 + 1) * 128], q_t[:, c, :], keyb[:, c, :], True, True, [q_n, "keyb"], [PB(b)])
                        yield
                    for b in range(4):
                        S.cp("act", sc[:, b * 4:(b + 1) * 4, :], ps[:, b, :].rearrange("p (c k) -> p c k", k=128), [PB(b)], ["sc"])
                        yield
                    for c in range(16):
                        S.op("dve", lambda e, c=c: e.max(m16[:, c, 0:8], sc[:, c, :]), ["sc"], ["m16"])
                        yield
                        S.op("dve", lambda e, c=c: e.max_index(i16[:, c, 0:8], m16[:, c, 0:8], sc[:, c, :]), ["sc", "m16"], ["i16"])
                        yield
                        S.op("dve", lambda e, c=c: e.match_replace(sc2[:, c, :], m16[:, c, 0:8], sc[:, c, :], NEG), ["sc", "m16"], ["sc2"])
                        yield
                        S.op("dve", lambda e, c=c: e.max(m16[:, c, 8:16], sc2[:, c, :]), ["sc2"], ["m16"])
                        yield
                        S.op("dve", lambda e, c=c: e.max_index(i16[:, c, 8:16], m16[:, c, 8:16], sc2[:, c, :]), ["sc2", "m16"], ["i16"])
                        yield
                    S.cp("dve", i16f[:], i16[:], ["i16"], ["i16f"])
                    yield
                    m4 = m16[:].rearrange("p (h j) a -> p h j a", j=2)
                    i4 = i16f[:].rearrange("p (h j) a -> p h j a", j=2)
                    for h in range(PH):
                        S.tt("dve", cand[:, h, :].rearrange("p (a b) -> p a b", b=16),
                             m4[:, h, 0, :].unsqueeze(2).to_broadcast([128, 16, 16]),
                             m4[:, h, 1, :].unsqueeze(1).to_broadcast([128, 16, 16]), ALU.add, ["m16"], ["cand"])
                        yield
                    for h in range(PH):
                        S.op("dve", lambda e, h=h: e.max(tops[:, h, 0:8], cand[:, h, :]), ["cand"], ["tops"])
                        yield
                        S.op("dve", lambda e, h=h: e.max_index(posu[:, h, 0:8], tops[:, h, 0:8], cand[:, h, :]), ["cand", "tops"], ["posu"])
                        yield
                        S.op("dve", lambda e, h=h: e.match_replace(cand2[:, h, :], tops[:, h, 0:8], cand[:, h, :], NEG), ["cand", "tops"], ["cand2"])
                        yield
                        S.op("dve", lambda e, h=h: e.max(tops[:, h, 8:16], cand2[:, h, :]), ["cand2"], ["tops"])
                        yield
                        S.op("dve", lambda e, h=h: e.max_index(posu[:, h, 8:16], tops[:, h, 8:16], cand2[:, h, :]), ["cand2", "tops"], ["posu"])
                        yield
                    S.op("dve", lambda e: e.tensor_single_scalar(pa[:], posu[:], 4, ALU.logical_shift_right), ["posu"], ["pa"])
                    yield
                    S.op("dve", lambda e: e.tensor_single_scalar(pbq[:], posu[:], 15, ALU.bitwise_and), ["posu"], ["pbq"])
                    yield
                    S.cp("dve", paf[:], pa[:], ["pa"], ["paf"])
                    yield
                    S.cp("dve", pbf[:], pbq[:], ["pbq"], ["pbf"])
                    yield
                    iota4 = iota[:].rearrange("p (h k a) -> p h k a", h=PH, k=16)
                    for (pf, pfn, jj, dst, dn) in ((paf, "paf", 0, i1s, "i1s"), (pbf, "pbf", 1, i2s, "i2s")):
                        S.tt("dve", oh[:], pf[:].unsqueeze(3).to_broadcast([128, PH, 16, 16]), iota4, ALU.is_equal, [pfn, "iota"], ["oh"])
                        yield
                        S.tt("dve", oh2[:], oh[:], i4[:, :, jj, :].unsqueeze(2).to_broadcast([128, PH, 16, 16]), ALU.mult, ["oh", "i16f"], ["oh2"])
                        yield
                        S.op("dve", lambda e, dst=dst: e.tensor_reduce(dst[:], oh2[:], AX.X, ALU.add), ["oh2"], [dn])
                        yield
                    S.stt(idf[:], i1s[:].rearrange("p h k -> p (h k)"), 128.0, i2s[:].rearrange("p h k -> p (h k)"), ALU.mult, ALU.add, ["i1s", "i2s"], ["idf"])
                    yield
                    S.ts("dve", idf[:], idf[:], float(base), None, ALU.add, None, ["idf"], ["idf"])
                    yield
                    S.cp("dve", idi[:], idf[:], ["idf"], [idn])
                    yield
                    S.tt("dve", gsm[:], tops[:], tops[:, :, 0:1].to_broadcast([128, PH, 16]), ALU.subtract, ["tops"], [gsn])
                    yield
                    S.act(gsm[:], gsm[:], AF.Exp, [gsn], [gsn])
                    yield
                    S.op("dve", lambda e: e.tensor_reduce(gst[:, :, 0], gsm[:], AX.X, ALU.add), [gsn], ["gst"])
                    yield
                    S.op("dve", lambda e: e.reciprocal(gst[:, :, 1], gst[:, :, 0]), ["gst"], ["gst"])
                    yield
                    S.tt("dve", gsm[:], gsm[:], gst[:, :, 1:2].to_broadcast([128, PH, 16]), ALU.mult, [gsn, "gst"], [gsn])
                    yield

                def start_routing(ti):
                    idi, idn = idir.next()
                    gsm, gsn = gsmr.next()
                    return routing(ti, idi, idn, gsm, gsn), (idi, idn, gsm, gsn)

                def drain(gen, n=None):
                    k = 0
                    for _ in gen:
                        k += 1
                        if n is not None and k >= n:
                            return

                gen, cur = start_routing(rows[0][1])
                drain(gen)
                for ri, (r, ti) in enumerate(rows):
                    idi, idn, gsm, gsn = cur
                    nxt_gen, nxt = (None, None)
                    if ri + 1 < len(rows):
                        nxt_gen, nxt = start_routing(rows[ri + 1][1])
                    x_t, x_n = xnr.next()
                    S.dma(x_t[:], XN[ti * 128:(ti + 1) * 128, :], ["XN"], [x_n])
                    apre, apn = aprer.next()
                    gl, gln = glr.next()
                    gflat = gsm[:].rearrange("p h k -> p (h k)")
                    for s_ in range(128):
                        uv, uvn = uvr.next()
                        dg, dgn = dgr.next()
                        S.op("pool", lambda e, uv=uv, s_=s_, idi=idi: e.indirect_dma_start(
                            out=uv[:], out_offset=None, in_=UVs[layer][:, :],
                            in_offset=bass.IndirectOffsetOnAxis(ap=idi[:, s_:s_ + 1], axis=0)), [idn], [uvn], dma=True)
                        an = "%s_%d" % (apn, s_ % 8)
                        S.stt(uv[:, 0:D], uv[:, 0:D], 1.0, x_t[:], ALU.mult, ALU.mult, [uvn, x_n], [uvn + "u", an], accum_out=apre[:, s_:s_ + 1])
                        S.act(gl[:, s_:s_ + 1], apre[:, s_:s_ + 1], AF.Gelu, [an], [an + "g"])
                        S.act(gl[:, s_:s_ + 1], gl[:, s_:s_ + 1], AF.Copy, [an + "g", gsn], [an + "g"], scale=gflat[:, s_:s_ + 1])
                        S.act(dg[:], identb[:], AF.Copy, ["identb", an + "g"], [dgn], scale=gl[:, s_:s_ + 1])
                        for nb in range(NBK):
                            S.mm(ps[:, 4 + nb, :], dg[:], uv[:, D + nb * 512:D + (nb + 1) * 512], s_ == 0, s_ == 127, [dgn, uvn], [PB(4 + nb)])
                        if nxt_gen is not None:
                            drain(nxt_gen, 2)
                    if nxt_gen is not None:
                        drain(nxt_gen)
                    h_t, h_n = hr.next()
                    S.dma(h_t[:], hbuf[ti * 128:(ti + 1) * 128, :], ["hbuf"], [h_n])
                    for nb in range(NBK):
                        tt_, tn = tmpr.next()
                        S.tt("dve", tt_[:], ps[:, 4 + nb, :], Gate[r][:, nb * 512:(nb + 1) * 512], ALU.mult, [PB(4 + nb), "gate%d" % r], [tn])
                        S.tt("dve", h_t[:, nb * 512:(nb + 1) * 512], h_t[:, nb * 512:(nb + 1) * 512], tt_[:], ALU.add, [h_n, tn], [h_n])
                    S.dma(hbuf[ti * 128:(ti + 1) * 128, :], h_t[:], [h_n], ["hbuf"], eng="act")
                    cur = nxt
            S.barrier()

        def convert_tables():
            st, alloc = phase()
            with st:
                fu = Ring(alloc, "cvu", [128, D], F32, 3)
                fv = Ring(alloc, "cvv", [128, D], F32, 3)
                ob = Ring(alloc, "cvo", [128, 2 * D], BF16, 3)
                engs = ["act", "dve", "pool"]
                prev = None
                for ch in range(DEPTH * NE // 128):
                    r0 = ch * 128
                    ut, un = fu.next()
                    vt, vn = fv.next()
                    ot, on = ob.next()
                    S.dma(ut[:], p_u[r0:r0 + 128, :], [], [un])
                    S.dma(vt[:], p_v[r0:r0 + 128, :], [], [vn])
                    if prev is not None:
                        S.dma(UVs[prev[0] // NE][prev[0] % NE:prev[0] % NE + 128, :], prev[1][:], [prev[2]], ["UV"])
                    S.cp(engs[ch % 3], ot[:, 0:D], ut[:], [un], [on])
                    S.cp(engs[(ch + 1) % 3], ot[:, D:2 * D], vt[:], [vn], [on])
                    prev = (r0, ot, on)
                S.dma(UVs[prev[0] // NE][prev[0] % NE:prev[0] % NE + 128, :], prev[1][:], [prev[2]], ["UV"])
            S.barrier()

        convert_tables()
        for i in range(DEPTH):
            is_attn = (i % 2) == 0
            j = i // 2
            ctx_update = i < last_attn
            lat_rows = [(0, t) for t in range(NLT)]
            ctx_rows = [(1, NLT + t) for t in range(NCT)]
            if is_attn:
                attention_phase(i, j, ctx_update)
                outproj_phase(i, w_o[j], lat_rows + (ctx_rows if ctx_update else []))
            else:
                hyena_spectra(j, "L", L)
                hyena_inproj(i, j, ctx_update)
                hyena_conv(j, "L", L, 0)
                if ctx_update:
                    hyena_spectra(j, "C", C)
                    hyena_conv(j, "C", C, L)
                outproj_phase(i, hy_wout[j], lat_rows + (ctx_rows if ctx_update else []))
            if stop_after == "mix%d" % i:
                break
            peer_phase(i, ctx_update)
            if stop_after == "ffn%d" % i:
                break

        st, alloc = phase()
        with st:
            if stop_after is None:
                gt = alloc("fg", [128, D], F32)
                S.dma(gt[:], fing[0:1, :].to_broadcast([128, D]), [], ["fg"])
            hr = Ring(alloc, "hF", [128, D], F32, 2)
            orr = Ring(alloc, "oF", [128, D], F32, 2)
            junk = alloc("junkF", [128, D], BF16)
            sr = Ring(alloc, "ssF", [128, 2], F32, 2)
            for t in range(NLT):
                ht, hn = hr.next()
                S.dma(ht[:], hbuf[t * 128:(t + 1) * 128, :], ["hbuf"], [hn])
                if stop_after is None:
                    ot, on = orr.next()
                    ss, sn = sr.next()
                    S.act(junk[:], ht[:], AF.Square, [hn], ["junkF", sn], accum_out=ss[:, 0:1])
                    S.ts("dve", ss[:, 1:2], ss[:, 0:1], 1.0 / D, RMS_EPS, ALU.mult, ALU.add, [sn], [sn])
                    S.act(ss[:, 1:2], ss[:, 1:2], AF.Sqrt, [sn], [sn])
                    S.op("dve", lambda e, ss=ss: e.reciprocal(ss[:, 1:2], ss[:, 1:2]), [sn], [sn])
                    S.stt(ot[:], ht[:], ss[:, 1:2], gt[:], ALU.mult, ALU.mult, [hn, sn, "fg"], [on])
                    S.dma(out_d[t * 128:(t + 1) * 128, :], ot[:], [on], ["out"], eng="pool")
                else:
                    S.dma(out_d[t * 128:(t + 1) * 128, :], ht[:], [hn], ["out"], eng="pool")
        S.barrier()
        S.emit()
        nops = S.nops
    return nc, nops


def make_in_maps(inputs, cfg, cores):
    D, L, C = cfg["D"], cfg["L"], cfg["C"]
    KD = D // 128
    f = lambda a: np.ascontiguousarray(np.asarray(a, dtype=np.float32))
    shared = {}
    for k in ("ada_w", "ada_b", "norm_mix_g", "norm_ffn_g", "attn_w_qkv", "attn_w_o", "attn_sink", "hy_w_in",
              "hy_conv_w", "hy_conv_b", "hy_f_w1", "hy_f_w2", "hy_f_w3", "hy_decay", "hy_fbias", "hy_w_out", "peer_w_q"):
        shared[k] = f(inputs[k])
    shared["final_g"] = f(inputs["final_g"]).reshape(1, D)
    shared["hy_f_b1"] = f(inputs["hy_f_b1"])[..., None]
    shared["hy_f_b2"] = f(inputs["hy_f_b2"])[..., None]
    k1 = f(inputs["peer_keys1"])
    k2 = f(inputs["peer_keys2"])
    ks = np.stack([k1, k2], axis=2)
    shared["peer_keysT"] = np.ascontiguousarray(ks.transpose(0, 4, 1, 2, 3).reshape(ks.shape[0], 128, 16, 128))
    shared["peer_u"] = f(inputs["peer_u"]).reshape(-1, D)
    shared["peer_v"] = f(inputs["peer_v"]).reshape(-1, D)
    shared.update(host_consts(cfg))
    c_ctx = f(inputs["c_ctx"])
    maps = []
    for b in cores:
        m = dict(shared)
        m["x"] = f(inputs["x"][b])
        m["ctx"] = f(inputs["ctx"][b])
        cb = f(inputs["c"][b])
        cT = np.stack([cb.reshape(KD, 128).T, c_ctx.reshape(KD, 128).T], axis=-1)
        m["cT"] = np.ascontiguousarray(cT)
        maps.append(m)
    return maps


_NC_CACHE = {}


def kernel(**inputs):
    cfg = CFG
    key = (cfg["D"], cfg["L"], cfg["C"])
    if key not in _NC_CACHE:
        _NC_CACHE[key] = build_program(cfg)[0]
    nc = _NC_CACHE[key]
    B = inputs["x"].shape[0]
    maps = make_in_maps(inputs, cfg, list(range(B)))
    res = run_bass_kernel_spmd(nc, maps, core_ids=list(range(B)))
    return np.stack([np.asarray(r["out"], dtype=np.float32) for r in res.results], axis=0)
```

```python
import math
from contextlib import ExitStack
import numpy as np
import ml_dtypes
import concourse.bass as bass
import concourse.mybir as mybir
from concourse.bass_utils import run_bass_kernel_spmd

F32 = mybir.dt.float32
BF16 = mybir.dt.bfloat16
I32 = mybir.dt.int32
U32 = mybir.dt.uint32
ALU = mybir.AluOpType
AF = mybir.ActivationFunctionType
AX = mybir.AxisListType

ENGS = ("pe", "act", "dve", "pool", "sp")
N_DMA_SEMS = 56
EPOCH_N = 12000

CFG = dict(D=2048, L=2048, C=256, DEPTH=4, NCORES=8)
NEG = -1e30
RMS_EPS = 1e-6
NKV = 4
PH, PK, TOPK, DQ = 8, 128, 16, 2048
NE = PK * PK
HYH = 64
HYE = 33


class Sched:
    def __init__(self, nc, stack):
        self.nc = nc
        self.stack = stack
        self.epoch = {e: 0 for e in ENGS}
        self.ops = {e: [] for e in ENGS}
        self.cnt = {e: 0 for e in ENGS}
        self.seen = {e: {} for e in ENGS}
        self.res = {}
        self.sems = {}
        for e in ENGS:
            self.sems[(e, 0)] = stack.enter_context(nc.semaphore("s_" + e))
        self.dma_cnt = [0] * N_DMA_SEMS
        self.dma_rr = 0
        for j in range(N_DMA_SEMS):
            self.sems[("d", j)] = stack.enter_context(nc.semaphore("s_d%d" % j))
        self.nops = 0

    def _need(self, eng, tok, waits):
        if tok is None:
            return
        k, v = tok
        if k[0] == eng and eng == "pe":
            return
        if self.seen[eng].get(k, 0) >= v:
            return
        self.seen[eng][k] = v
        waits.append((k, v))

    def op(self, eng, fn, reads=(), writes=(), dma=False):
        waits = []
        for r in reads:
            st = self.res.get(r)
            if st is not None:
                self._need(eng, st[0], waits)
        for w in writes:
            st = self.res.get(w)
            if st is not None:
                self._need(eng, st[0], waits)
                for t in st[1].items():
                    self._need(eng, t, waits)
        if dma:
            j = self.dma_rr
            self.dma_rr = (self.dma_rr + 1) % N_DMA_SEMS
            if self.dma_cnt[j] > 0:
                self._need(eng, (("d", j), 16 * self.dma_cnt[j]), waits)
            self.dma_cnt[j] += 1
            tok = (("d", j), 16 * self.dma_cnt[j])
            inc = 16
        else:
            self.cnt[eng] += 1
            tok = ((eng, self.epoch[eng]), self.cnt[eng])
            inc = 1
        k, v = tok
        for r in reads:
            st = self.res.setdefault(r, [None, {}])
            if st[1].get(k, 0) < v:
                st[1][k] = v
        for w in writes:
            self.res[w] = [tok, {}]
        self.ops[eng].append((waits, fn, tok[0], inc))
        self.nops += 1
        return tok

    def barrier(self):
        toks = [((e, self.epoch[e]), self.cnt[e]) for e in ENGS if self.cnt[e] > 0]
        toks += [(("d", j), 16 * self.dma_cnt[j]) for j in range(N_DMA_SEMS) if self.dma_cnt[j] > 0]
        for e in ENGS:
            waits = []
            for t in toks:
                self._need(e, t, waits)
            self.ops[e].append((waits, None, None, 0))
        self.res = {}
        for e in ENGS:
            if self.cnt[e] > EPOCH_N:
                self.epoch[e] += 1
                self.cnt[e] = 0
                self.sems[(e, self.epoch[e])] = self.stack.enter_context(
                    self.nc.semaphore("s_%s_%d" % (e, self.epoch[e])))

    def emit(self):
        nc = self.nc
        sems = self.sems
        with nc.Block() as block:
            def mk(engname):
                lst = self.ops[engname]

                def body(e):
                    for waits, fn, semkey, inc in lst:
                        for k, v in waits:
                            e.wait_ge(sems[k], v)
                        if fn is not None:
                            fn(e).then_inc(sems[semkey], inc)
                return body
            block.tensor(mk("pe"))
            block.scalar(mk("act"))
            block.vector(mk("dve"))
            block.gpsimd(mk("pool"))
            block.sync(mk("sp"))

    def dma(self, out, in_, r, w, eng="sp"):
        self.op(eng, lambda e: e.dma_start(out=out, in_=in_), r, w, dma=True)

    def dma_k(self, out, in_, r, w, eng="sp", g=4):
        K_ = out.shape[1]
        for k0 in range(0, K_, g):
            k1 = min(K_, k0 + g)
            self.dma(out[:, k0:k1, :], in_[:, k0:k1, :], r, w, eng)

    def mm(self, out, lhsT, rhs, start, stop, r, w):
        self.op("pe", lambda e: e.matmul(out, lhsT, rhs, start=start, stop=stop), r, w)

    def tr(self, out, in_, ident, r, w):
        self.op("pe", lambda e: e.transpose(out, in_, ident), r, w)

    def act(self, out, in_, func, r, w, bias=None, scale=None, accum_out=None):
        kw = {}
        if bias is not None:
            kw["bias"] = bias
        if scale is not None:
            kw["scale"] = scale
        if accum_out is not None:
            kw["accum_out"] = accum_out
        self.op("act", lambda e: e.activation(out, in_, func, **kw), r, w)

    def tt(self, eng, out, in0, in1, op, r, w):
        self.op(eng, lambda e: e.tensor_tensor(out, in0, in1, op), r, w)

    def ts(self, eng, out, in0, s1, s2, op0, op1, r, w):
        if op1 is None:
            self.op(eng, lambda e: e.tensor_scalar(out, in0, s1, None, op0), r, w)
        else:
            self.op(eng, lambda e: e.tensor_scalar(out, in0, s1, s2, op0, op1), r, w)

    def stt(self, out, in0, scalar, in1, op0, op1, r, w, accum_out=None):
        if accum_out is None:
            self.op("dve", lambda e: e.scalar_tensor_tensor(out, in0, scalar, in1, op0, op1), r, w)
        else:
            self.op("dve", lambda e: e.scalar_tensor_tensor(out, in0, scalar, in1, op0, op1, accum_out=accum_out), r, w)

    def cp(self, eng, out, in_, r, w):
        if eng == "act":
            self.op("act", lambda e: e.copy(out, in_), r, w)
        else:
            self.op(eng, lambda e: e.tensor_copy(out, in_), r, w)

    def memset(self, eng, ap, val, w):
        self.op(eng, lambda e: e.memset(ap, val), (), w)


class Ring:
    def __init__(self, alloc, name, shape, dt, n):
        self.t = [alloc(name + str(i), shape, dt) for i in range(n)]
        self.names = [name + str(i) for i in range(n)]
        self.i = -1

    def next(self):
        self.i = (self.i + 1) % len(self.t)
        return self.t[self.i], self.names[self.i]


def host_consts(cfg):
    D, L, C = cfg["D"], cfg["L"], cfg["C"]
    k = {}
    k["ident_f"] = np.eye(128, dtype=np.float32)
    k["ident_b"] = np.eye(128).astype(ml_dtypes.bfloat16)
    P = np.zeros((128, 128), np.float32)
    for j in range(32):
        P[32 + j, j] = -1.0
        P[j, 32 + j] = 1.0
        P[96 + j, 64 + j] = -1.0
        P[64 + j, 96 + j] = 1.0
    k["prot"] = P
    GW = 64
    rows = L // GW
    row = np.repeat(np.arange(rows), GW).astype(np.float32)
    col = np.tile(np.arange(GW), rows).astype(np.float32)
    inv = (10000.0 ** (-np.arange(32, dtype=np.float32) / 32)).astype(np.float32)
    ar, ac = row[:, None] * inv, col[:, None] * inv
    ang = np.concatenate([ar, ar, ac, ac], axis=-1)
    k["cosT"] = np.ascontiguousarray(np.cos(ang).T.astype(np.float32))
    k["sinT"] = np.ascontiguousarray(np.sin(ang).T.astype(np.float32))
    q = np.arange(128)[:, None]
    kk = np.arange(384)[None, :]
    masks = np.zeros((128, 3, 384), np.float32)
    for v, off in enumerate((0, 128, 256)):
        masks[:, v, :] = np.where(np.abs(kk - (q + off)) <= 128, 0.0, NEG)
    k["masks"] = masks
    io = np.tile(np.arange(16, dtype=np.float32), 128)[None, :].repeat(128, 0)
    k["iota16"] = np.ascontiguousarray(io)
    for n, tag in ((L, "L"), (C, "C")):
        t = np.arange(n, dtype=np.float64)
        f = np.arange(n, dtype=np.float64)
        th = 2.0 * np.pi * np.outer(t, f) / (2 * n)
        Fre = np.cos(th)
        Fim = -np.sin(th)
        Fim[:, 0] = np.cos(np.pi * t)
        Fimb = np.sin(th)
        Fimb[:, 0] = np.cos(np.pi * t)
        Gre = (2.0 / (2 * n)) * np.cos(th.T)
        Gre[0, :] = 1.0 / (2 * n)
        Gim = -(2.0 / (2 * n)) * np.sin(th.T)
        Gim[0, :] = np.cos(np.pi * t) / (2 * n)
        nt = n // 128

        def fwd_layout(M):
            return np.ascontiguousarray(M.reshape(nt, 128, nt, 128).transpose(2, 1, 0, 3)).astype(ml_dtypes.bfloat16)

        def inv_layout(M):
            return np.ascontiguousarray(M.reshape(nt, 128, nt, 128).transpose(2, 1, 0, 3)).astype(ml_dtypes.bfloat16)
        k["Fre" + tag] = fwd_layout(Fre)
        k["Fim" + tag] = fwd_layout(Fim)
        k["Gre" + tag] = inv_layout(Gre)
        k["Gim" + tag] = inv_layout(Gim)
        tt = np.arange(n, dtype=np.float32)
        t01 = tt / n
        bands = np.arange(1, 17, dtype=np.float32)
        a = 2.0 * math.pi * tt[:, None] * bands[None, :] / n
        z = np.concatenate([t01[:, None], np.cos(a), np.sin(a)], axis=-1).astype(np.float32)
        k["zT" + tag] = np.ascontiguousarray(z.T)
        k["nt01" + tag] = np.ascontiguousarray((-t01).reshape(nt, 128).T.astype(np.float32))
    return k


def build_program(cfg, stop_after=None):
    D, L, C, DEPTH = cfg["D"], cfg["L"], cfg["C"], cfg["DEPTH"]
    KD = D // 128
    T = L + C
    NLT, NCT, NT = L // 128, C // 128, (L + C) // 128
    NQ = D // 128
    GRP = NQ // NKV
    QKVW = (NQ + 2 * NKV) * 128
    NA, NH = (DEPTH + 1) // 2, DEPTH // 2
    CB = D // 512
    last_attn = max(i for i in range(DEPTH) if i % 2 == 0)
    TP = T + 4
    LOFF, COFF = 1, L + 3

    nc = bass.Bass("TRN2", target_bir_lowering=False)
    inp = {}

    def ein(name, shape, dt=F32):
        inp[name] = nc.dram_tensor(name, list(shape), dt, kind="ExternalInput").ap()
        return inp[name]

    x_in = ein("x", [L, D])
    ctx_in = ein("ctx", [C, D])
    cT_in = ein("cT", [128, KD, 2])
    ada_w = ein("ada_w", [DEPTH, D, 6 * D])
    ada_b = ein("ada_b", [DEPTH, 6 * D])
    nmg = ein("norm_mix_g", [DEPTH, D])
    nfg = ein("norm_ffn_g", [DEPTH, D])
    fing = ein("final_g", [1, D])
    w_qkv = ein("attn_w_qkv", [NA, D, QKVW])
    w_o = ein("attn_w_o", [NA, D, D])
    sink_in = ein("attn_sink", [NA, NQ])
    hy_w_in = ein("hy_w_in", [NH, D, 3 * D])
    hy_cw = ein("hy_conv_w", [NH, 3, 3 * D])
    hy_cb = ein("hy_conv_b", [NH, 3 * D])
    hy_w1 = ein("hy_f_w1", [NH, HYE, HYH])
    hy_b1 = ein("hy_f_b1", [NH, HYH, 1])
    hy_w2 = ein("hy_f_w2", [NH, 2, HYH, HYH])
    hy_b2 = ein("hy_f_b2", [NH, 2, HYH, 1])
    hy_w3 = ein("hy_f_w3", [NH, HYH, 4 * D])
    hy_dec = ein("hy_decay", [NH, 2, D])
    hy_fb = ein("hy_fbias", [NH, 2, D])
    hy_wout = ein("hy_w_out", [NH, D, D])
    p_wq = ein("peer_w_q", [DEPTH, D, DQ])
    p_keysT = ein("peer_keysT", [DEPTH, 128, 16, 128])
    p_u = ein("peer_u", [DEPTH * NE, D])
    p_v = ein("peer_v", [DEPTH * NE, D])
    cst = {}
    cst["ident_f"] = ein("ident_f", [128, 128])
    cst["ident_b"] = ein("ident_b", [128, 128], BF16)
    cst["prot"] = ein("prot", [128, 128])
    cst["cosT"] = ein("cosT", [128, L])
    cst["sinT"] = ein("sinT", [128, L])
    cst["masks"] = ein("masks", [128, 3, 384])
    cst["iota16"] = ein("iota16", [128, 2048])
    for n, tag in ((L, "L"), (C, "C")):
        nt = n // 128
        for nm in ("Fre", "Fim", "Gre", "Gim"):
            cst[nm + tag] = ein(nm + tag, [nt, 128, nt, 128], BF16)
        cst["zT" + tag] = ein("zT" + tag, [HYE, n])
        cst["nt01" + tag] = ein("nt01" + tag, [128, nt])

    out_d = nc.dram_tensor("out", [L, D], F32, kind="ExternalOutput").ap()

    def scratch(name, shape, dt=F32):
        return nc.dram_tensor(name, list(shape), dt, kind="Internal").ap()

    hbuf = scratch("hbuf", [T, D])
    moddr = scratch("moddr", [DEPTH, 2, 6 * D])
    MT = scratch("MT", [D, T], BF16)
    XN = scratch("XN", [T, D])
    QT = scratch("QT", [16, 128, T], BF16)
    U_d = scratch("U_d", [T, 3 * D])
    UVs = [scratch("UV%d" % l_, [NE, 2 * D], BF16) for l_ in range(DEPTH)]
    spec = {}
    for tag, n in (("L", L), ("C", C)):
        for o in range(2):
            spec[(tag, o, "re")] = scratch("specre%s%d" % (tag, o), [n, D])
            spec[(tag, o, "im")] = scratch("specim%s%d" % (tag, o), [n, D])
            spec[(tag, o, "d0")] = scratch("specd0%s%d" % (tag, o), [128, D])

    with ExitStack() as top:
        S = Sched(nc, top)
        ps = top.enter_context(nc.psum_tensor("ps", [128, 8, 512], F32))
        psb = ps[:, :, :].bitcast(BF16)

        uid = [0]

        def PB(b):
            return "psb%d" % b

        def phase():
            st = ExitStack()

            def alloc(name, shape, dt):
                uid[0] += 1
                return st.enter_context(nc.sbuf_tensor("%s_%d" % (name, uid[0]), list(shape), dt))
            return st, alloc

        def convert_gen(alloc, ch0, ch1, nring=2):
            fu = Ring(alloc, "cvu", [128, D], F32, nring)
            fv = Ring(alloc, "cvv", [128, D], F32, nring)
            ob = Ring(alloc, "cvo", [128, 2 * D], BF16, nring)
            prev = None

            def store(p):
                S.dma(UVs[p[0] // NE][p[0] % NE:p[0] % NE + 128, :], p[1][:], [p[2]], ["UV"])
            for ch in range(ch0, ch1):
                r0 = ch * 128
                ut, un = fu.next()
                vt, vn = fv.next()
                ot, on = ob.next()
                S.dma(ut[:], p_u[r0:r0 + 128, :], [], [un])
                S.dma(vt[:], p_v[r0:r0 + 128, :], [], [vn])
                if prev is not None:
                    store(prev)
                e1, e2 = ("act", "dve") if ch % 2 == 0 else ("dve", "act")
                S.cp(e1, ot[:, 0:D], ut[:], [un], [on])
                S.cp(e2, ot[:, D:2 * D], vt[:], [vn], [on])
                prev = (r0, ot, on)
                yield
            if prev is not None:
                store(prev)
            yield

        for t in range(NLT):
            S.dma(hbuf[t * 128:(t + 1) * 128, :], x_in[t * 128:(t + 1) * 128, :], [], ["hbuf"])
        for t in range(NCT):
            S.dma(hbuf[L + t * 128:L + (t + 1) * 128, :], ctx_in[t * 128:(t + 1) * 128, :], [], ["hbuf"])
        st, alloc = phase()
        with st:
            cT = alloc("cT", [128, KD, 2], F32)
            cS = alloc("cS", [128, KD, 2], F32)
            S.dma(cT[:], cT_in[:, :, :], [], ["cT"])
            S.act(cS[:], cT[:], AF.Silu, ["cT"], ["cS"])
            NB6 = 6 * D // 512
            GB = 8
            wr = Ring(alloc, "adaw", [128, GB * 512], F32, 2)
            modg = Ring(alloc, "modg", [2, GB * 512], F32, 2)
            biasg = Ring(alloc, "biasg", [2, GB * 512], F32, 2)
            cg = convert_gen(alloc, 0, NE // 128)
            for i in range(DEPTH):
                for g0 in range(0, NB6, GB):
                    nb = min(GB, NB6 - g0)
                    bt, bn = biasg.next()
                    mt_, mn_ = modg.next()
                    S.dma(bt[:, 0:nb * 512], ada_b[i:i + 1, g0 * 512:(g0 + nb) * 512].to_broadcast([2, nb * 512]), [], [bn])
                    for k in range(KD):
                        wt, wn = wr.next()
                        S.dma(wt[:, 0:nb * 512], ada_w[i, k * 128:(k + 1) * 128, g0 * 512:(g0 + nb) * 512], [], [wn])
                        for b in range(nb):
                            S.mm(ps[0:2, b, :], cS[:, k, :], wt[:, b * 512:(b + 1) * 512], k == 0, k == KD - 1, [wn, "cS"], [PB(b)])
                        next(cg, None)
                    for b in range(nb):
                        S.tt("dve", mt_[:, b * 512:(b + 1) * 512], ps[0:2, b, :], bt[:, b * 512:(b + 1) * 512], ALU.add, [PB(b), bn], [mn_])
                    S.dma(moddr[i, :, g0 * 512:(g0 + nb) * 512], mt_[:, 0:nb * 512], [mn_], ["moddr"])
            for _ in cg:
                pass
        S.barrier()

        def load_mod(alloc, name, layer, row, idx, gvec=None):
            t = alloc(name, [128, D], F32)
            S.dma(t[:], moddr[layer, row:row + 1, idx * D:(idx + 1) * D].to_broadcast([128, D]), ["moddr"], [name])
            if gvec is not None:
                gt = alloc(name + "_g", [128, D], F32)
                S.dma(gt[:], gvec.to_broadcast([128, D]), [], [name + "_g"])
                S.stt(t[:], t[:], 1.0, gt[:], ALU.add, ALU.mult, [name, name + "_g"], [name])
            return t

        def modulate_phase(layer, which, rows, aT, xn_store):
            st, alloc = phase()
            with st:
                ident = alloc("ident", [128, 128], F32)
                S.dma(ident[:], cst["ident_f"][:, :], [], ["ident"])
                gvec = (nmg if which == 1 else nfg)[layer:layer + 1, :]
                sc_i, sh_i = (1, 0) if which == 1 else (4, 3)
                need_rows = sorted(set(r for r, _, _ in rows))
                Gt, St = {}, {}
                for r in need_rows:
                    Gt[r] = load_mod(alloc, "G%d" % r, layer, r, sc_i, gvec)
                    St[r] = load_mod(alloc, "S%d" % r, layer, r, sh_i)
                hr = Ring(alloc, "hT", [128, D], F32, 2)
                ar = Ring(alloc, "aT_", [128, D], F32, 2)
                junk = alloc("junk", [128, D], BF16)
                sr = Ring(alloc, "ss", [128, 2], F32, 2)
                pb = 0
                for (r, ti, coff) in rows:
                    ht, hn = hr.next()
                    at, an = ar.next()
                    ss, sn = sr.next()
                    S.dma(ht[:], hbuf[ti * 128:(ti + 1) * 128, :], ["hbuf"], [hn])
                    S.act(junk[:], ht[:], AF.Square, [hn], ["junk", sn], accum_out=ss[:, 0:1])
                    S.ts("dve", ss[:, 1:2], ss[:, 0:1], 1.0 / D, RMS_EPS, ALU.mult, ALU.add, [sn], [sn])
                    S.act(ss[:, 1:2], ss[:, 1:2], AF.Sqrt, [sn], [sn])
                    S.op("dve", lambda e, ss=ss: e.reciprocal(ss[:, 1:2], ss[:, 1:2]), [sn], [sn])
                    S.stt(at[:], ht[:], ss[:, 1:2], Gt[r][:], ALU.mult, ALU.mult, [hn, sn, "G%d" % r], [an])
                    S.tt("pool", at[:], at[:], St[r][:], ALU.add, [an, "S%d" % r], [an])
                    if xn_store:
                        S.dma(XN[ti * 128:(ti + 1) * 128, :], at[:], [an], ["XN"], eng="pool")
                    for k0 in range(0, KD, 4):
                        b = pb % 2
                        pb += 1
                        for kk in range(4):
                            k = k0 + kk
                            S.tr(ps[:, b, kk * 128:(kk + 1) * 128], at[:, k * 128:(k + 1) * 128], ident[:], [an, "ident"], [PB(b)])
                        S.cp("act" if (k0 // 4) % 2 == 0 else "dve", aT[:, k0:k0 + 4, coff:coff + 128],
                             ps[:, b, :].rearrange("p (k t) -> p k t", k=4), [PB(b)], ["aT"])

        def outproj_phase(layer, wmat, rows):
            st, alloc = phase()
            with st:
                wob = alloc("wob", [128, KD, D], BF16)
                wst = Ring(alloc, "wst", [128, D], F32, 2)
                for k in range(KD):
                    wt, wn = wst.next()
                    S.dma(wt[:], wmat[k * 128:(k + 1) * 128, :], [], [wn])
                    S.cp("pool" if k % 2 else "act", wob[:, k, :], wt[:], [wn], ["wob"])
                need_rows = sorted(set(r for r, _ in rows))
                Gate = {r: load_mod(alloc, "gate%d" % r, layer, r, 2) for r in need_rows}
                mr = Ring(alloc, "mT", [128, KD, 128], BF16, 2)
                hr = Ring(alloc, "hO", [128, D], F32, 2)
                tmp = Ring(alloc, "tmpO", [128, 512], F32, 2)
                NBK = D // 512
                pbi = 0
                for (r, ti) in rows:
                    mt, mn = mr.next()
                    ht, hn = hr.next()
                    S.dma_k(mt[:], MT[:, ti * 128:(ti + 1) * 128].rearrange("(k p) t -> p k t", p=128), ["MT"], [mn])
                    S.dma(ht[:], hbuf[ti * 128:(ti + 1) * 128, :], ["hbuf"], [hn])
                    for nb in range(NBK):
                        b = pbi % 4
                        pbi += 1
                        for k in range(KD):
                            S.mm(ps[:, b, :], mt[:, k, :], wob[:, k, nb * 512:(nb + 1) * 512], k == 0, k == KD - 1, [mn, "wob"], [PB(b)])
                        tt_, tn = tmp.next()
                        S.tt("dve", tt_[:], ps[:, b, :], Gate[r][:, nb * 512:(nb + 1) * 512], ALU.mult, [PB(b), "gate%d" % r], [tn])
                        S.tt("pool", ht[:, nb * 512:(nb + 1) * 512], ht[:, nb * 512:(nb + 1) * 512], tt_[:], ALU.add, [hn, tn], [hn])
                    S.dma(hbuf[ti * 128:(ti + 1) * 128, :], ht[:], [hn], ["hbuf"], eng="pool")
            S.barrier()

        def attention_phase(layer, j, ctx_q):
            Tq = L + (C if ctx_q else 0)
            st, alloc = phase()
            with st:
                aT = alloc("aT", [128, KD, TP], BF16)
                rows = [(0, t, LOFF + t * 128) for t in range(NLT)] + [(1, NLT + t, COFF + t * 128) for t in range(NCT)]
                modulate_phase(layer, 1, rows, aT, False)
                S.barrier()
                st2, alloc2 = phase()
                with st2:
                    alloc = alloc2
                    identb = alloc("identb", [128, 128], BF16)
                    prot = alloc("prot", [128, 128], F32)
                    cosT = alloc("cosT", [128, L], F32)
                    sinT = alloc("sinT", [128, L], F32)
                    masks = alloc("masks", [128, 3, 384], F32)
                    sinkb = alloc("sinkb", [128, NQ], F32)
                    nsinkb = alloc("nsinkb", [128, NQ], F32)
                    S.dma(identb[:], cst["ident_b"][:, :], [], ["identb"])
                    S.dma(prot[:], cst["prot"][:, :], [], ["prot"])
                    S.dma(cosT[:], cst["cosT"][:, :], [], ["cosT"])
                    S.dma(sinT[:], cst["sinT"][:, :], [], ["sinT"])
                    S.dma(masks[:], cst["masks"][:, :, :], [], ["masks"])
                    S.dma(sinkb[:], sink_in[j:j + 1, :].to_broadcast([128, NQ]), [], ["sinkb"])
                    S.ts("dve", nsinkb[:], sinkb[:], -1.0, None, ALU.mult, None, ["sinkb"], ["nsinkb"])
                    wf = Ring(alloc, "wf", [128, KD, 128], F32, 1)
                    wb = Ring(alloc, "wb", [128, KD, 128], BF16, 2)
                    kT = alloc("kT", [128, T], BF16)
                    vtok = alloc("vtok", [128, NT, 128], BF16)
                    qT = alloc("qT", [128, T], BF16)
                    oT = Ring(alloc, "oT", [128, T], BF16, 2)
                    qf = Ring(alloc, "qf", [128, 512], F32, 2)
                    t1 = Ring(alloc, "t1", [128, 512], F32, 2)
                    t2 = Ring(alloc, "t2", [128, 512], F32, 2)
                    sb_s = Ring(alloc, "s", [128, 640], F32, 4)
                    sb_p = Ring(alloc, "p", [128, 640], BF16, 4)
                    sb_pn = Ring(alloc, "pn", [128, 640], BF16, 4)
                    sb_pT = Ring(alloc, "pT", [128, 5, 128], BF16, 4)
                    stat = Ring(alloc, "stat", [128, 8], F32, 6)
                    scale = 128 ** -0.5

                    def load_w(c0):
                        wft, wfn = wf.next()
                        wbt, wbn = wb.next()
                        S.dma_k(wft[:], w_qkv[j, :, c0:c0 + 128].rearrange("(k p) c -> p k c", p=128), [], [wfn])
                        S.cp("pool", wbt[:], wft[:], [wfn], [wbn])
                        return wbt, wbn

                    def proj_fm(wbt, wbn, dst, dname, rope, ranges):
                        for (acol, dcol, n, pos0) in ranges:
                            for c in range(0, n, 512):
                                w = min(512, n - c)
                                b = 0 if (c // 512) % 2 == 0 else 1
                                for k in range(KD):
                                    S.mm(ps[:, b, 0:w], wbt[:, k, :], aT[:, k, acol + c:acol + c + w], k == 0, k == KD - 1, [wbn, "aT"], [PB(b)])
                                if not rope:
                                    S.cp("act", dst[:, dcol + c:dcol + c + w], ps[:, b, 0:w], [PB(b)], [dname])
                                else:
                                    qft, qfn = qf.next()
                                    t1t, t1n = t1.next()
                                    t2t, t2n = t2.next()
                                    S.cp("act", qft[:, 0:w], ps[:, b, 0:w], [PB(b)], [qfn])
                                    S.mm(ps[:, 2, 0:w], prot[:], qft[:, 0:w], True, True, ["prot", qfn], [PB(2)])
                                    S.tt("dve", t1t[:, 0:w], qft[:, 0:w], cosT[:, pos0 + c:pos0 + c + w], ALU.mult, [qfn, "cosT"], [t1n])
                                    S.tt("dve", t2t[:, 0:w], ps[:, 2, 0:w], sinT[:, pos0 + c:pos0 + c + w], ALU.mult, [PB(2), "sinT"], [t2n])
                                    S.tt("pool", dst[:, dcol + c:dcol + c + w], t1t[:, 0:w], t2t[:, 0:w], ALU.add, [t1n, t2n], [dname])

                    for g in range(NKV):
                        wbt, wbn = load_w((NQ + g) * 128)
                        proj_fm(wbt, wbn, kT, "kT", True, [(LOFF, 0, L, 0)])
                        proj_fm(wbt, wbn, kT, "kT", False, [(COFF, L, C, 0)])
                        wbt, wbn = load_w((NQ + NKV + g) * 128)
                        for ti in range(NT):
                            acol = LOFF + ti * 128 if ti < NLT else COFF + (ti - NLT) * 128
                            for k in range(KD):
                                S.mm(ps[:, 7, 0:128], aT[:, k, acol:acol + 128], wbt[:, k, :], k == 0, k == KD - 1, ["aT", wbn], [PB(7)])
                            S.cp("act", vtok[:, ti, :], ps[:, 7, 0:128], [PB(7)], ["vtok"])
                        for hq in range(GRP):
                            h = g * GRP + hq
                            wbt, wbn = load_w(h * 128)
                            proj_fm(wbt, wbn, qT, "qT", True, [(LOFF, 0, L, 0)])
                            if ctx_q:
                                proj_fm(wbt, wbn, qT, "qT", False, [(COFF, L, C, 0)])
                            ot, on = oT.next()
                            nqb = Tq // 128
                            SLOTS = [dict(win=2, ctx=(3, 0), pt=4, o=5), dict(win=6, ctx=(3, 256), pt=7, o=1)]

                            def unit(qb, sl, h=h, ot=ot, on=on):
                                is_lat = qb < NLT
                                s_t, s_n = sb_s.next()
                                p_t, p_n = sb_p.next()
                                pn_t, pn_n = sb_pn.next()
                                pT_t, pT_n = sb_pT.next()
                                stt_, stn = stat.next()
                                qsl = qT[:, qb * 128:(qb + 1) * 128]
                                bw, (bc, cc0), bp, bo = sl["win"], sl["ctx"], sl["pt"], sl["o"]
                                cn = "psb%d_%d" % (bc, cc0)
                                k0 = 0
                                if is_lat:
                                    if qb == 0:
                                        k0, mv = 0, 0
                                    elif qb == NLT - 1:
                                        k0, mv = (NLT - 3) * 128, 2
                                    else:
                                        k0, mv = (qb - 1) * 128, 1
                                    S.mm(ps[:, bw, 0:384], qsl, kT[:, k0:k0 + 384], True, True, ["qT", "kT"], [PB(bw)])
                                    yield
                                    S.stt(s_t[:, 0:384], ps[:, bw, 0:384], scale, masks[:, mv, :], ALU.mult, ALU.add, [PB(bw), "masks"], [s_n])
                                    yield
                                    lo = 0
                                else:
                                    lo = 384
                                S.mm(ps[:, bc, cc0:cc0 + C], qsl, kT[:, L:L + C], True, True, ["qT", "kT"], [cn])
                                yield
                                S.act(s_t[:, 384:384 + C], ps[:, bc, cc0:cc0 + C], AF.Copy, [cn], [s_n], scale=scale)
                                yield
                                hi = 384 + C
                                S.op("dve", lambda e, o=stt_[:, 0:1], i=s_t[:, lo:hi]: e.tensor_reduce(o, i, AX.X, ALU.max, negate=True), [s_n], [stn])
                                yield
                                S.tt("dve", stt_[:, 1:2], stt_[:, 0:1], nsinkb[:, h:h + 1], ALU.min, [stn, "nsinkb"], [stn])
                                yield
                                S.act(p_t[:, lo:hi], s_t[:, lo:hi], AF.Exp, [s_n, stn], [p_n, stn], bias=stt_[:, 1:2], accum_out=stt_[:, 2:3])
                                yield
                                S.act(stt_[:, 3:4], sinkb[:, h:h + 1], AF.Exp, ["sinkb", stn], [stn], bias=stt_[:, 1:2])
                                yield
                                S.tt("dve", stt_[:, 4:5], stt_[:, 2:3], stt_[:, 3:4], ALU.add, [stn], [stn])
                                yield
                                S.op("dve", lambda e, o=stt_[:, 5:6], i=stt_[:, 4:5]: e.reciprocal(o, i), [stn], [stn])
                                yield
                                S.ts("dve", pn_t[:, lo:hi], p_t[:, lo:hi], stt_[:, 5:6], None, ALU.mult, None, [p_n, stn], [pn_n])
                                yield
                                nch = (hi - lo) // 128
                                c_lo = lo // 128
                                for cc in range(nch):
                                    c = c_lo + cc
                                    S.tr(psb[:, bp, c * 128:(c + 1) * 128], pn_t[:, c * 128:(c + 1) * 128], identb[:], [pn_n, "identb"], [PB(bp)])
                                yield
                                S.cp("act", pT_t[:, c_lo:c_lo + nch, :], psb[:, bp, c_lo * 128:(c_lo + nch) * 128].rearrange("p (c t) -> p c t", t=128), [PB(bp)], [pT_n])
                                yield
                                for cc in range(nch):
                                    c = c_lo + cc
                                    if c < 3:
                                        vt = k0 // 128 + c
                                    else:
                                        vt = NLT + (c - 3)
                                    S.mm(ps[:, bo, 0:128], vtok[:, vt, :], pT_t[:, c, :], cc == 0, cc == nch - 1, ["vtok", pT_n], [PB(bo)])
                                yield
                                S.cp("dve", ot[:, qb * 128:(qb + 1) * 128], ps[:, bo, 0:128], [PB(bo)], [on + "_%d" % (qb % 2)])
                                yield

                            for qb0 in range(0, nqb, 2):
                                gens = [unit(qb0 + u, SLOTS[u]) for u in range(min(2, nqb - qb0))]
                                while gens:
                                    for g_ in list(gens):
                                        try:
                                            next(g_)
                                        except StopIteration:
                                            gens.remove(g_)
                            S.dma(MT[h * 128:(h + 1) * 128, 0:Tq], ot[:, 0:Tq], [on + "_0", on + "_1"], ["MT"], eng="pool")
            S.barrier()

        def hyena_spectra(j, tag, n, conv_range=None):
            nt = n // 128
            st, alloc = phase()
            with st:
                hA = alloc("hA", [HYH, n], F32)
                nt01 = alloc("nt01", [128, nt], F32)
                S.dma(nt01[:], cst["nt01" + tag][:, :], [], ["nt01"])
                st_m, alloc_m = phase()
                with st_m:
                    zT = alloc_m("zT", [HYE, n], F32)
                    w1 = alloc_m("w1", [HYE, HYH], F32)
                    w2 = alloc_m("w2", [HYH, 2, HYH], F32)
                    b1 = alloc_m("b1", [HYH, 1], F32)
                    b2 = alloc_m("b2", [HYH, 2], F32)
                    hB = alloc_m("hB", [HYH, n], F32)
                    S.dma(zT[:], cst["zT" + tag][:, :], [], ["zT"])
                    S.dma(w1[:], hy_w1[j, :, :], [], ["w1"])
                    for i in range(2):
                        S.dma(w2[:, i, :], hy_w2[j, i, :, :], [], ["w2"])
                        S.dma(b2[:, i:i + 1], hy_b2[j, i, :, :], [], ["b2"])
                    S.dma(b1[:], hy_b1[j, :, :], [], ["b1"])
                    pre = Ring(alloc_m, "pre", [HYH, 512], F32, 2)
                    TWO_PI = 2.0 * math.pi

                    def sin_layer(lhsT, lname, bias_ap, bname, src, sname, dst, dname, kdim):
                        for c in range(0, n, 512):
                            w = min(512, n - c)
                            b = (c // 512) % 2
                            S.mm(ps[0:HYH, b, 0:w], lhsT, src[0:kdim, c:c + w], True, True, [lname, sname], [PB(b)])
                            pt, pn = pre.next()
                            S.act(pt[:, 0:w], ps[0:HYH, b, 0:w], AF.Identity, [PB(b), bname], [pn], bias=bias_ap)
                            for _ in range(2):
                                S.ts("dve", dst[:, c:c + w], pt[:, 0:w], math.pi, -TWO_PI, ALU.is_gt, ALU.mult, [pn], [dname])
                                S.tt("dve", pt[:, 0:w], pt[:, 0:w], dst[:, c:c + w], ALU.add, [pn, dname], [pn])
                                S.ts("dve", dst[:, c:c + w], pt[:, 0:w], -math.pi, TWO_PI, ALU.is_lt, ALU.mult, [pn], [dname])
                                S.tt("dve", pt[:, 0:w], pt[:, 0:w], dst[:, c:c + w], ALU.add, [pn, dname], [pn])
                            S.act(dst[:, c:c + w], pt[:, 0:w], AF.Sin, [pn], [dname])
                    sin_layer(w1[:], "w1", b1[:, 0:1], "b1", zT, "zT", hA, "hA", HYE)
                    sin_layer(w2[:, 0, :], "w2", b2[:, 0:1], "b2", hA, "hA", hB, "hB", HYH)
                    sin_layer(w2[:, 1, :], "w2", b2[:, 1:2], "b2", hB, "hB", hA, "hA", HYH)
                S.barrier()
                cg = convert_gen(alloc, conv_range[0], conv_range[1]) if conv_range else None

                def pull(k):
                    if cg is not None:
                        for _ in range(k):
                            next(cg, None)
                w3f = Ring(alloc, "w3f", [HYH, 512], F32, 2)
                w3b = Ring(alloc, "w3b", [HYH, 512], F32, 2)
                absd = Ring(alloc, "absd", [128, 512], F32, 2)
                wnd = Ring(alloc, "wnd", [128, 512], F32, 2)
                ff = Ring(alloc, "ff", [128, 512], F32, 2)
                fbk = Ring(alloc, "fbk", [128, 512], F32, 2)
                FS = alloc("FS", [128, nt, 512], BF16)
                FD = alloc("FD", [128, nt, 512], BF16)
                fre = Ring(alloc, "fre", [128, nt, 128], BF16, 2)
                fim = Ring(alloc, "fim", [128, nt, 128], BF16, 2)
                ore = Ring(alloc, "ore", [128, 512], F32, 2)
                oim = Ring(alloc, "oim", [128, 512], F32, 2)
                od0 = alloc("od0", [128, 512], F32)
                for o in range(2):
                    for cb in range(CB):
                        c0 = cb * 512
                        wft, wfn = w3f.next()
                        wbt, wbn = w3b.next()
                        adt, adn = absd.next()
                        S.dma(wft[:], hy_w3[j, :, o * 2 * D + c0:o * 2 * D + c0 + 512], [], [wfn])
                        S.dma(wbt[:], hy_w3[j, :, o * 2 * D + D + c0:o * 2 * D + D + c0 + 512], [], [wbn])
                        S.dma(adt[:], hy_dec[j, o:o + 1, c0:c0 + 512].to_broadcast([128, 512]), [], [adn])
                        S.act(adt[:], adt[:], AF.Abs, [adn], [adn])
                        for lt in range(nt):
                            wt, wn = wnd.next()
                            fft, ffn = ff.next()
                            fbt, fbn = fbk.next()
                            S.mm(ps[:, 0, :], hA[:, lt * 128:(lt + 1) * 128], wft[:], True, True, ["hA", wfn], [PB(0)])
                            S.mm(ps[:, 1, :], hA[:, lt * 128:(lt + 1) * 128], wbt[:], True, True, ["hA", wbn], [PB(1)])
                            S.act(wt[:], adt[:], AF.Exp, [adn, "nt01"], [wn], scale=nt01[:, lt:lt + 1])
                            S.stt(fft[:], wt[:], 0.05, ps[:, 0, :], ALU.add, ALU.mult, [wn, PB(0)], [ffn])
                            S.stt(fbt[:], wt[:], 0.05, ps[:, 1, :], ALU.add, ALU.mult, [wn, PB(1)], [fbn])
                            if lt == 0:
                                S.memset("dve", fbt[0:1, :], 0.0, [fbn])
                            S.tt("pool", FS[:, lt, :], fft[:], fbt[:], ALU.add, [ffn, fbn], ["FS"])
                            S.tt("dve", FD[:, lt, :], fft[:], fbt[:], ALU.subtract, [ffn, fbn], ["FD"])
                            pull(1)
                        for kf in range(nt):
                            fr_t, fr_n = fre.next()
                            fi_t, fi_n = fim.next()
                            S.dma(fr_t[:], cst["Fre" + tag][kf, :, :, :], [], [fr_n])
                            S.dma(fi_t[:], cst["Fim" + tag][kf, :, :, :], [], [fi_n])
                            for lt in range(nt):
                                S.mm(ps[:, 2, :], fr_t[:, lt, :], FS[:, lt, :], lt == 0, lt == nt - 1, [fr_n, "FS"], [PB(2)])
                            for lt in range(nt):
                                S.mm(ps[:, 3, :], fi_t[:, lt, :], FD[:, lt, :], lt == 0, lt == nt - 1, [fi_n, "FD"], [PB(3)])
                            if kf == 0:
                                for lt in range(nt):
                                    S.mm(ps[0:1, 4, :], fi_t[:, lt, 0:1], FS[:, lt, :], lt == 0, lt == nt - 1, [fi_n, "FS"], [PB(4)])
                            ot, on = ore.next()
                            it, in_ = oim.next()
                            S.cp("act", ot[:], ps[:, 2, :], [PB(2)], [on])
                            S.cp("dve", it[:], ps[:, 3, :], [PB(3)], [in_])
                            if kf == 0:
                                S.cp("act", od0[:], ot[:], [on], ["od0"])
                                S.cp("dve", od0[0:1, :], ps[0:1, 4, :], [PB(4), "od0"], ["od0"])
                                S.memset("dve", it[0:1, :], 0.0, [in_])
                                S.dma(spec[(tag, o, "d0")][:, c0:c0 + 512], od0[:], ["od0"], ["spec"])
                            S.dma(spec[(tag, o, "re")][kf * 128:(kf + 1) * 128, c0:c0 + 512], ot[:], [on], ["spec"])
                            S.dma(spec[(tag, o, "im")][kf * 128:(kf + 1) * 128, c0:c0 + 512], it[:], [in_], ["spec"])
                            pull(2)
                if cg is not None:
                    for _ in cg:
                        pass
            S.barrier()

        def hyena_inproj(layer, j, with_ctx):
            st, alloc = phase()
            with st:
                aT = alloc("aT", [128, KD, TP], BF16)
                S.memset("pool", aT[:, :, 0:1], 0.0, ["aT"])
                S.memset("pool", aT[:, :, L + 1:L + 3], 0.0, ["aT"])
                S.memset("pool", aT[:, :, TP - 1:TP], 0.0, ["aT"])
                rows = [(0, t, LOFF + t * 128) for t in range(NLT)]
                if with_ctx:
                    rows += [(1, NLT + t, COFF + t * 128) for t in range(NCT)]
                modulate_phase(layer, 1, rows, aT, False)
                S.barrier()
                st2, alloc = phase()
                with st2:
                    CW = 256
                    wf = Ring(alloc, "wf", [128, KD, CW], F32, 1)
                    wj = [Ring(alloc, "wj%d" % jj, [128, KD, CW], BF16, 2) for jj in range(3)]
                    cwb = Ring(alloc, "cwb", [128, 3, CW], F32, 2)
                    cbb = Ring(alloc, "cbb", [128, CW], F32, 2)
                    ost = Ring(alloc, "ost", [128, CW], F32, 3)
                    for cb in range(3 * D // CW):
                        c0 = cb * CW
                        wft, wfn = wf.next()
                        cwt, cwn = cwb.next()
                        cbt, cbn = cbb.next()
                        S.dma_k(wft[:], hy_w_in[j, :, c0:c0 + CW].rearrange("(k p) c -> p k c", p=128), [], [wfn])
                        S.dma(cwt[:], hy_cw[j:j + 1, :, c0:c0 + CW].to_broadcast([128, 3, CW]), [], [cwn])
                        S.dma(cbt[:], hy_cb[j:j + 1, c0:c0 + CW].to_broadcast([128, CW]), [], [cbn])
                        wjs = []
                        for jj in range(3):
                            wt, wn = wj[jj].next()
                            S.tt("pool" if jj == 1 else "dve", wt[:], wft[:], cwt[:, jj:jj + 1, :].to_broadcast([128, KD, CW]), ALU.mult, [wfn, cwn], [wn])
                            wjs.append((wt, wn))
                        for (r, ti, coff) in rows:
                            b = ti % 4
                            n_mm = 3 * KD
                            i_mm = 0
                            for jj in range(3):
                                wt, wn = wjs[jj]
                                for k in range(KD):
                                    S.mm(ps[:, b, 0:CW], aT[:, k, coff + jj - 1:coff + jj - 1 + 128], wt[:, k, :], i_mm == 0, i_mm == n_mm - 1, ["aT", wn], [PB(b)])
                                    i_mm += 1
                            ot, on = ost.next()
                            S.tt("dve", ot[:], ps[:, b, 0:CW], cbt[:], ALU.add, [PB(b), cbn], [on])
                            S.dma(U_d[ti * 128:(ti + 1) * 128, c0:c0 + CW], ot[:], [on], ["U_d"], eng="pool")
            S.barrier()

        def hyena_conv(j, tag, n, toff):
            nt = n // 128
            st, alloc = phase()
            with st:
                identb = alloc("identb", [128, 128], BF16)
                S.dma(identb[:], cst["ident_b"][:, :], [], ["identb"])
                vbuf = alloc("vbuf", [128, nt, 512], F32)
                zsb = alloc("zsb", [128, nt, 512], BF16)
                Yre = alloc("Yre", [128, nt, 512], BF16)
                Yim = alloc("Yim", [128, nt, 512], BF16)
                fre = Ring(alloc, "fre", [128, nt, 128], BF16, 2)
                fim = Ring(alloc, "fim", [128, nt, 128], BF16, 2)
                sA = Ring(alloc, "sA", [128, 512], F32, 2)
                sB = Ring(alloc, "sB", [128, 512], F32, 2)
                sD = alloc("sD", [128, 512], F32)
                xt = Ring(alloc, "xt", [128, 512], F32, 2)
                fbb = alloc("fbb", [128, 2, 512], F32)
                e1 = Ring(alloc, "e1", [128, 512], F32, 2)
                e2 = Ring(alloc, "e2", [128, 512], F32, 2)
                e3 = Ring(alloc, "e3", [128, 512], F32, 2)
                e4 = Ring(alloc, "e4", [128, 512], F32, 2)
                yo = Ring(alloc, "yo", [128, 512], BF16, 2)
                yT = Ring(alloc, "yT", [128, 4, 128], BF16, 2)
                for cb in range(CB):
                    c0 = cb * 512
                    S.dma_k(vbuf[:], U_d[toff:toff + n, c0:c0 + 512].rearrange("(t p) c -> p t c", p=128), ["U_d"], ["vbuf"])
                    S.dma(fbb[:], hy_fb[j:j + 1, :, c0:c0 + 512].to_broadcast([128, 2, 512]), [], ["fbb"])
                    for o in range(2):
                        S.cp("act", zsb[:], vbuf[:], ["vbuf"], ["zsb"])
                        for kf in range(nt):
                            fr_t, fr_n = fre.next()
                            fi_t, fi_n = fim.next()
                            S.dma(fr_t[:], cst["Fre" + tag][kf, :, :, :], [], [fr_n])
                            S.dma(fi_t[:], cst["Fim" + tag][kf, :, :, :], [], [fi_n])
                            At, An = sA.next()
                            Bt, Bn = sB.next()
                            S.dma(At[:], spec[(tag, o, "re")][kf * 128:(kf + 1) * 128, c0:c0 + 512], ["spec"], [An])
                            S.dma(Bt[:], spec[(tag, o, "im")][kf * 128:(kf + 1) * 128, c0:c0 + 512], ["spec"], [Bn])
                            Dt, Dn = At, An
                            if kf == 0:
                                S.dma(sD[:], spec[(tag, o, "d0")][:, c0:c0 + 512], ["spec"], ["sD"])
                                Dt, Dn = sD, "sD"
                            br, bi = (0, 2) if kf % 2 == 0 else (1, 3)
                            for lt in range(nt):
                                S.mm(ps[:, br, :], fr_t[:, lt, :], zsb[:, lt, :], lt == 0, lt == nt - 1, [fr_n, "zsb"], [PB(br)])
                            for lt in range(nt):
                                S.mm(ps[:, bi, :], fi_t[:, lt, :], zsb[:, lt, :], lt == 0, lt == nt - 1, [fi_n, "zsb"], [PB(bi)])
                            a1, n1 = e1.next()
                            a2, n2 = e2.next()
                            a3, n3 = e3.next()
                            a4, n4 = e4.next()
                            S.tt("dve", a1[:], ps[:, br, :], At[:], ALU.mult, [PB(br), An], [n1])
                            S.tt("dve", a2[:], ps[:, bi, :], Bt[:], ALU.mult, [PB(bi), Bn], [n2])
                            S.tt("pool", Yre[:, kf, :], a1[:], a2[:], ALU.subtract, [n1, n2], ["Yre"])
                            S.tt("dve", a3[:], ps[:, br, :], Bt[:], ALU.mult, [PB(br), Bn], [n3])
                            S.tt("dve", a4[:], ps[:, bi, :], Dt[:], ALU.mult, [PB(bi), Dn], [n4])
                            S.tt("pool", Yim[:, kf, :], a3[:], a4[:], ALU.add, [n3, n4], ["Yim"])
                        for tt_i in range(nt):
                            gr_t, gr_n = fre.next()
                            gi_t, gi_n = fim.next()
                            S.dma(gr_t[:], cst["Gre" + tag][tt_i, :, :, :], [], [gr_n])
                            S.dma(gi_t[:], cst["Gim" + tag][tt_i, :, :, :], [], [gi_n])
                            x_t, x_n = xt.next()
                            S.dma(x_t[:], U_d[toff + tt_i * 128:toff + (tt_i + 1) * 128, (o + 1) * D + c0:(o + 1) * D + c0 + 512], ["U_d"], [x_n])
                            b = 4 + tt_i % 2
                            for kf in range(nt):
                                S.mm(ps[:, b, :], gr_t[:, kf, :], Yre[:, kf, :], kf == 0, False, [gr_n, "Yre"], [PB(b)])
                            for kf in range(nt):
                                S.mm(ps[:, b, :], gi_t[:, kf, :], Yim[:, kf, :], False, kf == nt - 1, [gi_n, "Yim"], [PB(b)])
                            a1, n1 = e1.next()
                            a2, n2 = e2.next()
                            S.tt("pool", a1[:], vbuf[:, tt_i, :], fbb[:, o, :], ALU.mult, ["vbuf", "fbb"], [n1])
                            S.tt("dve", a2[:], ps[:, b, :], a1[:], ALU.add, [PB(b), n1], [n2])
                            if o == 0:
                                S.tt("dve", vbuf[:, tt_i, :], a2[:], x_t[:], ALU.mult, [n2, x_n], ["vbuf"])
                            else:
                                y_t, y_n = yo.next()
                                yT_t, yT_n = yT.next()
                                S.tt("dve", y_t[:], a2[:], x_t[:], ALU.mult, [n2, x_n], [y_n])
                                for q4 in range(4):
                                    S.tr(psb[:, 6, q4 * 128:(q4 + 1) * 128], y_t[:, q4 * 128:(q4 + 1) * 128], identb[:], [y_n, "identb"], [PB(6)])
                                S.cp("act", yT_t[:], psb[:, 6, 0:512].rearrange("p (c t) -> p c t", t=128), [PB(6)], [yT_n])
                                S.dma(MT[c0:c0 + 512, toff + tt_i * 128:toff + (tt_i + 1) * 128].rearrange("(c p) t -> p c t", p=128), yT_t[:], [yT_n], ["MT"], eng="pool")
            S.barrier()

        def peer_phase(layer, with_ctx):
            rows = [(0, t) for t in range(NLT)] + ([(1, NLT + t) for t in range(NCT)] if with_ctx else [])
            st, alloc = phase()
            with st:
                xnT = alloc("aT", [128, KD, T], BF16)
                modulate_phase(layer, 2, [(r, ti, ti * 128) for (r, ti) in rows], xnT, True)
                S.barrier()
                st2, alloc = phase()
                with st2:
                    wf = Ring(alloc, "wf", [128, KD, 128], F32, 2)
                    wb = Ring(alloc, "wb", [128, KD, 128], BF16, 2)
                    qo = Ring(alloc, "qo", [128, T], BF16, 2)
                    ntok = len(rows) * 128
                    for c in range(16):
                        wft, wfn = wf.next()
                        wbt, wbn = wb.next()
                        S.dma_k(wft[:], p_wq[layer, :, c * 128:(c + 1) * 128].rearrange("(k p) c -> p k c", p=128), [], [wfn])
                        S.cp("pool", wbt[:], wft[:], [wfn], [wbn])
                        qt, qn = qo.next()
                        for t0 in range(0, ntok, 512):
                            w = min(512, ntok - t0)
                            b = (t0 // 512) % 4
                            for k in range(KD):
                                S.mm(ps[:, b, 0:w], wbt[:, k, :], xnT[:, k, t0:t0 + w], k == 0, k == KD - 1, [wbn, "aT"], [PB(b)])
                            S.cp("act" if (t0 // 512) % 2 == 0 else "dve", qt[:, t0:t0 + w], ps[:, b, 0:w], [PB(b)], [qn])
                        S.dma(QT[c, :, 0:ntok], qt[:, 0:ntok], [qn], ["QT"], eng="pool")
            S.barrier()
            st, alloc = phase()
            with st:
                keyf = alloc("keyf", [128, 16, 128], F32)
                keyb = alloc("keyb", [128, 16, 128], BF16)
                iota = alloc("iota", [128, 2048], F32)
                identb = alloc("identb", [128, 128], BF16)
                S.dma(keyf[:], p_keysT[layer, :, :, :], [], ["keyf"])
                S.cp("act", keyb[:], keyf[:], ["keyf"], ["keyb"])
                S.dma(iota[:], cst["iota16"][:, :], [], ["iota"])
                S.dma(identb[:], cst["ident_b"][:, :], [], ["identb"])
                Gate = {r: load_mod(alloc, "gate%d" % r, layer, r, 5) for r in sorted(set(r for r, _ in rows))}
                qtt = Ring(alloc, "qtt", [128, 16, 128], BF16, 2)
                xnr = Ring(alloc, "xn", [128, D], F32, 2)
                hr = Ring(alloc, "hP", [128, D], F32, 1)
                uvr = Ring(alloc, "uv", [128, 2 * D], BF16, 6)
                dgr = Ring(alloc, "dg", [128, 128], BF16, 4)
                tmpr = Ring(alloc, "tmpP", [128, 512], F32, 2)
                sc = alloc("sc", [128, 16, 128], F32)
                sc2 = alloc("sc2", [128, 16, 128], F32)
                m16 = alloc("m16", [128, 16, 16], F32)
                i16 = alloc("i16", [128, 16, 16], U32)
                i16f = alloc("i16f", [128, 16, 16], F32)
                cand = alloc("cand", [128, PH, 256], F32)
                cand2 = alloc("cand2", [128, PH, 256], F32)
                tops = alloc("tops", [128, PH, 16], F32)
                posu = alloc("posu", [128, PH, 16], U32)
                pa = alloc("pa", [128, PH, 16], U32)
                pbq = alloc("pbq", [128, PH, 16], U32)
                paf = alloc("paf", [128, PH, 16], F32)
                pbf = alloc("pbf", [128, PH, 16], F32)
                oh = alloc("oh", [128, PH, 16, 16], F32)
                oh2 = alloc("oh2", [128, PH, 16, 16], F32)
                i1s = alloc("i1s", [128, PH, 16], F32)
                i2s = alloc("i2s", [128, PH, 16], F32)
                idf = alloc("idf", [128, 128], F32)
                idir = Ring(alloc, "idi", [128, 128], I32, 2)
                gsmr = Ring(alloc, "gsm", [128, PH, 16], F32, 2)
                gst = alloc("gst", [128, PH, 2], F32)
                aprer = Ring(alloc, "apre", [128, 128], F32, 2)
                glr = Ring(alloc, "gl", [128, 128], F32, 2)
                base = 0
                NBK = D // 512

                def routing(ti, idi, idn, gsm, gsn):
                    q_t, q_n = qtt.next()
                    S.dma_k(q_t[:], QT[:, :, ti * 128:(ti + 1) * 128].rearrange("c p t -> p c t"), ["QT"], [q_n])
                    yield
                    for c in range(16):
                        b = c // 4
                        S.mm(ps[:, b, (c % 4) * 128:(c % 4 + 1) * 128], q_t[:, c, :], keyb[:, c, :], True, True, [q_n, "keyb"], [PB(b)])
                        yield
                    for b in range(4):
                        S.cp("act", sc[:, b * 4:(b + 1) * 4, :], ps[:, b, :].rearrange("p (c k) -> p c k", k=128), [PB(b)], ["sc"])
                        yield
                    for c in range(16):
                        S.op("dve", lambda e, c=c: e.max(m16[:, c, 0:8], sc[:, c, :]), ["sc"], ["m16"])
                        yield
                        S.op("dve", lambda e, c=c: e.max_index(i16[:, c, 0:8], m16[:, c, 0:8], sc[:, c, :]), ["sc", "m16"], ["i16"])
                        yield
                        S.op("dve", lambda e, c=c: e.match_replace(sc2[:, c, :], m16[:, c, 0:8], sc[:, c, :], NEG), ["sc", "m16"], ["sc2"])
                        yield
                        S.op("dve", lambda e, c=c: e.max(m16[:, c, 8:16], sc2[:, c, :]), ["sc2"], ["m16"])
                        yield
                        S.op("dve", lambda e, c=c: e.max_index(i16[:, c, 8:16], m16[:, c, 8:16], sc2[:, c, :]), ["sc2", "m16"], ["i16"])
                        yield
                    S.cp("dve", i16f[:], i16[:], ["i16"], ["i16f"])
                    yield
                    m4 = m16[:].rearrange("p (h j) a -> p h j a", j=2)
                    i4 = i16f[:].rearrange("p (h j) a -> p h j a", j=2)
                    for h in range(PH):
                        S.tt("dve", cand[:, h, :].rearrange("p (a b) -> p a b", b=16),
                             m4[:, h, 0, :].unsqueeze(2).to_broadcast([128, 16, 16]),
                             m4[:, h, 1, :].unsqueeze(1).to_broadcast([128, 16, 16]), ALU.add, ["m16"], ["cand"])
                        yield
                    for h in range(PH):
                        S.op("dve", lambda e, h=h: e.max(tops[:, h, 0:8], cand[:, h, :]), ["cand"], ["tops"])
                        yield
                        S.op("dve", lambda e, h=h: e.max_index(posu[:, h, 0:8], tops[:, h, 0:8], cand[:, h, :]), ["cand", "tops"], ["posu"])
                        yield
                        S.op("dve", lambda e, h=h: e.match_replace(cand2[:, h, :], tops[:, h, 0:8], cand[:, h, :], NEG), ["cand", "tops"], ["cand2"])
                        yield
                        S.op("dve", lambda e, h=h: e.max(tops[:, h, 8:16], cand2[:, h, :]), ["cand2"], ["tops"])
                        yield
                        S.op("dve", lambda e, h=h: e.max_index(posu[:, h, 8:16], tops[:, h, 8:16], cand2[:, h, :]), ["cand2", "tops"], ["posu"])
                        yield
                    S.op("dve", lambda e: e.tensor_single_scalar(pa[:], posu[:], 4, ALU.logical_shift_right), ["posu"], ["pa"])
                    yield
                    S.op("dve", lambda e: e.tensor_single_scalar(pbq[:], posu[:], 15, ALU.bitwise_and), ["posu"], ["pbq"])
                    yield
                    S.cp("dve", paf[:], pa[:], ["pa"], ["paf"])
                    yield
                    S.cp("dve", pbf[:], pbq[:], ["pbq"], ["pbf"])
                    yield
                    iota4 = iota[:].rearrange("p (h k a) -> p h k a", h=PH, k=16)
                    for (pf, pfn, jj, dst, dn) in ((paf, "paf", 0, i1s, "i1s"), (pbf, "pbf", 1, i2s, "i2s")):
                        S.tt("dve", oh[:], pf[:].unsqueeze(3).to_broadcast([128, PH, 16, 16]), iota4, ALU.is_equal, [pfn, "iota"], ["oh"])
                        yield
                        S.tt("dve", oh2[:], oh[:], i4[:, :, jj, :].unsqueeze(2).to_broadcast([128, PH, 16, 16]), ALU.mult, ["oh", "i16f"], ["oh2"])
                        yield
                        S.op("dve", lambda e, dst=dst: e.tensor_reduce(dst[:], oh2[:], AX.X, ALU.add), ["oh2"], [dn])
                        yield
                    S.stt(idf[:], i1s[:].rearrange("p h k -> p (h k)"), 128.0, i2s[:].rearrange("p h k -> p (h k)"), ALU.mult, ALU.add, ["i1s", "i2s"], ["idf"])
                    yield
                    S.ts("dve", idf[:], idf[:], float(base), None, ALU.add, None, ["idf"], ["idf"])
                    yield
                    S.cp("dve", idi[:], idf[:], ["idf"], [idn])
                    yield
                    S.tt("dve", gsm[:], tops[:], tops[:, :, 0:1].to_broadcast([128, PH, 16]), ALU.subtract, ["tops"], [gsn])
                    yield
                    S.act(gsm[:], gsm[:], AF.Exp, [gsn], [gsn])
                    yield
                    S.op("dve", lambda e: e.tensor_reduce(gst[:, :, 0], gsm[:], AX.X, ALU.add), [gsn], ["gst"])
                    yield
                    S.op("dve", lambda e: e.reciprocal(gst[:, :, 1], gst[:, :, 0]), ["gst"], ["gst"])
                    yield
                    S.tt("dve", gsm[:], gsm[:], gst[:, :, 1:2].to_broadcast([128, PH, 16]), ALU.mult, [gsn, "gst"], [gsn])
                    yield

                def start_routing(ti):
                    idi, idn = idir.next()
                    gsm, gsn = gsmr.next()
                    return routing(ti, idi, idn, gsm, gsn), (idi, idn, gsm, gsn)

                def drain(gen, n=None):
                    k = 0
                    for _ in gen:
                        k += 1
                        if n is not None and k >= n:
                            return

                gen, cur = start_routing(rows[0][1])
                drain(gen)
                for ri, (r, ti) in enumerate(rows):
                    idi, idn, gsm, gsn = cur
                    nxt_gen, nxt = (None, None)
                    if ri + 1 < len(rows):
                        nxt_gen, nxt = start_routing(rows[ri + 1][1])
                    x_t, x_n = xnr.next()
                    S.dma(x_t[:], XN[ti * 128:(ti + 1) * 128, :], ["XN"], [x_n])
                    apre, apn = aprer.next()
                    gl, gln = glr.next()
                    gflat = gsm[:].rearrange("p h k -> p (h k)")
                    for s_ in range(128):
                        uv, uvn = uvr.next()
                        dg, dgn = dgr.next()
                        S.op("pool", lambda e, uv=uv, s_=s_, idi=idi: e.indirect_dma_start(
                            out=uv[:], out_offset=None, in_=UVs[layer][:, :],
                            in_offset=bass.IndirectOffsetOnAxis(ap=idi[:, s_:s_ + 1], axis=0)), [idn], [uvn], dma=True)
                        an = "%s_%d" % (apn, s_ % 8)
                        S.stt(uv[:, 0:D], uv[:, 0:D], 1.0, x_t[:], ALU.mult, ALU.mult, [uvn, x_n], [uvn + "u", an], accum_out=apre[:, s_:s_ + 1])
                        S.act(gl[:, s_:s_ + 1], apre[:, s_:s_ + 1], AF.Gelu, [an], [an + "g"])
                        S.act(gl[:, s_:s_ + 1], gl[:, s_:s_ + 1], AF.Copy, [an + "g", gsn], [an + "g"], scale=gflat[:, s_:s_ + 1])
                        S.act(dg[:], identb[:], AF.Copy, ["identb", an + "g"], [dgn], scale=gl[:, s_:s_ + 1])
                        for nb in range(NBK):
                            S.mm(ps[:, 4 + nb, :], dg[:], uv[:, D + nb * 512:D + (nb + 1) * 512], s_ == 0, s_ == 127, [dgn, uvn], [PB(4 + nb)])
                        if nxt_gen is not None:
                            drain(nxt_gen, 2)
                    if nxt_gen is not None:
                        drain(nxt_gen)
                    h_t, h_n = hr.next()
                    S.dma(h_t[:], hbuf[ti * 128:(ti + 1) * 128, :], ["hbuf"], [h_n])
                    for nb in range(NBK):
                        tt_, tn = tmpr.next()
                        S.tt("dve", tt_[:], ps[:, 4 + nb, :], Gate[r][:, nb * 512:(nb + 1) * 512], ALU.mult, [PB(4 + nb), "gate%d" % r], [tn])
                        S.tt("dve", h_t[:, nb * 512:(nb + 1) * 512], h_t[:, nb * 512:(nb + 1) * 512], tt_[:], ALU.add, [h_n, tn], [h_n])
                    S.dma(hbuf[ti * 128:(ti + 1) * 128, :], h_t[:], [h_n], ["hbuf"], eng="act")
                    cur = nxt
            S.barrier()

        for i in range(DEPTH):
            is_attn = (i % 2) == 0
            j = i // 2
            ctx_update = i < last_attn
            lat_rows = [(0, t) for t in range(NLT)]
            ctx_rows = [(1, NLT + t) for t in range(NCT)]
            if is_attn:
                attention_phase(i, j, ctx_update)
                outproj_phase(i, w_o[j], lat_rows + (ctx_rows if ctx_update else []))
            else:
                hyena_spectra(j, "L", L, conv_range=(NE // 128, DEPTH * NE // 128) if i == 1 else None)
                hyena_inproj(i, j, ctx_update)
                hyena_conv(j, "L", L, 0)
                if ctx_update:
                    hyena_spectra(j, "C", C)
                    hyena_conv(j, "C", C, L)
                outproj_phase(i, hy_wout[j], lat_rows + (ctx_rows if ctx_update else []))
            if stop_after == "mix%d" % i:
                break
            peer_phase(i, ctx_update)
            if stop_after == "ffn%d" % i:
                break

        st, alloc = phase()
        with st:
            if stop_after is None:
                gt = alloc("fg", [128, D], F32)
                S.dma(gt[:], fing[0:1, :].to_broadcast([128, D]), [], ["fg"])
            hr = Ring(alloc, "hF", [128, D], F32, 2)
            orr = Ring(alloc, "oF", [128, D], F32, 2)
            junk = alloc("junkF", [128, D], BF16)
            sr = Ring(alloc, "ssF", [128, 2], F32, 2)
            for t in range(NLT):
                ht, hn = hr.next()
                S.dma(ht[:], hbuf[t * 128:(t + 1) * 128, :], ["hbuf"], [hn])
                if stop_after is None:
                    ot, on = orr.next()
                    ss, sn = sr.next()
                    S.act(junk[:], ht[:], AF.Square, [hn], ["junkF", sn], accum_out=ss[:, 0:1])
                    S.ts("dve", ss[:, 1:2], ss[:, 0:1], 1.0 / D, RMS_EPS, ALU.mult, ALU.add, [sn], [sn])
                    S.act(ss[:, 1:2], ss[:, 1:2], AF.Sqrt, [sn], [sn])
                    S.op("dve", lambda e, ss=ss: e.reciprocal(ss[:, 1:2], ss[:, 1:2]), [sn], [sn])
                    S.stt(ot[:], ht[:], ss[:, 1:2], gt[:], ALU.mult, ALU.mult, [hn, sn, "fg"], [on])
                    S.dma(out_d[t * 128:(t + 1) * 128, :], ot[:], [on], ["out"], eng="pool")
                else:
                    S.dma(out_d[t * 128:(t + 1) * 128, :], ht[:], [hn], ["out"], eng="pool")
        S.barrier()
        S.emit()
        nops = S.nops
    return nc, nops


def make_in_maps(inputs, cfg, cores):
    D, L, C = cfg["D"], cfg["L"], cfg["C"]
    KD = D // 128
    f = lambda a: np.ascontiguousarray(np.asarray(a, dtype=np.float32))
    shared = {}
    for k in ("ada_w", "ada_b", "norm_mix_g", "norm_ffn_g", "attn_w_qkv", "attn_w_o", "attn_sink", "hy_w_in",
              "hy_conv_w", "hy_conv_b", "hy_f_w1", "hy_f_w2", "hy_f_w3", "hy_decay", "hy_fbias", "hy_w_out", "peer_w_q"):
        shared[k] = f(inputs[k])
    shared["final_g"] = f(inputs["final_g"]).reshape(1, D)
    shared["hy_f_b1"] = f(inputs["hy_f_b1"])[..., None]
    shared["hy_f_b2"] = f(inputs["hy_f_b2"])[..., None]
    k1 = f(inputs["peer_keys1"])
    k2 = f(inputs["peer_keys2"])
    ks = np.stack([k1, k2], axis=2)
    shared["peer_keysT"] = np.ascontiguousarray(ks.transpose(0, 4, 1, 2, 3).reshape(ks.shape[0], 128, 16, 128))
    shared["peer_u"] = f(inputs["peer_u"]).reshape(-1, D)
    shared["peer_v"] = f(inputs["peer_v"]).reshape(-1, D)
    shared.update(host_consts(cfg))
    c_ctx = f(inputs["c_ctx"])
    maps = []
    for b in cores:
        m = dict(shared)
        m["x"] = f(inputs["x"][b])
        m["ctx"] = f(inputs["ctx"][b])
        cb = f(inputs["c"][b])
        cT = np.stack([cb.reshape(KD, 128).T, c_ctx.reshape(KD, 128).T], axis=-1)
        m["cT"] = np.ascontiguousarray(cT)
        maps.append(m)
    return maps


_NC_CACHE = {}


def kernel(**inputs):
    cfg = CFG
    key = (cfg["D"], cfg["L"], cfg["C"])
    if key not in _NC_CACHE:
        _NC_CACHE[key] = build_program(cfg)[0]
    nc = _NC_CACHE[key]
    B = inputs["x"].shape[0]
    maps = make_in_maps(inputs, cfg, list(range(B)))
    res = run_bass_kernel_spmd(nc, maps, core_ids=list(range(B)))
    return np.stack([np.asarray(r["out"], dtype=np.float32) for r in res.results], axis=0)
```

```python
import math
from contextlib import ExitStack
import numpy as np
import ml_dtypes
import concourse.bass as bass
import concourse.mybir as mybir
from concourse.bass_utils import run_bass_kernel_spmd

F32 = mybir.dt.float32
BF16 = mybir.dt.bfloat16
I32 = mybir.dt.int32
U32 = mybir.dt.uint32
ALU = mybir.AluOpType
AF = mybir.ActivationFunctionType
AX = mybir.AxisListType

ENGS = ("pe", "act", "dve", "pool", "sp")
N_DMA_SEMS = 56
EPOCH_N = 12000

CFG = dict(D=2048, L=2048, C=256, DEPTH=4, NCORES=8)
NEG = -1e30
RMS_EPS = 1e-6
NKV = 4
PH, PK, TOPK, DQ = 8, 128, 16, 2048
NE = PK * PK
HYH = 64
HYE = 33


class Sched:
    def __init__(self, nc, stack):
        self.nc = nc
        self.stack = stack
        self.epoch = {e: 0 for e in ENGS}
        self.ops = {e: [] for e in ENGS}
        self.cnt = {e: 0 for e in ENGS}
        self.seen = {e: {} for e in ENGS}
        self.res = {}
        self.sems = {}
        for e in ENGS:
            self.sems[(e, 0)] = stack.enter_context(nc.semaphore("s_" + e))
        self.dma_cnt = [0] * N_DMA_SEMS
        self.dma_rr = 0
        for j in range(N_DMA_SEMS):
            self.sems[("d", j)] = stack.enter_context(nc.semaphore("s_d%d" % j))
        self.nops = 0

    def _need(self, eng, tok, waits):
        if tok is None:
            return
        k, v = tok
        if k[0] == eng and eng == "pe":
            return
        if self.seen[eng].get(k, 0) >= v:
            return
        self.seen[eng][k] = v
        waits.append((k, v))

    def op(self, eng, fn, reads=(), writes=(), dma=False):
        waits = []
        for r in reads:
            st = self.res.get(r)
            if st is not None:
                self._need(eng, st[0], waits)
        for w in writes:
            st = self.res.get(w)
            if st is not None:
                self._need(eng, st[0], waits)
                for t in st[1].items():
                    self._need(eng, t, waits)
        if dma:
            j = self.dma_rr
            self.dma_rr = (self.dma_rr + 1) % N_DMA_SEMS
            if self.dma_cnt[j] > 0:
                self._need(eng, (("d", j), 16 * self.dma_cnt[j]), waits)
            self.dma_cnt[j] += 1
            tok = (("d", j), 16 * self.dma_cnt[j])
            inc = 16
        else:
            self.cnt[eng] += 1
            tok = ((eng, self.epoch[eng]), self.cnt[eng])
            inc = 1
        k, v = tok
        for r in reads:
            st = self.res.setdefault(r, [None, {}])
            if st[1].get(k, 0) < v:
                st[1][k] = v
        for w in writes:
            self.res[w] = [tok, {}]
        self.ops[eng].append((waits, fn, tok[0], inc))
        self.nops += 1
        return tok

    def barrier(self):
        toks = [((e, self.epoch[e]), self.cnt[e]) for e in ENGS if self.cnt[e] > 0]
        toks += [(("d", j), 16 * self.dma_cnt[j]) for j in range(N_DMA_SEMS) if self.dma_cnt[j] > 0]
        for e in ENGS:
            waits = []
            for t in toks:
                self._need(e, t, waits)
            self.ops[e].append((waits, None, None, 0))
        self.res = {}
        for e in ENGS:
            if self.cnt[e] > EPOCH_N:
                self.epoch[e] += 1
                self.cnt[e] = 0
                self.sems[(e, self.epoch[e])] = self.stack.enter_context(
                    self.nc.semaphore("s_%s_%d" % (e, self.epoch[e])))

    def emit(self):
        nc = self.nc
        sems = self.sems
        with nc.Block() as block:
            def mk(engname):
                lst = self.ops[engname]

                def body(e):
                    for waits, fn, semkey, inc in lst:
                        for k, v in waits:
                            e.wait_ge(sems[k], v)
                        if fn is not None:
                            fn(e).then_inc(sems[semkey], inc)
                return body
            block.tensor(mk("pe"))
            block.scalar(mk("act"))
            block.vector(mk("dve"))
            block.gpsimd(mk("pool"))
            block.sync(mk("sp"))

    def dma(self, out, in_, r, w, eng="sp"):
        self.op(eng, lambda e: e.dma_start(out=out, in_=in_), r, w, dma=True)

    def dma_k(self, out, in_, r, w, eng="sp", g=4):
        K_ = out.shape[1]
        for k0 in range(0, K_, g):
            k1 = min(K_, k0 + g)
            self.dma(out[:, k0:k1, :], in_[:, k0:k1, :], r, w, eng)

    def mm(self, out, lhsT, rhs, start, stop, r, w):
        self.op("pe", lambda e: e.matmul(out, lhsT, rhs, start=start, stop=stop), r, w)

    def tr(self, out, in_, ident, r, w):
        self.op("pe", lambda e: e.transpose(out, in_, ident), r, w)

    def act(self, out, in_, func, r, w, bias=None, scale=None, accum_out=None):
        kw = {}
        if bias is not None:
            kw["bias"] = bias
        if scale is not None:
            kw["scale"] = scale
        if accum_out is not None:
            kw["accum_out"] = accum_out
        self.op("act", lambda e: e.activation(out, in_, func, **kw), r, w)

    def tt(self, eng, out, in0, in1, op, r, w):
        self.op(eng, lambda e: e.tensor_tensor(out, in0, in1, op), r, w)

    def ts(self, eng, out, in0, s1, s2, op0, op1, r, w):
        if op1 is None:
            self.op(eng, lambda e: e.tensor_scalar(out, in0, s1, None, op0), r, w)
        else:
            self.op(eng, lambda e: e.tensor_scalar(out, in0, s1, s2, op0, op1), r, w)

    def stt(self, out, in0, scalar, in1, op0, op1, r, w, accum_out=None):
        if accum_out is None:
            self.op("dve", lambda e: e.scalar_tensor_tensor(out, in0, scalar, in1, op0, op1), r, w)
        else:
            self.op("dve", lambda e: e.scalar_tensor_tensor(out, in0, scalar, in1, op0, op1, accum_out=accum_out), r, w)

    def cp(self, eng, out, in_, r, w):
        if eng == "act":
            self.op("act", lambda e: e.copy(out, in_), r, w)
        else:
            self.op(eng, lambda e: e.tensor_copy(out, in_), r, w)

    def memset(self, eng, ap, val, w):
        self.op(eng, lambda e: e.memset(ap, val), (), w)


class Ring:
    def __init__(self, alloc, name, shape, dt, n):
        self.t = [alloc(name + str(i), shape, dt) for i in range(n)]
        self.names = [name + str(i) for i in range(n)]
        self.i = -1

    def next(self):
        self.i = (self.i + 1) % len(self.t)
        return self.t[self.i], self.names[self.i]


def host_consts(cfg):
    D, L, C = cfg["D"], cfg["L"], cfg["C"]
    k = {}
    k["ident_f"] = np.eye(128, dtype=np.float32)
    k["ident_b"] = np.eye(128).astype(ml_dtypes.bfloat16)
    P = np.zeros((128, 128), np.float32)
    for j in range(32):
        P[32 + j, j] = -1.0
        P[j, 32 + j] = 1.0
        P[96 + j, 64 + j] = -1.0
        P[64 + j, 96 + j] = 1.0
    k["prot"] = P
    GW = 64
    rows = L // GW
    row = np.repeat(np.arange(rows), GW).astype(np.float32)
    col = np.tile(np.arange(GW), rows).astype(np.float32)
    inv = (10000.0 ** (-np.arange(32, dtype=np.float32) / 32)).astype(np.float32)
    ar, ac = row[:, None] * inv, col[:, None] * inv
    ang = np.concatenate([ar, ar, ac, ac], axis=-1)
    k["cosT"] = np.ascontiguousarray(np.cos(ang).T.astype(np.float32))
    k["sinT"] = np.ascontiguousarray(np.sin(ang).T.astype(np.float32))
    q = np.arange(128)[:, None]
    kk = np.arange(384)[None, :]
    masks = np.zeros((128, 3, 384), np.float32)
    for v, off in enumerate((0, 128, 256)):
        masks[:, v, :] = np.where(np.abs(kk - (q + off)) <= 128, 0.0, NEG)
    k["masks"] = masks
    io = np.tile(np.arange(16, dtype=np.float32), 128)[None, :].repeat(128, 0)
    k["iota16"] = np.ascontiguousarray(io)
    for n, tag in ((L, "L"), (C, "C")):
        t = np.arange(n, dtype=np.float64)
        f = np.arange(n, dtype=np.float64)
        th = 2.0 * np.pi * np.outer(t, f) / (2 * n)
        Fre = np.cos(th)
        Fim = -np.sin(th)
        Fim[:, 0] = np.cos(np.pi * t)
        Fimb = np.sin(th)
        Fimb[:, 0] = np.cos(np.pi * t)
        Gre = (2.0 / (2 * n)) * np.cos(th.T)
        Gre[0, :] = 1.0 / (2 * n)
        Gim = -(2.0 / (2 * n)) * np.sin(th.T)
        Gim[0, :] = np.cos(np.pi * t) / (2 * n)
        nt = n // 128

        def fwd_layout(M):
            return np.ascontiguousarray(M.reshape(nt, 128, nt, 128).transpose(2, 1, 0, 3)).astype(ml_dtypes.bfloat16)

        def inv_layout(M):
            return np.ascontiguousarray(M.reshape(nt, 128, nt, 128).transpose(2, 1, 0, 3)).astype(ml_dtypes.bfloat16)
        k["Fre" + tag] = fwd_layout(Fre)
        k["Fim" + tag] = fwd_layout(Fim)
        k["Gre" + tag] = inv_layout(Gre)
        k["Gim" + tag] = inv_layout(Gim)
        tt = np.arange(n, dtype=np.float32)
        t01 = tt / n
        bands = np.arange(1, 17, dtype=np.float32)
        a = 2.0 * math.pi * tt[:, None] * bands[None, :] / n
        z = np.concatenate([t01[:, None], np.cos(a), np.sin(a)], axis=-1).astype(np.float32)
        k["zT" + tag] = np.ascontiguousarray(z.T)
        k["nt01" + tag] = np.ascontiguousarray((-t01).reshape(nt, 128).T.astype(np.float32))
    return k


def build_program(cfg, stop_after=None):
    D, L, C, DEPTH = cfg["D"], cfg["L"], cfg["C"], cfg["DEPTH"]
    KD = D // 128
    T = L + C
    NLT, NCT, NT = L // 128, C // 128, (L + C) // 128
    NQ = D // 128
    GRP = NQ // NKV
    QKVW = (NQ + 2 * NKV) * 128
    NA, NH = (DEPTH + 1) // 2, DEPTH // 2
    CB = D // 512
    last_attn = max(i for i in range(DEPTH) if i % 2 == 0)
    TP = T + 4
    LOFF, COFF = 1, L + 3

    nc = bass.Bass("TRN2", target_bir_lowering=False)
    inp = {}

    def ein(name, shape, dt=F32):
        inp[name] = nc.dram_tensor(name, list(shape), dt, kind="ExternalInput").ap()
        return inp[name]

    x_in = ein("x", [L, D])
    ctx_in = ein("ctx", [C, D])
    cT_in = ein("cT", [128, KD, 2])
    ada_w = ein("ada_w", [DEPTH, D, 6 * D])
    ada_b = ein("ada_b", [DEPTH, 6 * D])
    nmg = ein("norm_mix_g", [DEPTH, D])
    nfg = ein("norm_ffn_g", [DEPTH, D])
    fing = ein("final_g", [1, D])
    w_qkv = ein("attn_w_qkv", [NA, D, QKVW])
    w_o = ein("attn_w_o", [NA, D, D])
    sink_in = ein("attn_sink", [NA, NQ])
    hy_w_in = ein("hy_w_in", [NH, D, 3 * D])
    hy_cwT = ein("hy_conv_wT", [NH, 128, 3 * D // 128, 3])
    hy_cbT = ein("hy_conv_bT", [NH, 128, 3 * D // 128])
    hy_w1 = ein("hy_f_w1", [NH, HYE, HYH])
    hy_b1 = ein("hy_f_b1", [NH, HYH, 1])
    hy_w2 = ein("hy_f_w2", [NH, 2, HYH, HYH])
    hy_b2 = ein("hy_f_b2", [NH, 2, HYH, 1])
    hy_w3 = ein("hy_f_w3", [NH, HYH, 4 * D])
    hy_dec = ein("hy_decay", [NH, 2, D])
    hy_fb = ein("hy_fbias", [NH, 2, D])
    hy_wout = ein("hy_w_out", [NH, D, D])
    p_wq = ein("peer_w_q", [DEPTH, D, DQ])
    p_keysT = ein("peer_keysT", [DEPTH, 128, 16, 128])
    p_u = ein("peer_u", [DEPTH * NE, D])
    p_v = ein("peer_v", [DEPTH * NE, D])
    cst = {}
    cst["ident_f"] = ein("ident_f", [128, 128])
    cst["ident_b"] = ein("ident_b", [128, 128], BF16)
    cst["prot"] = ein("prot", [128, 128])
    cst["cosT"] = ein("cosT", [128, L])
    cst["sinT"] = ein("sinT", [128, L])
    cst["masks"] = ein("masks", [128, 3, 384])
    cst["iota16"] = ein("iota16", [128, 2048])
    for n, tag in ((L, "L"), (C, "C")):
        nt = n // 128
        for nm in ("Fre", "Fim", "Gre", "Gim"):
            cst[nm + tag] = ein(nm + tag, [nt, 128, nt, 128], BF16)
        cst["zT" + tag] = ein("zT" + tag, [HYE, n])
        cst["nt01" + tag] = ein("nt01" + tag, [128, nt])

    out_d = nc.dram_tensor("out", [L, D], F32, kind="ExternalOutput").ap()

    def scratch(name, shape, dt=F32):
        return nc.dram_tensor(name, list(shape), dt, kind="Internal").ap()

    hbuf = scratch("hbuf", [T, D])
    moddr = scratch("moddr", [DEPTH, 2, 6 * D])
    MT = scratch("MT", [D, T], BF16)
    XN = scratch("XN", [T, D])
    QT = scratch("QT", [16, 128, T], BF16)
    U_d = scratch("U_d", [T, 3 * D])
    UVs = [scratch("UV%d" % l_, [NE, 2 * D], BF16) for l_ in range(DEPTH)]
    spec = {}
    for tag, n in (("L", L), ("C", C)):
        for o in range(2):
            spec[(tag, o, "re")] = scratch("specre%s%d" % (tag, o), [n, D])
            spec[(tag, o, "im")] = scratch("specim%s%d" % (tag, o), [n, D])
            spec[(tag, o, "d0")] = scratch("specd0%s%d" % (tag, o), [128, D])

    with ExitStack() as top:
        S = Sched(nc, top)
        ps = top.enter_context(nc.psum_tensor("ps", [128, 8, 512], F32))
        psb = ps[:, :, :].bitcast(BF16)

        uid = [0]

        def PB(b):
            return "psb%d" % b

        def phase():
            st = ExitStack()

            def alloc(name, shape, dt):
                uid[0] += 1
                return st.enter_context(nc.sbuf_tensor("%s_%d" % (name, uid[0]), list(shape), dt))
            return st, alloc

        def convert_gen(alloc, ch0, ch1, nring=2):
            fu = Ring(alloc, "cvu", [128, D], F32, nring)
            fv = Ring(alloc, "cvv", [128, D], F32, nring)
            ob = Ring(alloc, "cvo", [128, 2 * D], BF16, nring)
            prev = None

            def store(p):
                S.dma(UVs[p[0] // NE][p[0] % NE:p[0] % NE + 128, :], p[1][:], [p[2]], ["UV"])
            for ch in range(ch0, ch1):
                r0 = ch * 128
                ut, un = fu.next()
                vt, vn = fv.next()
                ot, on = ob.next()
                S.dma(ut[:], p_u[r0:r0 + 128, :], [], [un])
                S.dma(vt[:], p_v[r0:r0 + 128, :], [], [vn])
                if prev is not None:
                    store(prev)
                e1, e2 = ("act", "dve") if ch % 2 == 0 else ("dve", "act")
                S.cp(e1, ot[:, 0:D], ut[:], [un], [on])
                S.cp(e2, ot[:, D:2 * D], vt[:], [vn], [on])
                prev = (r0, ot, on)
                yield
            if prev is not None:
                store(prev)
            yield

        for t in range(NLT):
            S.dma(hbuf[t * 128:(t + 1) * 128, :], x_in[t * 128:(t + 1) * 128, :], [], ["hbuf"])
        for t in range(NCT):
            S.dma(hbuf[L + t * 128:L + (t + 1) * 128, :], ctx_in[t * 128:(t + 1) * 128, :], [], ["hbuf"])
        st, alloc = phase()
        with st:
            cT = alloc("cT", [128, KD, 2], F32)
            cS = alloc("cS", [128, KD, 2], F32)
            S.dma(cT[:], cT_in[:, :, :], [], ["cT"])
            S.act(cS[:], cT[:], AF.Silu, ["cT"], ["cS"])
            NB6 = 6 * D // 512
            GB = 8
            wr = Ring(alloc, "adaw", [128, GB * 512], F32, 2)
            modg = Ring(alloc, "modg", [2, GB * 512], F32, 2)
            biasg = Ring(alloc, "biasg", [2, GB * 512], F32, 2)
            cg = convert_gen(alloc, 0, NE // 128)
            for i in range(DEPTH):
                for g0 in range(0, NB6, GB):
                    nb = min(GB, NB6 - g0)
                    bt, bn = biasg.next()
                    mt_, mn_ = modg.next()
                    S.dma(bt[:, 0:nb * 512], ada_b[i:i + 1, g0 * 512:(g0 + nb) * 512].to_broadcast([2, nb * 512]), [], [bn])
                    for k in range(KD):
                        wt, wn = wr.next()
                        S.dma(wt[:, 0:nb * 512], ada_w[i, k * 128:(k + 1) * 128, g0 * 512:(g0 + nb) * 512], [], [wn])
                        for b in range(nb):
                            S.mm(ps[0:2, b, :], cS[:, k, :], wt[:, b * 512:(b + 1) * 512], k == 0, k == KD - 1, [wn, "cS"], [PB(b)])
                        next(cg, None)
                    for b in range(nb):
                        S.tt("dve", mt_[:, b * 512:(b + 1) * 512], ps[0:2, b, :], bt[:, b * 512:(b + 1) * 512], ALU.add, [PB(b), bn], [mn_])
                    S.dma(moddr[i, :, g0 * 512:(g0 + nb) * 512], mt_[:, 0:nb * 512], [mn_], ["moddr"])
            for _ in cg:
                pass
        S.barrier()

        def load_mod(alloc, name, layer, row, idx, gvec=None):
            t = alloc(name, [128, D], F32)
            S.dma(t[:], moddr[layer, row:row + 1, idx * D:(idx + 1) * D].to_broadcast([128, D]), ["moddr"], [name])
            if gvec is not None:
                gt = alloc(name + "_g", [128, D], F32)
                S.dma(gt[:], gvec.to_broadcast([128, D]), [], [name + "_g"])
                S.stt(t[:], t[:], 1.0, gt[:], ALU.add, ALU.mult, [name, name + "_g"], [name])
            return t

        def modulate_phase(layer, which, rows, aT, xn_store):
            st, alloc = phase()
            with st:
                ident = alloc("ident", [128, 128], F32)
                S.dma(ident[:], cst["ident_f"][:, :], [], ["ident"])
                gvec = (nmg if which == 1 else nfg)[layer:layer + 1, :]
                sc_i, sh_i = (1, 0) if which == 1 else (4, 3)
                need_rows = sorted(set(r for r, _, _ in rows))
                Gt, St = {}, {}
                for r in need_rows:
                    Gt[r] = load_mod(alloc, "G%d" % r, layer, r, sc_i, gvec)
                    St[r] = load_mod(alloc, "S%d" % r, layer, r, sh_i)
                hr = Ring(alloc, "hT", [128, D], F32, 2)
                ar = Ring(alloc, "aT_", [128, D], F32, 2)
                junk = alloc("junk", [128, D], BF16)
                sr = Ring(alloc, "ss", [128, 2], F32, 2)
                pb = 0
                for (r, ti, coff) in rows:
                    ht, hn = hr.next()
                    at, an = ar.next()
                    ss, sn = sr.next()
                    S.dma(ht[:], hbuf[ti * 128:(ti + 1) * 128, :], ["hbuf"], [hn])
                    S.act(junk[:], ht[:], AF.Square, [hn], ["junk", sn], accum_out=ss[:, 0:1])
                    S.ts("dve", ss[:, 1:2], ss[:, 0:1], 1.0 / D, RMS_EPS, ALU.mult, ALU.add, [sn], [sn])
                    S.act(ss[:, 1:2], ss[:, 1:2], AF.Sqrt, [sn], [sn])
                    S.op("dve", lambda e, ss=ss: e.reciprocal(ss[:, 1:2], ss[:, 1:2]), [sn], [sn])
                    S.stt(at[:], ht[:], ss[:, 1:2], Gt[r][:], ALU.mult, ALU.mult, [hn, sn, "G%d" % r], [an])
                    S.tt("dve", at[:], at[:], St[r][:], ALU.add, [an, "S%d" % r], [an])
                    if xn_store:
                        S.dma(XN[ti * 128:(ti + 1) * 128, :], at[:], [an], ["XN"], eng="pool")
                    for k0 in range(0, KD, 4):
                        b = pb % 2
                        pb += 1
                        for kk in range(4):
                            k = k0 + kk
                            S.tr(ps[:, b, kk * 128:(kk + 1) * 128], at[:, k * 128:(k + 1) * 128], ident[:], [an, "ident"], [PB(b)])
                        S.cp("act" if (k0 // 4) % 2 == 0 else "dve", aT[:, k0:k0 + 4, coff:coff + 128],
                             ps[:, b, :].rearrange("p (k t) -> p k t", k=4), [PB(b)], ["aT"])

        def outproj_phase(layer, wmat, rows):
            st, alloc = phase()
            with st:
                wob = alloc("wob", [128, KD, D], BF16)
                wst = Ring(alloc, "wst", [128, D], F32, 2)
                for k in range(KD):
                    wt, wn = wst.next()
                    S.dma(wt[:], wmat[k * 128:(k + 1) * 128, :], [], [wn])
                    S.cp("dve" if k % 2 else "act", wob[:, k, :], wt[:], [wn], ["wob"])
                need_rows = sorted(set(r for r, _ in rows))
                Gate = {r: load_mod(alloc, "gate%d" % r, layer, r, 2) for r in need_rows}
                mr = Ring(alloc, "mT", [128, KD, 128], BF16, 2)
                hr = Ring(alloc, "hO", [128, D], F32, 2)
                tmp = Ring(alloc, "tmpO", [128, 512], F32, 2)
                NBK = D // 512
                pbi = 0
                for (r, ti) in rows:
                    mt, mn = mr.next()
                    ht, hn = hr.next()
                    S.dma_k(mt[:], MT[:, ti * 128:(ti + 1) * 128].rearrange("(k p) t -> p k t", p=128), ["MT"], [mn])
                    S.dma(ht[:], hbuf[ti * 128:(ti + 1) * 128, :], ["hbuf"], [hn])
                    for nb in range(NBK):
                        b = pbi % 4
                        pbi += 1
                        for k in range(KD):
                            S.mm(ps[:, b, :], mt[:, k, :], wob[:, k, nb * 512:(nb + 1) * 512], k == 0, k == KD - 1, [mn, "wob"], [PB(b)])
                        tt_, tn = tmp.next()
                        S.tt("dve", tt_[:], ps[:, b, :], Gate[r][:, nb * 512:(nb + 1) * 512], ALU.mult, [PB(b), "gate%d" % r], [tn])
                        S.tt("pool", ht[:, nb * 512:(nb + 1) * 512], ht[:, nb * 512:(nb + 1) * 512], tt_[:], ALU.add, [hn, tn], [hn])
                    S.dma(hbuf[ti * 128:(ti + 1) * 128, :], ht[:], [hn], ["hbuf"], eng="pool")
            S.barrier()

        def attention_phase(layer, j, ctx_q):
            Tq = L + (C if ctx_q else 0)
            st, alloc = phase()
            with st:
                aT = alloc("aT", [128, KD, TP], BF16)
                rows = [(0, t, LOFF + t * 128) for t in range(NLT)] + [(1, NLT + t, COFF + t * 128) for t in range(NCT)]
                modulate_phase(layer, 1, rows, aT, False)
                S.barrier()
                st2, alloc2 = phase()
                with st2:
                    alloc = alloc2
                    identb = alloc("identb", [128, 128], BF16)
                    prot = alloc("prot", [128, 128], F32)
                    cosT = alloc("cosT", [128, L], F32)
                    sinT = alloc("sinT", [128, L], F32)
                    masks = alloc("masks", [128, 3, 384], F32)
                    sinkb = alloc("sinkb", [128, NQ], F32)
                    nsinkb = alloc("nsinkb", [128, NQ], F32)
                    S.dma(identb[:], cst["ident_b"][:, :], [], ["identb"])
                    S.dma(prot[:], cst["prot"][:, :], [], ["prot"])
                    S.dma(cosT[:], cst["cosT"][:, :], [], ["cosT"])
                    S.dma(sinT[:], cst["sinT"][:, :], [], ["sinT"])
                    S.dma(masks[:], cst["masks"][:, :, :], [], ["masks"])
                    S.dma(sinkb[:], sink_in[j:j + 1, :].to_broadcast([128, NQ]), [], ["sinkb"])
                    S.ts("dve", nsinkb[:], sinkb[:], -1.0, None, ALU.mult, None, ["sinkb"], ["nsinkb"])
                    wf = Ring(alloc, "wf", [128, KD, 128], F32, 1)
                    wb = Ring(alloc, "wb", [128, KD, 128], BF16, 2)
                    kT = alloc("kT", [128, T], BF16)
                    vtok = alloc("vtok", [128, NT, 128], BF16)
                    qT = alloc("qT", [128, T], BF16)
                    oT = Ring(alloc, "oT", [128, T], BF16, 2)
                    qf = Ring(alloc, "qf", [128, 512], F32, 2)
                    t1 = Ring(alloc, "t1", [128, 512], F32, 2)
                    t2 = Ring(alloc, "t2", [128, 512], F32, 2)
                    sb_s = Ring(alloc, "s", [128, 640], F32, 4)
                    sb_p = Ring(alloc, "p", [128, 640], BF16, 4)
                    sb_pn = Ring(alloc, "pn", [128, 640], BF16, 4)
                    sb_pT = Ring(alloc, "pT", [128, 5, 128], BF16, 4)
                    stat = Ring(alloc, "stat", [128, 8], F32, 6)
                    scale = 128 ** -0.5

                    def load_w(c0):
                        wft, wfn = wf.next()
                        wbt, wbn = wb.next()
                        S.dma_k(wft[:], w_qkv[j, :, c0:c0 + 128].rearrange("(k p) c -> p k c", p=128), [], [wfn])
                        load_w.i = getattr(load_w, "i", 0) + 1
                        S.cp("act" if load_w.i % 2 else "dve", wbt[:], wft[:], [wfn], [wbn])
                        return wbt, wbn

                    def proj_fm(wbt, wbn, dst, dname, rope, ranges):
                        for (acol, dcol, n, pos0) in ranges:
                            for c in range(0, n, 512):
                                w = min(512, n - c)
                                b = 0 if (c // 512) % 2 == 0 else 1
                                for k in range(KD):
                                    S.mm(ps[:, b, 0:w], wbt[:, k, :], aT[:, k, acol + c:acol + c + w], k == 0, k == KD - 1, [wbn, "aT"], [PB(b)])
                                if not rope:
                                    S.cp("act", dst[:, dcol + c:dcol + c + w], ps[:, b, 0:w], [PB(b)], [dname])
                                else:
                                    qft, qfn = qf.next()
                                    t1t, t1n = t1.next()
                                    t2t, t2n = t2.next()
                                    S.cp("act", qft[:, 0:w], ps[:, b, 0:w], [PB(b)], [qfn])
                                    S.mm(ps[:, 2, 0:w], prot[:], qft[:, 0:w], True, True, ["prot", qfn], [PB(2)])
                                    S.tt("dve", t1t[:, 0:w], qft[:, 0:w], cosT[:, pos0 + c:pos0 + c + w], ALU.mult, [qfn, "cosT"], [t1n])
                                    S.tt("dve", t2t[:, 0:w], ps[:, 2, 0:w], sinT[:, pos0 + c:pos0 + c + w], ALU.mult, [PB(2), "sinT"], [t2n])
                                    S.tt("pool", dst[:, dcol + c:dcol + c + w], t1t[:, 0:w], t2t[:, 0:w], ALU.add, [t1n, t2n], [dname])

                    for g in range(NKV):
                        wbt, wbn = load_w((NQ + g) * 128)
                        proj_fm(wbt, wbn, kT, "kT", True, [(LOFF, 0, L, 0)])
                        proj_fm(wbt, wbn, kT, "kT", False, [(COFF, L, C, 0)])
                        wbt, wbn = load_w((NQ + NKV + g) * 128)
                        for ti in range(NT):
                            acol = LOFF + ti * 128 if ti < NLT else COFF + (ti - NLT) * 128
                            for k in range(KD):
                                S.mm(ps[:, 7, 0:128], aT[:, k, acol:acol + 128], wbt[:, k, :], k == 0, k == KD - 1, ["aT", wbn], [PB(7)])
                            S.cp("act", vtok[:, ti, :], ps[:, 7, 0:128], [PB(7)], ["vtok"])
                        for hq in range(GRP):
                            h = g * GRP + hq
                            wbt, wbn = load_w(h * 128)
                            proj_fm(wbt, wbn, qT, "qT", True, [(LOFF, 0, L, 0)])
                            if ctx_q:
                                proj_fm(wbt, wbn, qT, "qT", False, [(COFF, L, C, 0)])
                            ot, on = oT.next()
                            nqb = Tq // 128
                            SLOTS = [dict(win=2, ctx=(3, 0), pt=4, o=5), dict(win=6, ctx=(3, 256), pt=7, o=1)]

                            def unit(qb, sl, h=h, ot=ot, on=on):
                                is_lat = qb < NLT
                                s_t, s_n = sb_s.next()
                                p_t, p_n = sb_p.next()
                                pn_t, pn_n = sb_pn.next()
                                pT_t, pT_n = sb_pT.next()
                                stt_, stn = stat.next()
                                qsl = qT[:, qb * 128:(qb + 1) * 128]
                                bw, (bc, cc0), bp, bo = sl["win"], sl["ctx"], sl["pt"], sl["o"]
                                cn = "psb%d_%d" % (bc, cc0)
                                k0 = 0
                                if is_lat:
                                    if qb == 0:
                                        k0, mv = 0, 0
                                    elif qb == NLT - 1:
                                        k0, mv = (NLT - 3) * 128, 2
                                    else:
                                        k0, mv = (qb - 1) * 128, 1
                                    S.mm(ps[:, bw, 0:384], qsl, kT[:, k0:k0 + 384], True, True, ["qT", "kT"], [PB(bw)])
                                    yield
                                    S.stt(s_t[:, 0:384], ps[:, bw, 0:384], scale, masks[:, mv, :], ALU.mult, ALU.add, [PB(bw), "masks"], [s_n])
                                    yield
                                    lo = 0
                                else:
                                    lo = 384
                                S.mm(ps[:, bc, cc0:cc0 + C], qsl, kT[:, L:L + C], True, True, ["qT", "kT"], [cn])
                                yield
                                S.act(s_t[:, 384:384 + C], ps[:, bc, cc0:cc0 + C], AF.Copy, [cn], [s_n], scale=scale)
                                yield
                                hi = 384 + C
                                S.op("dve", lambda e, o=stt_[:, 0:1], i=s_t[:, lo:hi]: e.tensor_reduce(o, i, AX.X, ALU.max, negate=True), [s_n], [stn])
                                yield
                                S.tt("dve", stt_[:, 1:2], stt_[:, 0:1], nsinkb[:, h:h + 1], ALU.min, [stn, "nsinkb"], [stn])
                                yield
                                S.act(p_t[:, lo:hi], s_t[:, lo:hi], AF.Exp, [s_n, stn], [p_n, stn], bias=stt_[:, 1:2], accum_out=stt_[:, 2:3])
                                yield
                                S.act(stt_[:, 3:4], sinkb[:, h:h + 1], AF.Exp, ["sinkb", stn], [stn], bias=stt_[:, 1:2])
                                yield
                                S.tt("dve", stt_[:, 4:5], stt_[:, 2:3], stt_[:, 3:4], ALU.add, [stn], [stn])
                                yield
                                S.op("dve", lambda e, o=stt_[:, 5:6], i=stt_[:, 4:5]: e.reciprocal(o, i), [stn], [stn])
                                yield
                                S.ts("dve", pn_t[:, lo:hi], p_t[:, lo:hi], stt_[:, 5:6], None, ALU.mult, None, [p_n, stn], [pn_n])
                                yield
                                nch = (hi - lo) // 128
                                c_lo = lo // 128
                                for cc in range(nch):
                                    c = c_lo + cc
                                    S.tr(psb[:, bp, c * 128:(c + 1) * 128], pn_t[:, c * 128:(c + 1) * 128], identb[:], [pn_n, "identb"], [PB(bp)])
                                yield
                                S.cp("act", pT_t[:, c_lo:c_lo + nch, :], psb[:, bp, c_lo * 128:(c_lo + nch) * 128].rearrange("p (c t) -> p c t", t=128), [PB(bp)], [pT_n])
                                yield
                                for cc in range(nch):
                                    c = c_lo + cc
                                    if c < 3:
                                        vt = k0 // 128 + c
                                    else:
                                        vt = NLT + (c - 3)
                                    S.mm(ps[:, bo, 0:128], vtok[:, vt, :], pT_t[:, c, :], cc == 0, cc == nch - 1, ["vtok", pT_n], [PB(bo)])
                                yield
                                S.cp("dve", ot[:, qb * 128:(qb + 1) * 128], ps[:, bo, 0:128], [PB(bo)], [on + "_%d" % (qb % 2)])
                                yield

                            for qb0 in range(0, nqb, 2):
                                gens = [unit(qb0 + u, SLOTS[u]) for u in range(min(2, nqb - qb0))]
                                while gens:
                                    for g_ in list(gens):
                                        try:
                                            next(g_)
                                        except StopIteration:
                                            gens.remove(g_)
                            S.dma(MT[h * 128:(h + 1) * 128, 0:Tq], ot[:, 0:Tq], [on + "_0", on + "_1"], ["MT"], eng="pool")
            S.barrier()

        def hyena_spectra(j, tag, n, conv_range=None):
            nt = n // 128
            st, alloc = phase()
            with st:
                hA = alloc("hA", [HYH, n], F32)
                nt01 = alloc("nt01", [128, nt], F32)
                S.dma(nt01[:], cst["nt01" + tag][:, :], [], ["nt01"])
                st_m, alloc_m = phase()
                with st_m:
                    zT = alloc_m("zT", [HYE, n], F32)
                    w1 = alloc_m("w1", [HYE, HYH], F32)
                    w2 = alloc_m("w2", [HYH, 2, HYH], F32)
                    b1 = alloc_m("b1", [HYH, 1], F32)
                    b2 = alloc_m("b2", [HYH, 2], F32)
                    hB = alloc_m("hB", [HYH, n], F32)
                    S.dma(zT[:], cst["zT" + tag][:, :], [], ["zT"])
                    S.dma(w1[:], hy_w1[j, :, :], [], ["w1"])
                    for i in range(2):
                        S.dma(w2[:, i, :], hy_w2[j, i, :, :], [], ["w2"])
                        S.dma(b2[:, i:i + 1], hy_b2[j, i, :, :], [], ["b2"])
                    S.dma(b1[:], hy_b1[j, :, :], [], ["b1"])
                    pre = Ring(alloc_m, "pre", [HYH, 512], F32, 2)
                    TWO_PI = 2.0 * math.pi

                    def sin_layer(lhsT, lname, bias_ap, bname, src, sname, dst, dname, kdim):
                        for c in range(0, n, 512):
                            w = min(512, n - c)
                            b = (c // 512) % 2
                            S.mm(ps[0:HYH, b, 0:w], lhsT, src[0:kdim, c:c + w], True, True, [lname, sname], [PB(b)])
                            pt, pn = pre.next()
                            S.act(pt[:, 0:w], ps[0:HYH, b, 0:w], AF.Identity, [PB(b), bname], [pn], bias=bias_ap)
                            for _ in range(2):
                                S.ts("dve", dst[:, c:c + w], pt[:, 0:w], math.pi, -TWO_PI, ALU.is_gt, ALU.mult, [pn], [dname])
                                S.tt("dve", pt[:, 0:w], pt[:, 0:w], dst[:, c:c + w], ALU.add, [pn, dname], [pn])
                                S.ts("dve", dst[:, c:c + w], pt[:, 0:w], -math.pi, TWO_PI, ALU.is_lt, ALU.mult, [pn], [dname])
                                S.tt("dve", pt[:, 0:w], pt[:, 0:w], dst[:, c:c + w], ALU.add, [pn, dname], [pn])
                            S.act(dst[:, c:c + w], pt[:, 0:w], AF.Sin, [pn], [dname])
                    sin_layer(w1[:], "w1", b1[:, 0:1], "b1", zT, "zT", hA, "hA", HYE)
                    sin_layer(w2[:, 0, :], "w2", b2[:, 0:1], "b2", hA, "hA", hB, "hB", HYH)
                    sin_layer(w2[:, 1, :], "w2", b2[:, 1:2], "b2", hB, "hB", hA, "hA", HYH)
                S.barrier()
                cg = convert_gen(alloc, conv_range[0], conv_range[1]) if conv_range else None

                def pull(k):
                    if cg is not None:
                        for _ in range(k):
                            next(cg, None)
                w3f = Ring(alloc, "w3f", [HYH, 512], F32, 2)
                w3b = Ring(alloc, "w3b", [HYH, 512], F32, 2)
                absd = Ring(alloc, "absd", [128, 512], F32, 2)
                wnd = Ring(alloc, "wnd", [128, 512], F32, 2)
                ff = Ring(alloc, "ff", [128, 512], F32, 2)
                fbk = Ring(alloc, "fbk", [128, 512], F32, 2)
                FS = alloc("FS", [128, nt, 512], BF16)
                FD = alloc("FD", [128, nt, 512], BF16)
                fre = Ring(alloc, "fre", [128, nt, 128], BF16, 2)
                fim = Ring(alloc, "fim", [128, nt, 128], BF16, 2)
                ore = Ring(alloc, "ore", [128, 512], F32, 2)
                oim = Ring(alloc, "oim", [128, 512], F32, 2)
                od0 = alloc("od0", [128, 512], F32)
                for o in range(2):
                    for cb in range(CB):
                        c0 = cb * 512
                        wft, wfn = w3f.next()
                        wbt, wbn = w3b.next()
                        adt, adn = absd.next()
                        S.dma(wft[:], hy_w3[j, :, o * 2 * D + c0:o * 2 * D + c0 + 512], [], [wfn])
                        S.dma(wbt[:], hy_w3[j, :, o * 2 * D + D + c0:o * 2 * D + D + c0 + 512], [], [wbn])
                        S.dma(adt[:], hy_dec[j, o:o + 1, c0:c0 + 512].to_broadcast([128, 512]), [], [adn])
                        S.act(adt[:], adt[:], AF.Abs, [adn], [adn])
                        for lt in range(nt):
                            wt, wn = wnd.next()
                            fft, ffn = ff.next()
                            fbt, fbn = fbk.next()
                            S.mm(ps[:, 0, :], hA[:, lt * 128:(lt + 1) * 128], wft[:], True, True, ["hA", wfn], [PB(0)])
                            S.mm(ps[:, 1, :], hA[:, lt * 128:(lt + 1) * 128], wbt[:], True, True, ["hA", wbn], [PB(1)])
                            S.act(wt[:], adt[:], AF.Exp, [adn, "nt01"], [wn], scale=nt01[:, lt:lt + 1])
                            S.stt(fft[:], wt[:], 0.05, ps[:, 0, :], ALU.add, ALU.mult, [wn, PB(0)], [ffn])
                            S.stt(fbt[:], wt[:], 0.05, ps[:, 1, :], ALU.add, ALU.mult, [wn, PB(1)], [fbn])
                            if lt == 0:
                                S.memset("dve", fbt[0:1, :], 0.0, [fbn])
                            S.tt("pool", FS[:, lt, :], fft[:], fbt[:], ALU.add, [ffn, fbn], ["FS"])
                            S.tt("dve", FD[:, lt, :], fft[:], fbt[:], ALU.subtract, [ffn, fbn], ["FD"])
                            pull(1)
                        for kf in range(nt):
                            fr_t, fr_n = fre.next()
                            fi_t, fi_n = fim.next()
                            S.dma(fr_t[:], cst["Fre" + tag][kf, :, :, :], [], [fr_n])
                            S.dma(fi_t[:], cst["Fim" + tag][kf, :, :, :], [], [fi_n])
                            for lt in range(nt):
                                S.mm(ps[:, 2, :], fr_t[:, lt, :], FS[:, lt, :], lt == 0, lt == nt - 1, [fr_n, "FS"], [PB(2)])
                            for lt in range(nt):
                                S.mm(ps[:, 3, :], fi_t[:, lt, :], FD[:, lt, :], lt == 0, lt == nt - 1, [fi_n, "FD"], [PB(3)])
                            if kf == 0:
                                for lt in range(nt):
                                    S.mm(ps[0:1, 4, :], fi_t[:, lt, 0:1], FS[:, lt, :], lt == 0, lt == nt - 1, [fi_n, "FS"], [PB(4)])
                            ot, on = ore.next()
                            it, in_ = oim.next()
                            S.cp("act", ot[:], ps[:, 2, :], [PB(2)], [on])
                            S.cp("dve", it[:], ps[:, 3, :], [PB(3)], [in_])
                            if kf == 0:
                                S.cp("act", od0[:], ot[:], [on], ["od0"])
                                S.cp("dve", od0[0:1, :], ps[0:1, 4, :], [PB(4), "od0"], ["od0"])
                                S.memset("dve", it[0:1, :], 0.0, [in_])
                                S.dma(spec[(tag, o, "d0")][:, c0:c0 + 512], od0[:], ["od0"], ["spec"])
                            S.dma(spec[(tag, o, "re")][kf * 128:(kf + 1) * 128, c0:c0 + 512], ot[:], [on], ["spec"])
                            S.dma(spec[(tag, o, "im")][kf * 128:(kf + 1) * 128, c0:c0 + 512], it[:], [in_], ["spec"])
                            pull(2)
                if cg is not None:
                    for _ in cg:
                        pass
            S.barrier()

        def hyena_inproj(layer, j, with_ctx):
            st, alloc = phase()
            with st:
                aT = alloc("aT", [128, KD, TP], BF16)
                S.memset("pool", aT[:, :, 0:1], 0.0, ["aT"])
                S.memset("pool", aT[:, :, L + 1:L + 3], 0.0, ["aT"])
                S.memset("pool", aT[:, :, TP - 1:TP], 0.0, ["aT"])
                rows = [(0, t, LOFF + t * 128) for t in range(NLT)]
                if with_ctx:
                    rows += [(1, NLT + t, COFF + t * 128) for t in range(NCT)]
                modulate_phase(layer, 1, rows, aT, False)
                S.barrier()
                st2, alloc = phase()
                with st2:
                    NCH = 3 * D // 128
                    identf = alloc("identf", [128, 128], F32)
                    cwT = alloc("cwT", [128, NCH, 3], F32)
                    cbT = alloc("cbT", [128, NCH], F32)
                    S.dma(identf[:], cst["ident_f"][:, :], [], ["identf"])
                    S.dma(cwT[:], hy_cwT[j, :, :, :], [], ["cwT"])
                    S.dma(cbT[:], hy_cbT[j, :, :], [], ["cbT"])
                    wf = Ring(alloc, "wf", [128, KD, 128], F32, 2)
                    wb = Ring(alloc, "wb", [128, KD, 128], BF16, 2)
                    WN = 3
                    ob = Ring(alloc, "ocv", [128, WN * 128], F32, 3)
                    stage = alloc("stageU", [128, NT, 512], F32)
                    wins = []
                    for t0 in range(0, NLT, WN):
                        wins.append((t0, min(NLT, t0 + WN), LOFF + t0 * 128))
                    if with_ctx:
                        for t0 in range(0, NCT, WN):
                            wins.append((NLT + t0, NLT + min(NCT, t0 + WN), COFF + t0 * 128))
                    ntile_used = NLT + (NCT if with_ctx else 0)
                    pbi = 0
                    for cc in range(NCH):
                        wft, wfn = wf.next()
                        wbt, wbn = wb.next()
                        S.dma_k(wft[:], hy_w_in[j, :, cc * 128:(cc + 1) * 128].rearrange("(k p) c -> p k c", p=128), [], [wfn])
                        S.cp("act" if cc % 2 == 0 else "dve", wbt[:], wft[:], [wfn], [wbn])
                        c4 = cc % 4
                        for (ta, tb, acol) in wins:
                            nout = (tb - ta) * 128
                            nin = nout + 2
                            b = pbi % 2
                            b2 = 2 + pbi % 2
                            pbi += 1
                            for k in range(KD):
                                S.mm(ps[:, b, 0:nin], wbt[:, k, :], aT[:, k, acol - 1:acol - 1 + nin], k == 0, k == KD - 1, [wbn, "aT"], [PB(b)])
                            ot, on = ob.next()
                            S.act(ot[:, 0:nout], ps[:, b, 1:1 + nout], AF.Identity, [PB(b), "cwT", "cbT"], [on],
                                  bias=cbT[:, cc:cc + 1], scale=cwT[:, cc, 1:2])
                            S.stt(ot[:, 0:nout], ps[:, b, 0:nout], cwT[:, cc, 0:1], ot[:, 0:nout], ALU.mult, ALU.add, [PB(b), "cwT", on], [on])
                            S.stt(ot[:, 0:nout], ps[:, b, 2:2 + nout], cwT[:, cc, 2:3], ot[:, 0:nout], ALU.mult, ALU.add, [PB(b), "cwT", on], [on])
                            for tl in range(tb - ta):
                                S.tr(ps[:, b2, tl * 128:(tl + 1) * 128], ot[:, tl * 128:(tl + 1) * 128], identf[:], [on, "identf"], [PB(b2)])
                            S.cp("act" if pbi % 2 == 0 else "dve", stage[:, ta:tb, c4 * 128:(c4 + 1) * 128],
                                 ps[:, b2, 0:nout].rearrange("p (t c) -> p t c", c=128), [PB(b2)], ["stageU"])
                        if c4 == 3:
                            g0 = (cc // 4) * 512
                            for t0 in range(0, ntile_used, 4):
                                t1 = min(ntile_used, t0 + 4)
                                S.dma(U_d[t0 * 128:t1 * 128, g0:g0 + 512].rearrange("(t p) c -> p t c", p=128), stage[:, t0:t1, :], ["stageU"], ["U_d"], eng="pool")
            S.barrier()

        def hyena_conv(j, tag, n, toff):
            nt = n // 128
            st, alloc = phase()
            with st:
                identb = alloc("identb", [128, 128], BF16)
                S.dma(identb[:], cst["ident_b"][:, :], [], ["identb"])
                vbuf = alloc("vbuf", [128, nt, 512], F32)
                zsb = alloc("zsb", [128, nt, 512], BF16)
                Yre = alloc("Yre", [128, nt, 512], BF16)
                Yim = alloc("Yim", [128, nt, 512], BF16)
                fre = Ring(alloc, "fre", [128, nt, 128], BF16, 2)
                fim = Ring(alloc, "fim", [128, nt, 128], BF16, 2)
                sA = Ring(alloc, "sA", [128, 512], F32, 2)
                sB = Ring(alloc, "sB", [128, 512], F32, 2)
                sD = alloc("sD", [128, 512], F32)
                xt = Ring(alloc, "xt", [128, 512], F32, 2)
                fbb = alloc("fbb", [128, 2, 512], F32)
                e1 = Ring(alloc, "e1", [128, 512], F32, 2)
                e2 = Ring(alloc, "e2", [128, 512], F32, 2)
                e3 = Ring(alloc, "e3", [128, 512], F32, 2)
                e4 = Ring(alloc, "e4", [128, 512], F32, 2)
                yo = Ring(alloc, "yo", [128, 512], BF16, 2)
                yT = Ring(alloc, "yT", [128, 4, 128], BF16, 2)
                for cb in range(CB):
                    c0 = cb * 512
                    S.dma_k(vbuf[:], U_d[toff:toff + n, c0:c0 + 512].rearrange("(t p) c -> p t c", p=128), ["U_d"], ["vbuf"])
                    S.dma(fbb[:], hy_fb[j:j + 1, :, c0:c0 + 512].to_broadcast([128, 2, 512]), [], ["fbb"])
                    for o in range(2):
                        S.cp("act", zsb[:], vbuf[:], ["vbuf"], ["zsb"])
                        for kf in range(nt):
                            fr_t, fr_n = fre.next()
                            fi_t, fi_n = fim.next()
                            S.dma(fr_t[:], cst["Fre" + tag][kf, :, :, :], [], [fr_n])
                            S.dma(fi_t[:], cst["Fim" + tag][kf, :, :, :], [], [fi_n])
                            At, An = sA.next()
                            Bt, Bn = sB.next()
                            S.dma(At[:], spec[(tag, o, "re")][kf * 128:(kf + 1) * 128, c0:c0 + 512], ["spec"], [An])
                            S.dma(Bt[:], spec[(tag, o, "im")][kf * 128:(kf + 1) * 128, c0:c0 + 512], ["spec"], [Bn])
                            Dt, Dn = At, An
                            if kf == 0:
                                S.dma(sD[:], spec[(tag, o, "d0")][:, c0:c0 + 512], ["spec"], ["sD"])
                                Dt, Dn = sD, "sD"
                            br, bi = (0, 2) if kf % 2 == 0 else (1, 3)
                            for lt in range(nt):
                                S.mm(ps[:, br, :], fr_t[:, lt, :], zsb[:, lt, :], lt == 0, lt == nt - 1, [fr_n, "zsb"], [PB(br)])
                            for lt in range(nt):
                                S.mm(ps[:, bi, :], fi_t[:, lt, :], zsb[:, lt, :], lt == 0, lt == nt - 1, [fi_n, "zsb"], [PB(bi)])
                            a1, n1 = e1.next()
                            a2, n2 = e2.next()
                            a3, n3 = e3.next()
                            a4, n4 = e4.next()
                            S.tt("dve", a1[:], ps[:, br, :], At[:], ALU.mult, [PB(br), An], [n1])
                            S.tt("dve", a2[:], ps[:, bi, :], Bt[:], ALU.mult, [PB(bi), Bn], [n2])
                            S.tt("pool", Yre[:, kf, :], a1[:], a2[:], ALU.subtract, [n1, n2], ["Yre"])
                            S.tt("dve", a3[:], ps[:, br, :], Bt[:], ALU.mult, [PB(br), Bn], [n3])
                            S.tt("dve", a4[:], ps[:, bi, :], Dt[:], ALU.mult, [PB(bi), Dn], [n4])
                            S.tt("pool", Yim[:, kf, :], a3[:], a4[:], ALU.add, [n3, n4], ["Yim"])
                        for tt_i in range(nt):
                            gr_t, gr_n = fre.next()
                            gi_t, gi_n = fim.next()
                            S.dma(gr_t[:], cst["Gre" + tag][tt_i, :, :, :], [], [gr_n])
                            S.dma(gi_t[:], cst["Gim" + tag][tt_i, :, :, :], [], [gi_n])
                            x_t, x_n = xt.next()
                            S.dma(x_t[:], U_d[toff + tt_i * 128:toff + (tt_i + 1) * 128, (o + 1) * D + c0:(o + 1) * D + c0 + 512], ["U_d"], [x_n])
                            b = 4 + tt_i % 2
                            for kf in range(nt):
                                S.mm(ps[:, b, :], gr_t[:, kf, :], Yre[:, kf, :], kf == 0, False, [gr_n, "Yre"], [PB(b)])
                            for kf in range(nt):
                                S.mm(ps[:, b, :], gi_t[:, kf, :], Yim[:, kf, :], False, kf == nt - 1, [gi_n, "Yim"], [PB(b)])
                            a1, n1 = e1.next()
                            a2, n2 = e2.next()
                            S.tt("pool", a1[:], vbuf[:, tt_i, :], fbb[:, o, :], ALU.mult, ["vbuf", "fbb"], [n1])
                            S.tt("dve", a2[:], ps[:, b, :], a1[:], ALU.add, [PB(b), n1], [n2])
                            if o == 0:
                                S.tt("dve", vbuf[:, tt_i, :], a2[:], x_t[:], ALU.mult, [n2, x_n], ["vbuf"])
                            else:
                                y_t, y_n = yo.next()
                                yT_t, yT_n = yT.next()
                                S.tt("dve", y_t[:], a2[:], x_t[:], ALU.mult, [n2, x_n], [y_n])
                                for q4 in range(4):
                                    S.tr(psb[:, 6, q4 * 128:(q4 + 1) * 128], y_t[:, q4 * 128:(q4 + 1) * 128], identb[:], [y_n, "identb"], [PB(6)])
                                S.cp("act", yT_t[:], psb[:, 6, 0:512].rearrange("p (c t) -> p c t", t=128), [PB(6)], [yT_n])
                                S.dma(MT[c0:c0 + 512, toff + tt_i * 128:toff + (tt_i + 1) * 128].rearrange("(c p) t -> p c t", p=128), yT_t[:], [yT_n], ["MT"], eng="pool")
            S.barrier()

        def peer_phase(layer, with_ctx):
            rows = [(0, t) for t in range(NLT)] + ([(1, NLT + t) for t in range(NCT)] if with_ctx else [])
            st, alloc = phase()
            with st:
                xnT = alloc("aT", [128, KD, T], BF16)
                modulate_phase(layer, 2, [(r, ti, ti * 128) for (r, ti) in rows], xnT, True)
                S.barrier()
                st2, alloc = phase()
                with st2:
                    wf = Ring(alloc, "wf", [128, KD, 128], F32, 2)
                    wb = Ring(alloc, "wb", [128, KD, 128], BF16, 2)
                    qo = Ring(alloc, "qo", [128, T], BF16, 2)
                    ntok = len(rows) * 128
                    for c in range(16):
                        wft, wfn = wf.next()
                        wbt, wbn = wb.next()
                        S.dma_k(wft[:], p_wq[layer, :, c * 128:(c + 1) * 128].rearrange("(k p) c -> p k c", p=128), [], [wfn])
                        S.cp("act" if c % 2 else "dve", wbt[:], wft[:], [wfn], [wbn])
                        qt, qn = qo.next()
                        for t0 in range(0, ntok, 512):
                            w = min(512, ntok - t0)
                            b = (t0 // 512) % 4
                            for k in range(KD):
                                S.mm(ps[:, b, 0:w], wbt[:, k, :], xnT[:, k, t0:t0 + w], k == 0, k == KD - 1, [wbn, "aT"], [PB(b)])
                            S.cp("act" if (t0 // 512) % 2 == 0 else "dve", qt[:, t0:t0 + w], ps[:, b, 0:w], [PB(b)], [qn])
                        S.dma(QT[c, :, 0:ntok], qt[:, 0:ntok], [qn], ["QT"], eng="pool")
            S.barrier()
            st, alloc = phase()
            with st:
                keyf = alloc("keyf", [128, 16, 128], F32)
                keyb = alloc("keyb", [128, 16, 128], BF16)
                iota = alloc("iota", [128, 2048], F32)
                identb = alloc("identb", [128, 128], BF16)
                S.dma(keyf[:], p_keysT[layer, :, :, :], [], ["keyf"])
                S.cp("act", keyb[:], keyf[:], ["keyf"], ["keyb"])
                S.dma(iota[:], cst["iota16"][:, :], [], ["iota"])
                S.dma(identb[:], cst["ident_b"][:, :], [], ["identb"])
                Gate = {r: load_mod(alloc, "gate%d" % r, layer, r, 5) for r in sorted(set(r for r, _ in rows))}
                qtt = Ring(alloc, "qtt", [128, 16, 128], BF16, 2)
                xnr = Ring(alloc, "xn", [128, D], F32, 2)
                hr = Ring(alloc, "hP", [128, D], F32, 1)
                uvr = Ring(alloc, "uv", [128, 2 * D], BF16, 6)
                dgr = Ring(alloc, "dg", [128, 128], BF16, 4)
                tmpr = Ring(alloc, "tmpP", [128, 512], F32, 2)
                sc = alloc("sc", [128, 16, 128], F32)
                sc2 = alloc("sc2", [128, 16, 128], F32)
                m16 = alloc("m16", [128, 16, 16], F32)
                i16 = alloc("i16", [128, 16, 16], U32)
                i16f = alloc("i16f", [128, 16, 16], F32)
                cand = alloc("cand", [128, PH, 256], F32)
                cand2 = alloc("cand2", [128, PH, 256], F32)
                tops = alloc("tops", [128, PH, 16], F32)
                posu = alloc("posu", [128, PH, 16], U32)
                pa = alloc("pa", [128, PH, 16], U32)
                pbq = alloc("pbq", [128, PH, 16], U32)
                paf = alloc("paf", [128, PH, 16], F32)
                pbf = alloc("pbf", [128, PH, 16], F32)
                oh = alloc("oh", [128, PH, 16, 16], F32)
                oh2 = alloc("oh2", [128, PH, 16, 16], F32)
                i1s = alloc("i1s", [128, PH, 16], F32)
                i2s = alloc("i2s", [128, PH, 16], F32)
                idf = alloc("idf", [128, 128], F32)
                idir = Ring(alloc, "idi", [128, 128], I32, 2)
                gsmr = Ring(alloc, "gsm", [128, PH, 16], F32, 2)
                gst = alloc("gst", [128, PH, 2], F32)
                aprer = Ring(alloc, "apre", [128, 128], F32, 2)
                glr = Ring(alloc, "gl", [128, 128], F32, 2)
                base = 0
                NBK = D // 512

                def routing(ti, idi, idn, gsm, gsn):
                    q_t, q_n = qtt.next()
                    S.dma_k(q_t[:], QT[:, :, ti * 128:(ti + 1) * 128].rearrange("c p t -> p c t"), ["QT"], [q_n])
                    yield
                    for c in range(16):
                        b = c // 4
                        S.mm(ps[:, b, (c % 4) * 128:(c % 4 + 1) * 128], q_t[:, c, :], keyb[:, c, :], True, True, [q_n, "keyb"], [PB(b)])
                        yield
                    for b in range(4):
                        S.cp("act", sc[:, b * 4:(b + 1) * 4, :], ps[:, b, :].rearrange("p (c k) -> p c k", k=128), [PB(b)], ["sc"])
                        yield
                    for c in range(16):
                        S.op("dve", lambda e, c=c: e.max(m16[:, c, 0:8], sc[:, c, :]), ["sc"], ["m16"])
                        yield
                        S.op("dve", lambda e, c=c: e.max_index(i16[:, c, 0:8], m16[:, c, 0:8], sc[:, c, :]), ["sc", "m16"], ["i16"])
                        yield
                        S.op("dve", lambda e, c=c: e.match_replace(sc2[:, c, :], m16[:, c, 0:8], sc[:, c, :], NEG), ["sc", "m16"], ["sc2"])
                        yield
                        S.op("dve", lambda e, c=c: e.max(m16[:, c, 8:16], sc2[:, c, :]), ["sc2"], ["m16"])
                        yield
                        S.op("dve", lambda e, c=c: e.max_index(i16[:, c, 8:16], m16[:, c, 8:16], sc2[:, c, :]), ["sc2", "m16"], ["i16"])
                        yield
                    S.cp("dve", i16f[:], i16[:], ["i16"], ["i16f"])
                    yield
                    m4 = m16[:].rearrange("p (h j) a -> p h j a", j=2)
                    i4 = i16f[:].rearrange("p (h j) a -> p h j a", j=2)
                    for h in range(PH):
                        S.tt("dve", cand[:, h, :].rearrange("p (a b) -> p a b", b=16),
                             m4[:, h, 0, :].unsqueeze(2).to_broadcast([128, 16, 16]),
                             m4[:, h, 1, :].unsqueeze(1).to_broadcast([128, 16, 16]), ALU.add, ["m16"], ["cand"])
                        yield
                    for h in range(PH):
                        S.op("dve", lambda e, h=h: e.max(tops[:, h, 0:8], cand[:, h, :]), ["cand"], ["tops"])
                        yield
                        S.op("dve", lambda e, h=h: e.max_index(posu[:, h, 0:8], tops[:, h, 0:8], cand[:, h, :]), ["cand", "tops"], ["posu"])
                        yield
                        S.op("dve", lambda e, h=h: e.match_replace(cand2[:, h, :], tops[:, h, 0:8], cand[:, h, :], NEG), ["cand", "tops"], ["cand2"])
                        yield
                        S.op("dve", lambda e, h=h: e.max(tops[:, h, 8:16], cand2[:, h, :]), ["cand2"], ["tops"])
                        yield
                        S.op("dve", lambda e, h=h: e.max_index(posu[:, h, 8:16], tops[:, h, 8:16], cand2[:, h, :]), ["cand2", "tops"], ["posu"])
                        yield
                    S.op("dve", lambda e: e.tensor_single_scalar(pa[:], posu[:], 4, ALU.logical_shift_right), ["posu"], ["pa"])
                    yield
                    S.op("dve", lambda e: e.tensor_single_scalar(pbq[:], posu[:], 15, ALU.bitwise_and), ["posu"], ["pbq"])
                    yield
                    S.cp("dve", paf[:], pa[:], ["pa"], ["paf"])
                    yield
                    S.cp("dve", pbf[:], pbq[:], ["pbq"], ["pbf"])
                    yield
                    iota4 = iota[:].rearrange("p (h k a) -> p h k a", h=PH, k=16)
                    for (pf, pfn, jj, dst, dn) in ((paf, "paf", 0, i1s, "i1s"), (pbf, "pbf", 1, i2s, "i2s")):
                        S.tt("dve", oh[:], pf[:].unsqueeze(3).to_broadcast([128, PH, 16, 16]), iota4, ALU.is_equal, [pfn, "iota"], ["oh"])
                        yield
                        S.tt("dve", oh2[:], oh[:], i4[:, :, jj, :].unsqueeze(2).to_broadcast([128, PH, 16, 16]), ALU.mult, ["oh", "i16f"], ["oh2"])
                        yield
                        S.op("dve", lambda e, dst=dst: e.tensor_reduce(dst[:], oh2[:], AX.X, ALU.add), ["oh2"], [dn])
                        yield
                    S.stt(idf[:], i1s[:].rearrange("p h k -> p (h k)"), 128.0, i2s[:].rearrange("p h k -> p (h k)"), ALU.mult, ALU.add, ["i1s", "i2s"], ["idf"])
                    yield
                    S.ts("dve", idf[:], idf[:], float(base), None, ALU.add, None, ["idf"], ["idf"])
                    yield
                    S.cp("dve", idi[:], idf[:], ["idf"], [idn])
                    yield
                    S.tt("dve", gsm[:], tops[:], tops[:, :, 0:1].to_broadcast([128, PH, 16]), ALU.subtract, ["tops"], [gsn])
                    yield
                    S.act(gsm[:], gsm[:], AF.Exp, [gsn], [gsn])
                    yield
                    S.op("dve", lambda e: e.tensor_reduce(gst[:, :, 0], gsm[:], AX.X, ALU.add), [gsn], ["gst"])
                    yield
                    S.op("dve", lambda e: e.reciprocal(gst[:, :, 1], gst[:, :, 0]), ["gst"], ["gst"])
                    yield
                    S.tt("dve", gsm[:], gsm[:], gst[:, :, 1:2].to_broadcast([128, PH, 16]), ALU.mult, [gsn, "gst"], [gsn])
                    yield

                def start_routing(ti):
                    idi, idn = idir.next()
                    gsm, gsn = gsmr.next()
                    return routing(ti, idi, idn, gsm, gsn), (idi, idn, gsm, gsn)

                def drain(gen, n=None):
                    k = 0
                    for _ in gen:
                        k += 1
                        if n is not None and k >= n:
                            return

                gen, cur = start_routing(rows[0][1])
                drain(gen)
                for ri, (r, ti) in enumerate(rows):
                    idi, idn, gsm, gsn = cur
                    nxt_gen, nxt = (None, None)
                    if ri + 1 < len(rows):
                        nxt_gen, nxt = start_routing(rows[ri + 1][1])
                    x_t, x_n = xnr.next()
                    S.dma(x_t[:], XN[ti * 128:(ti + 1) * 128, :], ["XN"], [x_n])
                    apre, apn = aprer.next()
                    gl, gln = glr.next()
                    gflat = gsm[:].rearrange("p h k -> p (h k)")
                    for s_ in range(128):
                        uv, uvn = uvr.next()
                        dg, dgn = dgr.next()
                        S.op("pool", lambda e, uv=uv, s_=s_, idi=idi: e.indirect_dma_start(
                            out=uv[:], out_offset=None, in_=UVs[layer][:, :],
                            in_offset=bass.IndirectOffsetOnAxis(ap=idi[:, s_:s_ + 1], axis=0)), [idn], [uvn], dma=True)
                        an = "%s_%d" % (apn, s_ % 8)
                        S.stt(uv[:, 0:D], uv[:, 0:D], 1.0, x_t[:], ALU.mult, ALU.mult, [uvn, x_n], [uvn + "u", an], accum_out=apre[:, s_:s_ + 1])
                        S.act(gl[:, s_:s_ + 1], apre[:, s_:s_ + 1], AF.Gelu, [an], [an + "g"])
                        S.act(gl[:, s_:s_ + 1], gl[:, s_:s_ + 1], AF.Copy, [an + "g", gsn], [an + "g"], scale=gflat[:, s_:s_ + 1])
                        S.act(dg[:], identb[:], AF.Copy, ["identb", an + "g"], [dgn], scale=gl[:, s_:s_ + 1])
                        for nb in range(NBK):
                            S.mm(ps[:, 4 + nb, :], dg[:], uv[:, D + nb * 512:D + (nb + 1) * 512], s_ == 0, s_ == 127, [dgn, uvn], [PB(4 + nb)])
                        if nxt_gen is not None:
                            drain(nxt_gen, 2)
                    if nxt_gen is not None:
                        drain(nxt_gen)
                    h_t, h_n = hr.next()
                    S.dma(h_t[:], hbuf[ti * 128:(ti + 1) * 128, :], ["hbuf"], [h_n])
                    for nb in range(NBK):
                        tt_, tn = tmpr.next()
                        S.tt("dve", tt_[:], ps[:, 4 + nb, :], Gate[r][:, nb * 512:(nb + 1) * 512], ALU.mult, [PB(4 + nb), "gate%d" % r], [tn])
                        S.tt("dve", h_t[:, nb * 512:(nb + 1) * 512], h_t[:, nb * 512:(nb + 1) * 512], tt_[:], ALU.add, [h_n, tn], [h_n])
                    S.dma(hbuf[ti * 128:(ti + 1) * 128, :], h_t[:], [h_n], ["hbuf"], eng="act")
                    cur = nxt
            S.barrier()

        for i in range(DEPTH):
            is_attn = (i % 2) == 0
            j = i // 2
            ctx_update = i < last_attn
            lat_rows = [(0, t) for t in range(NLT)]
            ctx_rows = [(1, NLT + t) for t in range(NCT)]
            if is_attn:
                attention_phase(i, j, ctx_update)
                outproj_phase(i, w_o[j], lat_rows + (ctx_rows if ctx_update else []))
            else:
                hyena_spectra(j, "L", L, conv_range=(NE // 128, DEPTH * NE // 128) if i == 1 else None)
                hyena_inproj(i, j, ctx_update)
                hyena_conv(j, "L", L, 0)
                if ctx_update:
                    hyena_spectra(j, "C", C)
                    hyena_conv(j, "C", C, L)
                outproj_phase(i, hy_wout[j], lat_rows + (ctx_rows if ctx_update else []))
            if stop_after == "mix%d" % i:
                break
            peer_phase(i, ctx_update)
            if stop_after == "ffn%d" % i:
                break

        st, alloc = phase()
        with st:
            if stop_after is None:
                gt = alloc("fg", [128, D], F32)
                S.dma(gt[:], fing[0:1, :].to_broadcast([128, D]), [], ["fg"])
            hr = Ring(alloc, "hF", [128, D], F32, 2)
            orr = Ring(alloc, "oF", [128, D], F32, 2)
            junk = alloc("junkF", [128, D], BF16)
            sr = Ring(alloc, "ssF", [128, 2], F32, 2)
            for t in range(NLT):
                ht, hn = hr.next()
                S.dma(ht[:], hbuf[t * 128:(t + 1) * 128, :], ["hbuf"], [hn])
                if stop_after is None:
                    ot, on = orr.next()
                    ss, sn = sr.next()
                    S.act(junk[:], ht[:], AF.Square, [hn], ["junkF", sn], accum_out=ss[:, 0:1])
                    S.ts("dve", ss[:, 1:2], ss[:, 0:1], 1.0 / D, RMS_EPS, ALU.mult, ALU.add, [sn], [sn])
                    S.act(ss[:, 1:2], ss[:, 1:2], AF.Sqrt, [sn], [sn])
                    S.op("dve", lambda e, ss=ss: e.reciprocal(ss[:, 1:2], ss[:, 1:2]), [sn], [sn])
                    S.stt(ot[:], ht[:], ss[:, 1:2], gt[:], ALU.mult, ALU.mult, [hn, sn, "fg"], [on])
                    S.dma(out_d[t * 128:(t + 1) * 128, :], ot[:], [on], ["out"], eng="pool")
                else:
                    S.dma(out_d[t * 128:(t + 1) * 128, :], ht[:], [hn], ["out"], eng="pool")
        S.barrier()
        S.emit()
        nops = S.nops
    return nc, nops


def make_in_maps(inputs, cfg, cores):
    D, L, C = cfg["D"], cfg["L"], cfg["C"]
    KD = D // 128
    f = lambda a: np.ascontiguousarray(np.asarray(a, dtype=np.float32))
    shared = {}
    for k in ("ada_w", "ada_b", "norm_mix_g", "norm_ffn_g", "attn_w_qkv", "attn_w_o", "attn_sink", "hy_w_in",
              "hy_f_w1", "hy_f_w2", "hy_f_w3", "hy_decay", "hy_fbias", "hy_w_out", "peer_w_q"):
        shared[k] = f(inputs[k])
    shared["final_g"] = f(inputs["final_g"]).reshape(1, D)
    cw = f(inputs["hy_conv_w"])
    shared["hy_conv_wT"] = np.ascontiguousarray(cw.reshape(cw.shape[0], 3, 3 * D // 128, 128).transpose(0, 3, 2, 1))
    cbv = f(inputs["hy_conv_b"])
    shared["hy_conv_bT"] = np.ascontiguousarray(cbv.reshape(cbv.shape[0], 3 * D // 128, 128).transpose(0, 2, 1))
    shared["hy_f_b1"] = f(inputs["hy_f_b1"])[..., None]
    shared["hy_f_b2"] = f(inputs["hy_f_b2"])[..., None]
    k1 = f(inputs["peer_keys1"])
    k2 = f(inputs["peer_keys2"])
    ks = np.stack([k1, k2], axis=2)
    shared["peer_keysT"] = np.ascontiguousarray(ks.transpose(0, 4, 1, 2, 3).reshape(ks.shape[0], 128, 16, 128))
    shared["peer_u"] = f(inputs["peer_u"]).reshape(-1, D)
    shared["peer_v"] = f(inputs["peer_v"]).reshape(-1, D)
    shared.update(host_consts(cfg))
    c_ctx = f(inputs["c_ctx"])
    maps = []
    for b in cores:
        m = dict(shared)
        m["x"] = f(inputs["x"][b])
        m["ctx"] = f(inputs["ctx"][b])
        cb = f(inputs["c"][b])
        cT = np.stack([cb.reshape(KD, 128).T, c_ctx.reshape(KD, 128).T], axis=-1)
        m["cT"] = np.ascontiguousarray(cT)
        maps.append(m)
    return maps


_NC_CACHE = {}


def kernel(**inputs):
    cfg = CFG
    key = (cfg["D"], cfg["L"], cfg["C"])
    if key not in _NC_CACHE:
        _NC_CACHE[key] = build_program(cfg)[0]
    nc = _NC_CACHE[key]
    B = inputs["x"].shape[0]
    maps = make_in_maps(inputs, cfg, list(range(B)))
    res = run_bass_kernel_spmd(nc, maps, core_ids=list(range(B)))
    return np.stack([np.asarray(r["out"], dtype=np.float32) for r in res.results], axis=0)
```

```python
import math
from contextlib import ExitStack
import numpy as np
import ml_dtypes
import concourse.bass as bass
import concourse.mybir as mybir
from concourse.bass_utils import run_bass_kernel_spmd

F32 = mybir.dt.float32
BF16 = mybir.dt.bfloat16
I32 = mybir.dt.int32
U32 = mybir.dt.uint32
ALU = mybir.AluOpType
AF = mybir.ActivationFunctionType
AX = mybir.AxisListType

ENGS = ("pe", "act", "dve", "pool", "sp")
N_DMA_SEMS = 56
EPOCH_N = 12000

CFG = dict(D=2048, L=2048, C=256, DEPTH=4, NCORES=8)
NEG = -1e30
RMS_EPS = 1e-6
NKV = 4
PH, PK, TOPK, DQ = 8, 128, 16, 2048
NE = PK * PK
HYH = 64
HYE = 33


class Sched:
    def __init__(self, nc, stack):
        self.nc = nc
        self.stack = stack
        self.epoch = {e: 0 for e in ENGS}
        self.ops = {e: [] for e in ENGS}
        self.cnt = {e: 0 for e in ENGS}
        self.seen = {e: {} for e in ENGS}
        self.res = {}
        self.sems = {}
        for e in ENGS:
            self.sems[(e, 0)] = stack.enter_context(nc.semaphore("s_" + e))
        self.dma_cnt = [0] * N_DMA_SEMS
        self.dma_rr = 0
        for j in range(N_DMA_SEMS):
            self.sems[("d", j)] = stack.enter_context(nc.semaphore("s_d%d" % j))
        self.nops = 0

    def _need(self, eng, tok, waits):
        if tok is None:
            return
        k, v = tok
        if k[0] == eng and eng == "pe":
            return
        if self.seen[eng].get(k, 0) >= v:
            return
        self.seen[eng][k] = v
        waits.append((k, v))

    def op(self, eng, fn, reads=(), writes=(), dma=False):
        waits = []
        for r in reads:
            st = self.res.get(r)
            if st is not None:
                self._need(eng, st[0], waits)
        for w in writes:
            st = self.res.get(w)
            if st is not None:
                self._need(eng, st[0], waits)
                for t in st[1].items():
                    self._need(eng, t, waits)
        if dma:
            j = self.dma_rr
            self.dma_rr = (self.dma_rr + 1) % N_DMA_SEMS
            if self.dma_cnt[j] > 0:
                self._need(eng, (("d", j), 16 * self.dma_cnt[j]), waits)
            self.dma_cnt[j] += 1
            tok = (("d", j), 16 * self.dma_cnt[j])
            inc = 16
        else:
            self.cnt[eng] += 1
            tok = ((eng, self.epoch[eng]), self.cnt[eng])
            inc = 1
        k, v = tok
        for r in reads:
            st = self.res.setdefault(r, [None, {}])
            if st[1].get(k, 0) < v:
                st[1][k] = v
        for w in writes:
            self.res[w] = [tok, {}]
        self.ops[eng].append((waits, fn, tok[0], inc))
        self.nops += 1
        return tok

    def barrier(self):
        toks = [((e, self.epoch[e]), self.cnt[e]) for e in ENGS if self.cnt[e] > 0]
        toks += [(("d", j), 16 * self.dma_cnt[j]) for j in range(N_DMA_SEMS) if self.dma_cnt[j] > 0]
        for e in ENGS:
            waits = []
            for t in toks:
                self._need(e, t, waits)
            self.ops[e].append((waits, None, None, 0))
        self.res = {}
        for e in ENGS:
            if self.cnt[e] > EPOCH_N:
                self.epoch[e] += 1
                self.cnt[e] = 0
                self.sems[(e, self.epoch[e])] = self.stack.enter_context(
                    self.nc.semaphore("s_%s_%d" % (e, self.epoch[e])))

    def emit(self):
        nc = self.nc
        sems = self.sems
        with nc.Block() as block:
            def mk(engname):
                lst = self.ops[engname]

                def body(e):
                    for waits, fn, semkey, inc in lst:
                        for k, v in waits:
                            e.wait_ge(sems[k], v)
                        if fn is not None:
                            fn(e).then_inc(sems[semkey], inc)
                return body
            block.tensor(mk("pe"))
            block.scalar(mk("act"))
            block.vector(mk("dve"))
            block.gpsimd(mk("pool"))
            block.sync(mk("sp"))

    def dma(self, out, in_, r, w, eng="sp"):
        self.op(eng, lambda e: e.dma_start(out=out, in_=in_), r, w, dma=True)

    def dma_k(self, out, in_, r, w, eng="sp", g=4):
        K_ = out.shape[1]
        for k0 in range(0, K_, g):
            k1 = min(K_, k0 + g)
            self.dma(out[:, k0:k1, :], in_[:, k0:k1, :], r, w, eng)

    def mm(self, out, lhsT, rhs, start, stop, r, w):
        self.op("pe", lambda e: e.matmul(out, lhsT, rhs, start=start, stop=stop), r, w)

    def tr(self, out, in_, ident, r, w):
        self.op("pe", lambda e: e.transpose(out, in_, ident), r, w)

    def act(self, out, in_, func, r, w, bias=None, scale=None, accum_out=None):
        kw = {}
        if bias is not None:
            kw["bias"] = bias
        if scale is not None:
            kw["scale"] = scale
        if accum_out is not None:
            kw["accum_out"] = accum_out
        self.op("act", lambda e: e.activation(out, in_, func, **kw), r, w)

    def tt(self, eng, out, in0, in1, op, r, w):
        self.op(eng, lambda e: e.tensor_tensor(out, in0, in1, op), r, w)

    def ts(self, eng, out, in0, s1, s2, op0, op1, r, w):
        if op1 is None:
            self.op(eng, lambda e: e.tensor_scalar(out, in0, s1, None, op0), r, w)
        else:
            self.op(eng, lambda e: e.tensor_scalar(out, in0, s1, s2, op0, op1), r, w)

    def stt(self, out, in0, scalar, in1, op0, op1, r, w, accum_out=None):
        if accum_out is None:
            self.op("dve", lambda e: e.scalar_tensor_tensor(out, in0, scalar, in1, op0, op1), r, w)
        else:
            self.op("dve", lambda e: e.scalar_tensor_tensor(out, in0, scalar, in1, op0, op1, accum_out=accum_out), r, w)

    def cp(self, eng, out, in_, r, w):
        if eng == "act":
            self.op("act", lambda e: e.copy(out, in_), r, w)
        else:
            self.op(eng, lambda e: e.tensor_copy(out, in_), r, w)

    def memset(self, eng, ap, val, w):
        self.op(eng, lambda e: e.memset(ap, val), (), w)


class Ring:
    def __init__(self, alloc, name, shape, dt, n):
        self.t = [alloc(name + str(i), shape, dt) for i in range(n)]
        self.names = [name + str(i) for i in range(n)]
        self.i = -1

    def next(self):
        self.i = (self.i + 1) % len(self.t)
        return self.t[self.i], self.names[self.i]


def host_consts(cfg):
    D, L, C = cfg["D"], cfg["L"], cfg["C"]
    k = {}
    k["ident_f"] = np.eye(128, dtype=np.float32)
    k["ident_b"] = np.eye(128).astype(ml_dtypes.bfloat16)
    P = np.zeros((128, 128), np.float32)
    for j in range(32):
        P[32 + j, j] = -1.0
        P[j, 32 + j] = 1.0
        P[96 + j, 64 + j] = -1.0
        P[64 + j, 96 + j] = 1.0
    k["prot"] = P
    GW = 64
    rows = L // GW
    row = np.repeat(np.arange(rows), GW).astype(np.float32)
    col = np.tile(np.arange(GW), rows).astype(np.float32)
    inv = (10000.0 ** (-np.arange(32, dtype=np.float32) / 32)).astype(np.float32)
    ar, ac = row[:, None] * inv, col[:, None] * inv
    ang = np.concatenate([ar, ar, ac, ac], axis=-1)
    k["cosT"] = np.ascontiguousarray(np.cos(ang).T.astype(np.float32))
    k["sinT"] = np.ascontiguousarray(np.sin(ang).T.astype(np.float32))
    q = np.arange(128)[:, None]
    kk = np.arange(384)[None, :]
    masks = np.zeros((128, 3, 384), np.float32)
    for v, off in enumerate((0, 128, 256)):
        masks[:, v, :] = np.where(np.abs(kk - (q + off)) <= 128, 0.0, NEG)
    k["masks"] = masks
    io = np.tile(np.arange(16, dtype=np.float32), 128)[None, :].repeat(128, 0)
    k["iota16"] = np.ascontiguousarray(io)
    for n, tag in ((L, "L"), (C, "C")):
        t = np.arange(n, dtype=np.float64)
        f = np.arange(n, dtype=np.float64)
        th = 2.0 * np.pi * np.outer(t, f) / (2 * n)
        Fre = np.cos(th)
        Fim = -np.sin(th)
        Fim[:, 0] = np.cos(np.pi * t)
        Fimb = np.sin(th)
        Fimb[:, 0] = np.cos(np.pi * t)
        Gre = (2.0 / (2 * n)) * np.cos(th.T)
        Gre[0, :] = 1.0 / (2 * n)
        Gim = -(2.0 / (2 * n)) * np.sin(th.T)
        Gim[0, :] = np.cos(np.pi * t) / (2 * n)
        nt = n // 128

        def fwd_layout(M):
            return np.ascontiguousarray(M.reshape(nt, 128, nt, 128).transpose(2, 1, 0, 3)).astype(ml_dtypes.bfloat16)

        def inv_layout(M):
            return np.ascontiguousarray(M.reshape(nt, 128, nt, 128).transpose(2, 1, 0, 3)).astype(ml_dtypes.bfloat16)
        k["Fre" + tag] = fwd_layout(Fre)
        k["Fim" + tag] = fwd_layout(Fim)
        k["Gre" + tag] = inv_layout(Gre)
        k["Gim" + tag] = inv_layout(Gim)
        tt = np.arange(n, dtype=np.float32)
        t01 = tt / n
        bands = np.arange(1, 17, dtype=np.float32)
        a = 2.0 * math.pi * tt[:, None] * bands[None, :] / n
        z = np.concatenate([t01[:, None], np.cos(a), np.sin(a)], axis=-1).astype(np.float32)
        k["zT" + tag] = np.ascontiguousarray(z.T)
        k["nt01" + tag] = np.ascontiguousarray((-t01).reshape(nt, 128).T.astype(np.float32))
    return k


def build_program(cfg, stop_after=None):
    D, L, C, DEPTH = cfg["D"], cfg["L"], cfg["C"], cfg["DEPTH"]
    KD = D // 128
    T = L + C
    NLT, NCT, NT = L // 128, C // 128, (L + C) // 128
    NQ = D // 128
    GRP = NQ // NKV
    QKVW = (NQ + 2 * NKV) * 128
    NA, NH = (DEPTH + 1) // 2, DEPTH // 2
    CB = D // 512
    last_attn = max(i for i in range(DEPTH) if i % 2 == 0)
    TP = T + 4
    LOFF, COFF = 1, L + 3

    nc = bass.Bass("TRN2", target_bir_lowering=False)
    inp = {}

    def ein(name, shape, dt=F32):
        inp[name] = nc.dram_tensor(name, list(shape), dt, kind="ExternalInput").ap()
        return inp[name]

    x_in = ein("x", [L, D])
    ctx_in = ein("ctx", [C, D])
    cT_in = ein("cT", [128, KD, 2])
    ada_w = ein("ada_w", [DEPTH, D, 6 * D])
    ada_b = ein("ada_b", [DEPTH, 6 * D])
    nmg = ein("norm_mix_g", [DEPTH, D])
    nfg = ein("norm_ffn_g", [DEPTH, D])
    fing = ein("final_g", [1, D])
    w_qkv = ein("attn_w_qkv", [NA, D, QKVW])
    w_o = ein("attn_w_o", [NA, D, D])
    sink_in = ein("attn_sink", [NA, NQ])
    hy_w_in = ein("hy_w_in", [NH, D, 3 * D])
    hy_cwT = ein("hy_conv_wT", [NH, 128, 3 * D // 128, 3])
    hy_cbT = ein("hy_conv_bT", [NH, 128, 3 * D // 128])
    hy_w1 = ein("hy_f_w1", [NH, HYE, HYH])
    hy_b1 = ein("hy_f_b1", [NH, HYH, 1])
    hy_w2 = ein("hy_f_w2", [NH, 2, HYH, HYH])
    hy_b2 = ein("hy_f_b2", [NH, 2, HYH, 1])
    hy_w3 = ein("hy_f_w3", [NH, HYH, 4 * D])
    hy_dec = ein("hy_decay", [NH, 2, D])
    hy_fb = ein("hy_fbias", [NH, 2, D])
    hy_wout = ein("hy_w_out", [NH, D, D])
    p_wq = ein("peer_w_q", [DEPTH, D, DQ])
    p_keysT = ein("peer_keysT", [DEPTH, 128, 16, 128])
    p_u = ein("peer_u", [DEPTH * NE, D])
    p_v = ein("peer_v", [DEPTH * NE, D])
    cst = {}
    cst["ident_f"] = ein("ident_f", [128, 128])
    cst["ident_b"] = ein("ident_b", [128, 128], BF16)
    cst["prot"] = ein("prot", [128, 128])
    cst["cosT"] = ein("cosT", [128, L])
    cst["sinT"] = ein("sinT", [128, L])
    cst["masks"] = ein("masks", [128, 3, 384])
    cst["iota16"] = ein("iota16", [128, 2048])
    for n, tag in ((L, "L"), (C, "C")):
        nt = n // 128
        for nm in ("Fre", "Fim", "Gre", "Gim"):
            cst[nm + tag] = ein(nm + tag, [nt, 128, nt, 128], BF16)
        cst["zT" + tag] = ein("zT" + tag, [HYE, n])
        cst["nt01" + tag] = ein("nt01" + tag, [128, nt])

    out_d = nc.dram_tensor("out", [L, D], F32, kind="ExternalOutput").ap()

    def scratch(name, shape, dt=F32):
        return nc.dram_tensor(name, list(shape), dt, kind="Internal").ap()

    hbuf = scratch("hbuf", [T, D])
    moddr = scratch("moddr", [DEPTH, 2, 6 * D])
    MT = scratch("MT", [D, T], BF16)
    XN = scratch("XN", [T, D])
    QT = scratch("QT", [16, 128, T], BF16)
    U_d = scratch("U_d", [T, 3 * D])
    UVs = [scratch("UV%d" % l_, [NE, 2 * D], BF16) for l_ in range(DEPTH)]
    spec = {}
    for tag, n in (("L", L), ("C", C)):
        for o in range(2):
            spec[(tag, o, "re")] = scratch("specre%s%d" % (tag, o), [n, D])
            spec[(tag, o, "im")] = scratch("specim%s%d" % (tag, o), [n, D])
            spec[(tag, o, "d0")] = scratch("specd0%s%d" % (tag, o), [128, D])

    with ExitStack() as top:
        S = Sched(nc, top)
        ps = top.enter_context(nc.psum_tensor("ps", [128, 8, 512], F32))
        psb = ps[:, :, :].bitcast(BF16)

        uid = [0]

        def PB(b):
            return "psb%d" % b

        def phase():
            st = ExitStack()

            def alloc(name, shape, dt):
                uid[0] += 1
                return st.enter_context(nc.sbuf_tensor("%s_%d" % (name, uid[0]), list(shape), dt))
            return st, alloc

        def convert_gen(alloc, ch0, ch1, nring=2, dma_eng="sp"):
            fu = Ring(alloc, "cvu", [128, D], F32, nring)
            fv = Ring(alloc, "cvv", [128, D], F32, nring)
            ob = Ring(alloc, "cvo", [128, 2 * D], BF16, nring)
            prev = None

            def store(p):
                S.dma(UVs[p[0] // NE][p[0] % NE:p[0] % NE + 128, :], p[1][:], [p[2]], ["UV"], eng=dma_eng)
            for ch in range(ch0, ch1):
                r0 = ch * 128
                ut, un = fu.next()
                vt, vn = fv.next()
                ot, on = ob.next()
                S.dma(ut[:], p_u[r0:r0 + 128, :], [], [un], eng=dma_eng)
                S.dma(vt[:], p_v[r0:r0 + 128, :], [], [vn], eng=dma_eng)
                if prev is not None:
                    store(prev)
                e1, e2 = ("act", "dve") if ch % 2 == 0 else ("dve", "act")
                S.cp(e1, ot[:, 0:D], ut[:], [un], [on])
                S.cp(e2, ot[:, D:2 * D], vt[:], [vn], [on])
                prev = (r0, ot, on)
                yield
            if prev is not None:
                store(prev)
            yield

        for t in range(NLT):
            S.dma(hbuf[t * 128:(t + 1) * 128, :], x_in[t * 128:(t + 1) * 128, :], [], ["hbuf"])
        for t in range(NCT):
            S.dma(hbuf[L + t * 128:L + (t + 1) * 128, :], ctx_in[t * 128:(t + 1) * 128, :], [], ["hbuf"])
        st, alloc = phase()
        with st:
            cT = alloc("cT", [128, KD, 2], F32)
            cS = alloc("cS", [128, KD, 2], F32)
            S.dma(cT[:], cT_in[:, :, :], [], ["cT"])
            S.act(cS[:], cT[:], AF.Silu, ["cT"], ["cS"])
            NB6 = 6 * D // 512
            GB = 8
            wr = Ring(alloc, "adaw", [128, GB * 512], F32, 2)
            modg = Ring(alloc, "modg", [2, GB * 512], F32, 2)
            biasg = Ring(alloc, "biasg", [2, GB * 512], F32, 2)
            cg = convert_gen(alloc, 0, NE // 128)
            for i in range(DEPTH):
                for g0 in range(0, NB6, GB):
                    nb = min(GB, NB6 - g0)
                    bt, bn = biasg.next()
                    mt_, mn_ = modg.next()
                    S.dma(bt[:, 0:nb * 512], ada_b[i:i + 1, g0 * 512:(g0 + nb) * 512].to_broadcast([2, nb * 512]), [], [bn])
                    for k in range(KD):
                        wt, wn = wr.next()
                        S.dma(wt[:, 0:nb * 512], ada_w[i, k * 128:(k + 1) * 128, g0 * 512:(g0 + nb) * 512], [], [wn])
                        for b in range(nb):
                            S.mm(ps[0:2, b, :], cS[:, k, :], wt[:, b * 512:(b + 1) * 512], k == 0, k == KD - 1, [wn, "cS"], [PB(b)])
                        next(cg, None)
                    for b in range(nb):
                        S.tt("dve", mt_[:, b * 512:(b + 1) * 512], ps[0:2, b, :], bt[:, b * 512:(b + 1) * 512], ALU.add, [PB(b), bn], [mn_])
                    S.dma(moddr[i, :, g0 * 512:(g0 + nb) * 512], mt_[:, 0:nb * 512], [mn_], ["moddr"])
            for _ in cg:
                pass
        S.barrier()

        def load_mod(alloc, name, layer, row, idx, gvec=None):
            t = alloc(name, [128, D], F32)
            S.dma(t[:], moddr[layer, row:row + 1, idx * D:(idx + 1) * D].to_broadcast([128, D]), ["moddr"], [name])
            if gvec is not None:
                gt = alloc(name + "_g", [128, D], F32)
                S.dma(gt[:], gvec.to_broadcast([128, D]), [], [name + "_g"])
                S.stt(t[:], t[:], 1.0, gt[:], ALU.add, ALU.mult, [name, name + "_g"], [name])
            return t

        def modulate_phase(layer, which, rows, aT, xn_store):
            st, alloc = phase()
            with st:
                ident = alloc("ident", [128, 128], F32)
                S.dma(ident[:], cst["ident_f"][:, :], [], ["ident"])
                gvec = (nmg if which == 1 else nfg)[layer:layer + 1, :]
                sc_i, sh_i = (1, 0) if which == 1 else (4, 3)
                need_rows = sorted(set(r for r, _, _ in rows))
                Gt, St = {}, {}
                for r in need_rows:
                    Gt[r] = load_mod(alloc, "G%d" % r, layer, r, sc_i, gvec)
                    St[r] = load_mod(alloc, "S%d" % r, layer, r, sh_i)
                hr = Ring(alloc, "hT", [128, D], F32, 2)
                ar = Ring(alloc, "aT_", [128, D], F32, 2)
                junk = alloc("junk", [128, D], BF16)
                sr = Ring(alloc, "ss", [128, 2], F32, 2)
                pb = 0
                for (r, ti, coff) in rows:
                    ht, hn = hr.next()
                    at, an = ar.next()
                    ss, sn = sr.next()
                    S.dma(ht[:], hbuf[ti * 128:(ti + 1) * 128, :], ["hbuf"], [hn])
                    S.act(junk[:], ht[:], AF.Square, [hn], ["junk", sn], accum_out=ss[:, 0:1])
                    S.ts("dve", ss[:, 1:2], ss[:, 0:1], 1.0 / D, RMS_EPS, ALU.mult, ALU.add, [sn], [sn])
                    S.act(ss[:, 1:2], ss[:, 1:2], AF.Sqrt, [sn], [sn])
                    S.op("dve", lambda e, ss=ss: e.reciprocal(ss[:, 1:2], ss[:, 1:2]), [sn], [sn])
                    S.stt(at[:], ht[:], ss[:, 1:2], Gt[r][:], ALU.mult, ALU.mult, [hn, sn, "G%d" % r], [an])
                    S.tt("dve", at[:], at[:], St[r][:], ALU.add, [an, "S%d" % r], [an])
                    if xn_store:
                        S.dma(XN[ti * 128:(ti + 1) * 128, :], at[:], [an], ["XN"], eng="pool")
                    for k0 in range(0, KD, 4):
                        b = pb % 2
                        pb += 1
                        for kk in range(4):
                            k = k0 + kk
                            S.tr(ps[:, b, kk * 128:(kk + 1) * 128], at[:, k * 128:(k + 1) * 128], ident[:], [an, "ident"], [PB(b)])
                        S.cp("act" if (k0 // 4) % 2 == 0 else "dve", aT[:, k0:k0 + 4, coff:coff + 128],
                             ps[:, b, :].rearrange("p (k t) -> p k t", k=4), [PB(b)], ["aT"])

        def outproj_phase(layer, wmat, rows):
            st, alloc = phase()
            with st:
                wob = alloc("wob", [128, KD, D], BF16)
                wst = Ring(alloc, "wst", [128, D], F32, 2)
                for k in range(KD):
                    wt, wn = wst.next()
                    S.dma(wt[:], wmat[k * 128:(k + 1) * 128, :], [], [wn])
                    S.cp("dve" if k % 2 else "act", wob[:, k, :], wt[:], [wn], ["wob"])
                need_rows = sorted(set(r for r, _ in rows))
                Gate = {r: load_mod(alloc, "gate%d" % r, layer, r, 2) for r in need_rows}
                mr = Ring(alloc, "mT", [128, KD, 128], BF16, 2)
                hr = Ring(alloc, "hO", [128, D], F32, 2)
                tmp = Ring(alloc, "tmpO", [128, 512], F32, 2)
                NBK = D // 512
                pbi = 0
                for (r, ti) in rows:
                    mt, mn = mr.next()
                    ht, hn = hr.next()
                    S.dma_k(mt[:], MT[:, ti * 128:(ti + 1) * 128].rearrange("(k p) t -> p k t", p=128), ["MT"], [mn])
                    S.dma(ht[:], hbuf[ti * 128:(ti + 1) * 128, :], ["hbuf"], [hn])
                    for nb in range(NBK):
                        b = pbi % 4
                        pbi += 1
                        for k in range(KD):
                            S.mm(ps[:, b, :], mt[:, k, :], wob[:, k, nb * 512:(nb + 1) * 512], k == 0, k == KD - 1, [mn, "wob"], [PB(b)])
                        tt_, tn = tmp.next()
                        S.tt("dve", tt_[:], ps[:, b, :], Gate[r][:, nb * 512:(nb + 1) * 512], ALU.mult, [PB(b), "gate%d" % r], [tn])
                        S.tt("pool", ht[:, nb * 512:(nb + 1) * 512], ht[:, nb * 512:(nb + 1) * 512], tt_[:], ALU.add, [hn, tn], [hn])
                    S.dma(hbuf[ti * 128:(ti + 1) * 128, :], ht[:], [hn], ["hbuf"], eng="pool")
            S.barrier()

        def attention_phase(layer, j, ctx_q):
            Tq = L + (C if ctx_q else 0)
            st, alloc = phase()
            with st:
                aT = alloc("aT", [128, KD, TP], BF16)
                rows = [(0, t, LOFF + t * 128) for t in range(NLT)] + [(1, NLT + t, COFF + t * 128) for t in range(NCT)]
                modulate_phase(layer, 1, rows, aT, False)
                S.barrier()
                st2, alloc2 = phase()
                with st2:
                    alloc = alloc2
                    identb = alloc("identb", [128, 128], BF16)
                    prot = alloc("prot", [128, 128], F32)
                    cosT = alloc("cosT", [128, L], F32)
                    sinT = alloc("sinT", [128, L], F32)
                    masks = alloc("masks", [128, 3, 384], F32)
                    sinkb = alloc("sinkb", [128, NQ], F32)
                    nsinkb = alloc("nsinkb", [128, NQ], F32)
                    S.dma(identb[:], cst["ident_b"][:, :], [], ["identb"])
                    S.dma(prot[:], cst["prot"][:, :], [], ["prot"])
                    S.dma(cosT[:], cst["cosT"][:, :], [], ["cosT"])
                    S.dma(sinT[:], cst["sinT"][:, :], [], ["sinT"])
                    S.dma(masks[:], cst["masks"][:, :, :], [], ["masks"])
                    S.dma(sinkb[:], sink_in[j:j + 1, :].to_broadcast([128, NQ]), [], ["sinkb"])
                    S.ts("dve", nsinkb[:], sinkb[:], -1.0, None, ALU.mult, None, ["sinkb"], ["nsinkb"])
                    wf = Ring(alloc, "wf", [128, KD, 128], F32, 1)
                    wb = Ring(alloc, "wb", [128, KD, 128], BF16, 2)
                    kT = alloc("kT", [128, T], BF16)
                    vtok = alloc("vtok", [128, NT, 128], BF16)
                    qT = alloc("qT", [128, T], BF16)
                    oT = Ring(alloc, "oT", [128, T], BF16, 2)
                    qf = Ring(alloc, "qf", [128, 512], F32, 2)
                    t1 = Ring(alloc, "t1", [128, 512], F32, 2)
                    t2 = Ring(alloc, "t2", [128, 512], F32, 2)
                    sb_s = Ring(alloc, "s", [128, 640], F32, 4)
                    sb_p = Ring(alloc, "p", [128, 640], BF16, 4)
                    sb_pn = Ring(alloc, "pn", [128, 640], BF16, 4)
                    sb_pT = Ring(alloc, "pT", [128, 5, 128], BF16, 4)
                    stat = Ring(alloc, "stat", [128, 8], F32, 6)
                    scale = 128 ** -0.5

                    def load_w(c0):
                        wft, wfn = wf.next()
                        wbt, wbn = wb.next()
                        S.dma_k(wft[:], w_qkv[j, :, c0:c0 + 128].rearrange("(k p) c -> p k c", p=128), [], [wfn])
                        load_w.i = getattr(load_w, "i", 0) + 1
                        S.cp("act" if load_w.i % 2 else "dve", wbt[:], wft[:], [wfn], [wbn])
                        return wbt, wbn

                    def proj_fm(wbt, wbn, dst, dname, rope, ranges):
                        for (acol, dcol, n, pos0) in ranges:
                            for c in range(0, n, 512):
                                w = min(512, n - c)
                                b = 0 if (c // 512) % 2 == 0 else 1
                                for k in range(KD):
                                    S.mm(ps[:, b, 0:w], wbt[:, k, :], aT[:, k, acol + c:acol + c + w], k == 0, k == KD - 1, [wbn, "aT"], [PB(b)])
                                if not rope:
                                    S.cp("act", dst[:, dcol + c:dcol + c + w], ps[:, b, 0:w], [PB(b)], [dname])
                                else:
                                    qft, qfn = qf.next()
                                    t1t, t1n = t1.next()
                                    t2t, t2n = t2.next()
                                    S.cp("act", qft[:, 0:w], ps[:, b, 0:w], [PB(b)], [qfn])
                                    S.mm(ps[:, 2, 0:w], prot[:], qft[:, 0:w], True, True, ["prot", qfn], [PB(2)])
                                    S.tt("dve", t1t[:, 0:w], qft[:, 0:w], cosT[:, pos0 + c:pos0 + c + w], ALU.mult, [qfn, "cosT"], [t1n])
                                    S.tt("dve", t2t[:, 0:w], ps[:, 2, 0:w], sinT[:, pos0 + c:pos0 + c + w], ALU.mult, [PB(2), "sinT"], [t2n])
                                    S.tt("pool", dst[:, dcol + c:dcol + c + w], t1t[:, 0:w], t2t[:, 0:w], ALU.add, [t1n, t2n], [dname])

                    for g in range(NKV):
                        wbt, wbn = load_w((NQ + g) * 128)
                        proj_fm(wbt, wbn, kT, "kT", True, [(LOFF, 0, L, 0)])
                        proj_fm(wbt, wbn, kT, "kT", False, [(COFF, L, C, 0)])
                        wbt, wbn = load_w((NQ + NKV + g) * 128)
                        for ti in range(NT):
                            acol = LOFF + ti * 128 if ti < NLT else COFF + (ti - NLT) * 128
                            for k in range(KD):
                                S.mm(ps[:, 7, 0:128], aT[:, k, acol:acol + 128], wbt[:, k, :], k == 0, k == KD - 1, ["aT", wbn], [PB(7)])
                            S.cp("act", vtok[:, ti, :], ps[:, 7, 0:128], [PB(7)], ["vtok"])
                        for hq in range(GRP):
                            h = g * GRP + hq
                            wbt, wbn = load_w(h * 128)
                            proj_fm(wbt, wbn, qT, "qT", True, [(LOFF, 0, L, 0)])
                            if ctx_q:
                                proj_fm(wbt, wbn, qT, "qT", False, [(COFF, L, C, 0)])
                            ot, on = oT.next()
                            nqb = Tq // 128
                            SLOTS = [dict(win=2, ctx=(3, 0), pt=4, o=5), dict(win=6, ctx=(3, 256), pt=7, o=1)]

                            def unit(qb, sl, h=h, ot=ot, on=on):
                                is_lat = qb < NLT
                                s_t, s_n = sb_s.next()
                                p_t, p_n = sb_p.next()
                                pn_t, pn_n = sb_pn.next()
                                pT_t, pT_n = sb_pT.next()
                                stt_, stn = stat.next()
                                qsl = qT[:, qb * 128:(qb + 1) * 128]
                                bw, (bc, cc0), bp, bo = sl["win"], sl["ctx"], sl["pt"], sl["o"]
                                cn = "psb%d_%d" % (bc, cc0)
                                k0 = 0
                                if is_lat:
                                    if qb == 0:
                                        k0, mv = 0, 0
                                    elif qb == NLT - 1:
                                        k0, mv = (NLT - 3) * 128, 2
                                    else:
                                        k0, mv = (qb - 1) * 128, 1
                                    S.mm(ps[:, bw, 0:384], qsl, kT[:, k0:k0 + 384], True, True, ["qT", "kT"], [PB(bw)])
                                    yield
                                    S.stt(s_t[:, 0:384], ps[:, bw, 0:384], scale, masks[:, mv, :], ALU.mult, ALU.add, [PB(bw), "masks"], [s_n])
                                    yield
                                    lo = 0
                                else:
                                    lo = 384
                                S.mm(ps[:, bc, cc0:cc0 + C], qsl, kT[:, L:L + C], True, True, ["qT", "kT"], [cn])
                                yield
                                S.act(s_t[:, 384:384 + C], ps[:, bc, cc0:cc0 + C], AF.Copy, [cn], [s_n], scale=scale)
                                yield
                                hi = 384 + C
                                S.op("dve", lambda e, o=stt_[:, 0:1], i=s_t[:, lo:hi]: e.tensor_reduce(o, i, AX.X, ALU.max, negate=True), [s_n], [stn])
                                yield
                                S.tt("dve", stt_[:, 1:2], stt_[:, 0:1], nsinkb[:, h:h + 1], ALU.min, [stn, "nsinkb"], [stn])
                                yield
                                S.act(p_t[:, lo:hi], s_t[:, lo:hi], AF.Exp, [s_n, stn], [p_n, stn], bias=stt_[:, 1:2], accum_out=stt_[:, 2:3])
                                yield
                                S.act(stt_[:, 3:4], sinkb[:, h:h + 1], AF.Exp, ["sinkb", stn], [stn], bias=stt_[:, 1:2])
                                yield
                                S.tt("dve", stt_[:, 4:5], stt_[:, 2:3], stt_[:, 3:4], ALU.add, [stn], [stn])
                                yield
                                S.op("dve", lambda e, o=stt_[:, 5:6], i=stt_[:, 4:5]: e.reciprocal(o, i), [stn], [stn])
                                yield
                                S.ts("dve", pn_t[:, lo:hi], p_t[:, lo:hi], stt_[:, 5:6], None, ALU.mult, None, [p_n, stn], [pn_n])
                                yield
                                nch = (hi - lo) // 128
                                c_lo = lo // 128
                                for cc in range(nch):
                                    c = c_lo + cc
                                    S.tr(psb[:, bp, c * 128:(c + 1) * 128], pn_t[:, c * 128:(c + 1) * 128], identb[:], [pn_n, "identb"], [PB(bp)])
                                yield
                                S.cp("act", pT_t[:, c_lo:c_lo + nch, :], psb[:, bp, c_lo * 128:(c_lo + nch) * 128].rearrange("p (c t) -> p c t", t=128), [PB(bp)], [pT_n])
                                yield
                                for cc in range(nch):
                                    c = c_lo + cc
                                    if c < 3:
                                        vt = k0 // 128 + c
                                    else:
                                        vt = NLT + (c - 3)
                                    S.mm(ps[:, bo, 0:128], vtok[:, vt, :], pT_t[:, c, :], cc == 0, cc == nch - 1, ["vtok", pT_n], [PB(bo)])
                                yield
                                S.cp("dve", ot[:, qb * 128:(qb + 1) * 128], ps[:, bo, 0:128], [PB(bo)], [on + "_%d" % (qb % 2)])
                                yield

                            for qb0 in range(0, nqb, 2):
                                gens = [unit(qb0 + u, SLOTS[u]) for u in range(min(2, nqb - qb0))]
                                while gens:
                                    for g_ in list(gens):
                                        try:
                                            next(g_)
                                        except StopIteration:
                                            gens.remove(g_)
                            S.dma(MT[h * 128:(h + 1) * 128, 0:Tq], ot[:, 0:Tq], [on + "_0", on + "_1"], ["MT"], eng="pool")
            S.barrier()

        def hyena_spectra(j, tag, n, conv_range=None):
            nt = n // 128
            st, alloc = phase()
            with st:
                hA = alloc("hA", [HYH, n], F32)
                nt01 = alloc("nt01", [128, nt], F32)
                S.dma(nt01[:], cst["nt01" + tag][:, :], [], ["nt01"])
                st_m, alloc_m = phase()
                with st_m:
                    zT = alloc_m("zT", [HYE, n], F32)
                    w1 = alloc_m("w1", [HYE, HYH], F32)
                    w2 = alloc_m("w2", [HYH, 2, HYH], F32)
                    b1 = alloc_m("b1", [HYH, 1], F32)
                    b2 = alloc_m("b2", [HYH, 2], F32)
                    hB = alloc_m("hB", [HYH, n], F32)
                    S.dma(zT[:], cst["zT" + tag][:, :], [], ["zT"])
                    S.dma(w1[:], hy_w1[j, :, :], [], ["w1"])
                    for i in range(2):
                        S.dma(w2[:, i, :], hy_w2[j, i, :, :], [], ["w2"])
                        S.dma(b2[:, i:i + 1], hy_b2[j, i, :, :], [], ["b2"])
                    S.dma(b1[:], hy_b1[j, :, :], [], ["b1"])
                    pre = Ring(alloc_m, "pre", [HYH, 512], F32, 2)
                    TWO_PI = 2.0 * math.pi

                    def sin_layer(lhsT, lname, bias_ap, bname, src, sname, dst, dname, kdim):
                        for c in range(0, n, 512):
                            w = min(512, n - c)
                            b = (c // 512) % 2
                            S.mm(ps[0:HYH, b, 0:w], lhsT, src[0:kdim, c:c + w], True, True, [lname, sname], [PB(b)])
                            pt, pn = pre.next()
                            S.act(pt[:, 0:w], ps[0:HYH, b, 0:w], AF.Identity, [PB(b), bname], [pn], bias=bias_ap)
                            for _ in range(2):
                                S.ts("dve", dst[:, c:c + w], pt[:, 0:w], math.pi, -TWO_PI, ALU.is_gt, ALU.mult, [pn], [dname])
                                S.tt("dve", pt[:, 0:w], pt[:, 0:w], dst[:, c:c + w], ALU.add, [pn, dname], [pn])
                                S.ts("dve", dst[:, c:c + w], pt[:, 0:w], -math.pi, TWO_PI, ALU.is_lt, ALU.mult, [pn], [dname])
                                S.tt("dve", pt[:, 0:w], pt[:, 0:w], dst[:, c:c + w], ALU.add, [pn, dname], [pn])
                            S.act(dst[:, c:c + w], pt[:, 0:w], AF.Sin, [pn], [dname])
                    sin_layer(w1[:], "w1", b1[:, 0:1], "b1", zT, "zT", hA, "hA", HYE)
                    sin_layer(w2[:, 0, :], "w2", b2[:, 0:1], "b2", hA, "hA", hB, "hB", HYH)
                    sin_layer(w2[:, 1, :], "w2", b2[:, 1:2], "b2", hB, "hB", hA, "hA", HYH)
                S.barrier()
                cg = convert_gen(alloc, conv_range[0], conv_range[1], dma_eng="pool") if conv_range else None

                def pull(k):
                    if cg is not None:
                        for _ in range(k):
                            next(cg, None)
                w3f = Ring(alloc, "w3f", [HYH, 512], F32, 2)
                w3b = Ring(alloc, "w3b", [HYH, 512], F32, 2)
                absd = Ring(alloc, "absd", [128, 512], F32, 2)
                wnd = Ring(alloc, "wnd", [128, 512], F32, 2)
                ff = Ring(alloc, "ff", [128, 512], F32, 2)
                fbk = Ring(alloc, "fbk", [128, 512], F32, 2)
                FS = alloc("FS", [128, nt, 512], BF16)
                FD = alloc("FD", [128, nt, 512], BF16)
                fre = Ring(alloc, "fre", [128, nt, 128], BF16, 2)
                fim = Ring(alloc, "fim", [128, nt, 128], BF16, 2)
                ore = Ring(alloc, "ore", [128, 512], F32, 2)
                oim = Ring(alloc, "oim", [128, 512], F32, 2)
                od0 = alloc("od0", [128, 512], F32)
                for o in range(2):
                    for cb in range(CB):
                        c0 = cb * 512
                        wft, wfn = w3f.next()
                        wbt, wbn = w3b.next()
                        adt, adn = absd.next()
                        S.dma(wft[:], hy_w3[j, :, o * 2 * D + c0:o * 2 * D + c0 + 512], [], [wfn])
                        S.dma(wbt[:], hy_w3[j, :, o * 2 * D + D + c0:o * 2 * D + D + c0 + 512], [], [wbn])
                        S.dma(adt[:], hy_dec[j, o:o + 1, c0:c0 + 512].to_broadcast([128, 512]), [], [adn])
                        S.act(adt[:], adt[:], AF.Abs, [adn], [adn])
                        for lt in range(nt):
                            wt, wn = wnd.next()
                            fft, ffn = ff.next()
                            fbt, fbn = fbk.next()
                            S.mm(ps[:, 0, :], hA[:, lt * 128:(lt + 1) * 128], wft[:], True, True, ["hA", wfn], [PB(0)])
                            S.mm(ps[:, 1, :], hA[:, lt * 128:(lt + 1) * 128], wbt[:], True, True, ["hA", wbn], [PB(1)])
                            S.act(wt[:], adt[:], AF.Exp, [adn, "nt01"], [wn], scale=nt01[:, lt:lt + 1])
                            S.stt(fft[:], wt[:], 0.05, ps[:, 0, :], ALU.add, ALU.mult, [wn, PB(0)], [ffn])
                            S.stt(fbt[:], wt[:], 0.05, ps[:, 1, :], ALU.add, ALU.mult, [wn, PB(1)], [fbn])
                            if lt == 0:
                                S.memset("dve", fbt[0:1, :], 0.0, [fbn])
                            S.tt("dve", FS[:, lt, :], fft[:], fbt[:], ALU.add, [ffn, fbn], ["FS"])
                            S.tt("dve", FD[:, lt, :], fft[:], fbt[:], ALU.subtract, [ffn, fbn], ["FD"])
                            pull(1)
                        for kf in range(nt):
                            fr_t, fr_n = fre.next()
                            fi_t, fi_n = fim.next()
                            S.dma(fr_t[:], cst["Fre" + tag][kf, :, :, :], [], [fr_n])
                            S.dma(fi_t[:], cst["Fim" + tag][kf, :, :, :], [], [fi_n])
                            for lt in range(nt):
                                S.mm(ps[:, 2, :], fr_t[:, lt, :], FS[:, lt, :], lt == 0, lt == nt - 1, [fr_n, "FS"], [PB(2)])
                            for lt in range(nt):
                                S.mm(ps[:, 3, :], fi_t[:, lt, :], FD[:, lt, :], lt == 0, lt == nt - 1, [fi_n, "FD"], [PB(3)])
                            if kf == 0:
                                for lt in range(nt):
                                    S.mm(ps[0:1, 4, :], fi_t[:, lt, 0:1], FS[:, lt, :], lt == 0, lt == nt - 1, [fi_n, "FS"], [PB(4)])
                            ot, on = ore.next()
                            it, in_ = oim.next()
                            S.cp("act", ot[:], ps[:, 2, :], [PB(2)], [on])
                            S.cp("dve", it[:], ps[:, 3, :], [PB(3)], [in_])
                            if kf == 0:
                                S.cp("act", od0[:], ot[:], [on], ["od0"])
                                S.cp("dve", od0[0:1, :], ps[0:1, 4, :], [PB(4), "od0"], ["od0"])
                                S.memset("dve", it[0:1, :], 0.0, [in_])
                                S.dma(spec[(tag, o, "d0")][:, c0:c0 + 512], od0[:], ["od0"], ["spec"])
                            S.dma(spec[(tag, o, "re")][kf * 128:(kf + 1) * 128, c0:c0 + 512], ot[:], [on], ["spec"])
                            S.dma(spec[(tag, o, "im")][kf * 128:(kf + 1) * 128, c0:c0 + 512], it[:], [in_], ["spec"])
                            pull(2)
                if cg is not None:
                    for _ in cg:
                        pass
            S.barrier()

        def hyena_inproj(layer, j, with_ctx):
            st, alloc = phase()
            with st:
                aT = alloc("aT", [128, KD, TP], BF16)
                S.memset("pool", aT[:, :, 0:1], 0.0, ["aT"])
                S.memset("pool", aT[:, :, L + 1:L + 3], 0.0, ["aT"])
                S.memset("pool", aT[:, :, TP - 1:TP], 0.0, ["aT"])
                rows = [(0, t, LOFF + t * 128) for t in range(NLT)]
                if with_ctx:
                    rows += [(1, NLT + t, COFF + t * 128) for t in range(NCT)]
                modulate_phase(layer, 1, rows, aT, False)
                S.barrier()
                st2, alloc = phase()
                with st2:
                    NCH = 3 * D // 128
                    identf = alloc("identf", [128, 128], F32)
                    cwT = alloc("cwT", [128, NCH, 3], F32)
                    cbT = alloc("cbT", [128, NCH], F32)
                    S.dma(identf[:], cst["ident_f"][:, :], [], ["identf"])
                    S.dma(cwT[:], hy_cwT[j, :, :, :], [], ["cwT"])
                    S.dma(cbT[:], hy_cbT[j, :, :], [], ["cbT"])
                    wf = Ring(alloc, "wf", [128, KD, 128], F32, 2)
                    wb = Ring(alloc, "wb", [128, KD, 128], BF16, 2)
                    WN = 3
                    ob = Ring(alloc, "ocv", [128, WN * 128], F32, 3)
                    stage = alloc("stageU", [128, NT, 512], F32)
                    wins = []
                    for t0 in range(0, NLT, WN):
                        wins.append((t0, min(NLT, t0 + WN), LOFF + t0 * 128))
                    if with_ctx:
                        for t0 in range(0, NCT, WN):
                            wins.append((NLT + t0, NLT + min(NCT, t0 + WN), COFF + t0 * 128))
                    ntile_used = NLT + (NCT if with_ctx else 0)
                    pbi = 0
                    pend = []
                    for cc in range(NCH):
                        wft, wfn = wf.next()
                        wbt, wbn = wb.next()
                        S.dma_k(wft[:], hy_w_in[j, :, cc * 128:(cc + 1) * 128].rearrange("(k p) c -> p k c", p=128), [], [wfn])
                        S.cp("act" if cc % 2 == 0 else "dve", wbt[:], wft[:], [wfn], [wbn])
                        c4 = cc % 4
                        for (ta, tb, acol) in wins:
                            nout = (tb - ta) * 128
                            nin = nout + 2
                            b = pbi % 2
                            b2 = 2 + pbi % 2
                            pbi += 1
                            for k in range(KD):
                                S.mm(ps[:, b, 0:nin], wbt[:, k, :], aT[:, k, acol - 1:acol - 1 + nin], k == 0, k == KD - 1, [wbn, "aT"], [PB(b)])
                            ot, on = ob.next()
                            S.act(ot[:, 0:nout], ps[:, b, 1:1 + nout], AF.Identity, [PB(b), "cwT", "cbT"], [on],
                                  bias=cbT[:, cc:cc + 1], scale=cwT[:, cc, 1:2])
                            S.stt(ot[:, 0:nout], ps[:, b, 0:nout], cwT[:, cc, 0:1], ot[:, 0:nout], ALU.mult, ALU.add, [PB(b), "cwT", on], [on])
                            S.stt(ot[:, 0:nout], ps[:, b, 2:2 + nout], cwT[:, cc, 2:3], ot[:, 0:nout], ALU.mult, ALU.add, [PB(b), "cwT", on], [on])
                            def fin(ta=ta, tb=tb, ot=ot, on=on, b2=b2, c4=c4, nout=nout, ev=("act" if pbi % 2 == 0 else "dve")):
                                for tl in range(tb - ta):
                                    S.tr(ps[:, b2, tl * 128:(tl + 1) * 128], ot[:, tl * 128:(tl + 1) * 128], identf[:], [on, "identf"], [PB(b2)])
                                S.cp(ev, stage[:, ta:tb, c4 * 128:(c4 + 1) * 128],
                                     ps[:, b2, 0:nout].rearrange("p (t c) -> p t c", c=128), [PB(b2)], ["stageU"])
                            pend.append(fin)
                            if len(pend) > 1:
                                pend.pop(0)()
                        if c4 == 3:
                            while pend:
                                pend.pop(0)()
                            g0 = (cc // 4) * 512
                            for t0 in range(0, ntile_used, 4):
                                t1 = min(ntile_used, t0 + 4)
                                S.dma(U_d[t0 * 128:t1 * 128, g0:g0 + 512].rearrange("(t p) c -> p t c", p=128), stage[:, t0:t1, :], ["stageU"], ["U_d"], eng="pool")
            S.barrier()

        def hyena_conv(j, tag, n, toff):
            nt = n // 128
            st, alloc = phase()
            with st:
                identb = alloc("identb", [128, 128], BF16)
                S.dma(identb[:], cst["ident_b"][:, :], [], ["identb"])
                vbuf = alloc("vbuf", [128, nt, 512], F32)
                zsb = alloc("zsb", [128, nt, 512], BF16)
                Yre = alloc("Yre", [128, nt, 512], BF16)
                Yim = alloc("Yim", [128, nt, 512], BF16)
                fre = Ring(alloc, "fre", [128, nt, 128], BF16, 2)
                fim = Ring(alloc, "fim", [128, nt, 128], BF16, 2)
                sA = Ring(alloc, "sA", [128, 512], F32, 2)
                sB = Ring(alloc, "sB", [128, 512], F32, 2)
                sD = alloc("sD", [128, 512], F32)
                xt = Ring(alloc, "xt", [128, 512], F32, 2)
                fbb = alloc("fbb", [128, 2, 512], F32)
                e1 = Ring(alloc, "e1", [128, 512], F32, 2)
                e2 = Ring(alloc, "e2", [128, 512], F32, 2)
                e3 = Ring(alloc, "e3", [128, 512], F32, 2)
                e4 = Ring(alloc, "e4", [128, 512], F32, 2)
                yo = Ring(alloc, "yo", [128, 512], BF16, 2)
                yT = Ring(alloc, "yT", [128, 4, 128], BF16, 2)
                for cb in range(CB):
                    c0 = cb * 512
                    S.dma_k(vbuf[:], U_d[toff:toff + n, c0:c0 + 512].rearrange("(t p) c -> p t c", p=128), ["U_d"], ["vbuf"])
                    S.dma(fbb[:], hy_fb[j:j + 1, :, c0:c0 + 512].to_broadcast([128, 2, 512]), [], ["fbb"])
                    for o in range(2):
                        S.cp("act", zsb[:], vbuf[:], ["vbuf"], ["zsb"])
                        for kf in range(nt):
                            fr_t, fr_n = fre.next()
                            fi_t, fi_n = fim.next()
                            S.dma(fr_t[:], cst["Fre" + tag][kf, :, :, :], [], [fr_n])
                            S.dma(fi_t[:], cst["Fim" + tag][kf, :, :, :], [], [fi_n])
                            At, An = sA.next()
                            Bt, Bn = sB.next()
                            S.dma(At[:], spec[(tag, o, "re")][kf * 128:(kf + 1) * 128, c0:c0 + 512], ["spec"], [An])
                            S.dma(Bt[:], spec[(tag, o, "im")][kf * 128:(kf + 1) * 128, c0:c0 + 512], ["spec"], [Bn])
                            Dt, Dn = At, An
                            if kf == 0:
                                S.dma(sD[:], spec[(tag, o, "d0")][:, c0:c0 + 512], ["spec"], ["sD"])
                                Dt, Dn = sD, "sD"
                            br, bi = (0, 2) if kf % 2 == 0 else (1, 3)
                            for lt in range(nt):
                                S.mm(ps[:, br, :], fr_t[:, lt, :], zsb[:, lt, :], lt == 0, lt == nt - 1, [fr_n, "zsb"], [PB(br)])
                            for lt in range(nt):
                                S.mm(ps[:, bi, :], fi_t[:, lt, :], zsb[:, lt, :], lt == 0, lt == nt - 1, [fi_n, "zsb"], [PB(bi)])
                            a1, n1 = e1.next()
                            a2, n2 = e2.next()
                            a3, n3 = e3.next()
                            a4, n4 = e4.next()
                            S.tt("dve", a1[:], ps[:, br, :], At[:], ALU.mult, [PB(br), An], [n1])
                            S.tt("dve", a2[:], ps[:, bi, :], Bt[:], ALU.mult, [PB(bi), Bn], [n2])
                            S.tt("pool", Yre[:, kf, :], a1[:], a2[:], ALU.subtract, [n1, n2], ["Yre"])
                            S.tt("dve", a3[:], ps[:, br, :], Bt[:], ALU.mult, [PB(br), Bn], [n3])
                            S.tt("dve", a4[:], ps[:, bi, :], Dt[:], ALU.mult, [PB(bi), Dn], [n4])
                            S.tt("pool", Yim[:, kf, :], a3[:], a4[:], ALU.add, [n3, n4], ["Yim"])
                        for tt_i in range(nt):
                            gr_t, gr_n = fre.next()
                            gi_t, gi_n = fim.next()
                            S.dma(gr_t[:], cst["Gre" + tag][tt_i, :, :, :], [], [gr_n])
                            S.dma(gi_t[:], cst["Gim" + tag][tt_i, :, :, :], [], [gi_n])
                            x_t, x_n = xt.next()
                            S.dma(x_t[:], U_d[toff + tt_i * 128:toff + (tt_i + 1) * 128, (o + 1) * D + c0:(o + 1) * D + c0 + 512], ["U_d"], [x_n])
                            b = 4 + tt_i % 2
                            for kf in range(nt):
                                S.mm(ps[:, b, :], gr_t[:, kf, :], Yre[:, kf, :], kf == 0, False, [gr_n, "Yre"], [PB(b)])
                            for kf in range(nt):
                                S.mm(ps[:, b, :], gi_t[:, kf, :], Yim[:, kf, :], False, kf == nt - 1, [gi_n, "Yim"], [PB(b)])
                            a1, n1 = e1.next()
                            a2, n2 = e2.next()
                            S.tt("pool", a1[:], vbuf[:, tt_i, :], fbb[:, o, :], ALU.mult, ["vbuf", "fbb"], [n1])
                            S.tt("dve", a2[:], ps[:, b, :], a1[:], ALU.add, [PB(b), n1], [n2])
                            if o == 0:
                                S.tt("dve", vbuf[:, tt_i, :], a2[:], x_t[:], ALU.mult, [n2, x_n], ["vbuf"])
                            else:
                                y_t, y_n = yo.next()
                                yT_t, yT_n = yT.next()
                                S.tt("dve", y_t[:], a2[:], x_t[:], ALU.mult, [n2, x_n], [y_n])
                                for q4 in range(4):
                                    S.tr(psb[:, 6, q4 * 128:(q4 + 1) * 128], y_t[:, q4 * 128:(q4 + 1) * 128], identb[:], [y_n, "identb"], [PB(6)])
                                S.cp("act", yT_t[:], psb[:, 6, 0:512].rearrange("p (c t) -> p c t", t=128), [PB(6)], [yT_n])
                                S.dma(MT[c0:c0 + 512, toff + tt_i * 128:toff + (tt_i + 1) * 128].rearrange("(c p) t -> p c t", p=128), yT_t[:], [yT_n], ["MT"], eng="pool")
            S.barrier()

        def peer_phase(layer, with_ctx):
            rows = [(0, t) for t in range(NLT)] + ([(1, NLT + t) for t in range(NCT)] if with_ctx else [])
            st, alloc = phase()
            with st:
                xnT = alloc("aT", [128, KD, T], BF16)
                modulate_phase(layer, 2, [(r, ti, ti * 128) for (r, ti) in rows], xnT, True)
                S.barrier()
                st2, alloc = phase()
                with st2:
                    wf = Ring(alloc, "wf", [128, KD, 128], F32, 2)
                    wb = Ring(alloc, "wb", [128, KD, 128], BF16, 2)
                    qo = Ring(alloc, "qo", [128, T], BF16, 2)
                    ntok = len(rows) * 128
                    for c in range(16):
                        wft, wfn = wf.next()
                        wbt, wbn = wb.next()
                        S.dma_k(wft[:], p_wq[layer, :, c * 128:(c + 1) * 128].rearrange("(k p) c -> p k c", p=128), [], [wfn])
                        S.cp("act" if c % 2 else "dve", wbt[:], wft[:], [wfn], [wbn])
                        qt, qn = qo.next()
                        for t0 in range(0, ntok, 512):
                            w = min(512, ntok - t0)
                            b = (t0 // 512) % 4
                            for k in range(KD):
                                S.mm(ps[:, b, 0:w], wbt[:, k, :], xnT[:, k, t0:t0 + w], k == 0, k == KD - 1, [wbn, "aT"], [PB(b)])
                            S.cp("act" if (t0 // 512) % 2 == 0 else "dve", qt[:, t0:t0 + w], ps[:, b, 0:w], [PB(b)], [qn])
                        S.dma(QT[c, :, 0:ntok], qt[:, 0:ntok], [qn], ["QT"], eng="pool")
            S.barrier()
            st, alloc = phase()
            with st:
                keyf = alloc("keyf", [128, 16, 128], F32)
                keyb = alloc("keyb", [128, 16, 128], BF16)
                iota = alloc("iota", [128, 2048], F32)
                identb = alloc("identb", [128, 128], BF16)
                S.dma(keyf[:], p_keysT[layer, :, :, :], [], ["keyf"])
                S.cp("act", keyb[:], keyf[:], ["keyf"], ["keyb"])
                S.dma(iota[:], cst["iota16"][:, :], [], ["iota"])
                S.dma(identb[:], cst["ident_b"][:, :], [], ["identb"])
                Gate = {r: load_mod(alloc, "gate%d" % r, layer, r, 5) for r in sorted(set(r for r, _ in rows))}
                qtt = Ring(alloc, "qtt", [128, 16, 128], BF16, 2)
                xnr = Ring(alloc, "xn", [128, D], F32, 2)
                hr = Ring(alloc, "hP", [128, D], F32, 1)
                uvr = Ring(alloc, "uv", [128, 2 * D], BF16, 6)
                dgr = Ring(alloc, "dg", [128, 128], BF16, 4)
                tmpr = Ring(alloc, "tmpP", [128, 512], F32, 2)
                sc = alloc("sc", [128, 16, 128], F32)
                sc2 = alloc("sc2", [128, 16, 128], F32)
                m16 = alloc("m16", [128, 16, 16], F32)
                i16 = alloc("i16", [128, 16, 16], U32)
                i16f = alloc("i16f", [128, 16, 16], F32)
                cand = alloc("cand", [128, PH, 256], F32)
                cand2 = alloc("cand2", [128, PH, 256], F32)
                tops = alloc("tops", [128, PH, 16], F32)
                posu = alloc("posu", [128, PH, 16], U32)
                pa = alloc("pa", [128, PH, 16], U32)
                pbq = alloc("pbq", [128, PH, 16], U32)
                paf = alloc("paf", [128, PH, 16], F32)
                pbf = alloc("pbf", [128, PH, 16], F32)
                oh = alloc("oh", [128, PH, 16, 16], F32)
                oh2 = alloc("oh2", [128, PH, 16, 16], F32)
                i1s = alloc("i1s", [128, PH, 16], F32)
                i2s = alloc("i2s", [128, PH, 16], F32)
                idf = alloc("idf", [128, 128], F32)
                idir = Ring(alloc, "idi", [128, 128], I32, 2)
                gsmr = Ring(alloc, "gsm", [128, PH, 16], F32, 2)
                gst = alloc("gst", [128, PH, 2], F32)
                aprer = Ring(alloc, "apre", [128, 128], F32, 2)
                glr = Ring(alloc, "gl", [128, 128], F32, 2)
                base = 0
                NBK = D // 512

                def routing(ti, idi, idn, gsm, gsn):
                    q_t, q_n = qtt.next()
                    S.dma_k(q_t[:], QT[:, :, ti * 128:(ti + 1) * 128].rearrange("c p t -> p c t"), ["QT"], [q_n])
                    yield
                    for c in range(16):
                        b = c // 4
                        S.mm(ps[:, b, (c % 4) * 128:(c % 4 + 1) * 128], q_t[:, c, :], keyb[:, c, :], True, True, [q_n, "keyb"], [PB(b)])
                        yield
                    for b in range(4):
                        S.cp("act", sc[:, b * 4:(b + 1) * 4, :], ps[:, b, :].rearrange("p (c k) -> p c k", k=128), [PB(b)], ["sc"])
                        yield
                    for c in range(16):
                        S.op("dve", lambda e, c=c: e.max(m16[:, c, 0:8], sc[:, c, :]), ["sc"], ["m16"])
                        yield
                        S.op("dve", lambda e, c=c: e.max_index(i16[:, c, 0:8], m16[:, c, 0:8], sc[:, c, :]), ["sc", "m16"], ["i16"])
                        yield
                        S.op("dve", lambda e, c=c: e.match_replace(sc2[:, c, :], m16[:, c, 0:8], sc[:, c, :], NEG), ["sc", "m16"], ["sc2"])
                        yield
                        S.op("dve", lambda e, c=c: e.max(m16[:, c, 8:16], sc2[:, c, :]), ["sc2"], ["m16"])
                        yield
                        S.op("dve", lambda e, c=c: e.max_index(i16[:, c, 8:16], m16[:, c, 8:16], sc2[:, c, :]), ["sc2", "m16"], ["i16"])
                        yield
                    S.cp("dve", i16f[:], i16[:], ["i16"], ["i16f"])
                    yield
                    m4 = m16[:].rearrange("p (h j) a -> p h j a", j=2)
                    i4 = i16f[:].rearrange("p (h j) a -> p h j a", j=2)
                    for h in range(PH):
                        S.tt("dve", cand[:, h, :].rearrange("p (a b) -> p a b", b=16),
                             m4[:, h, 0, :].unsqueeze(2).to_broadcast([128, 16, 16]),
                             m4[:, h, 1, :].unsqueeze(1).to_broadcast([128, 16, 16]), ALU.add, ["m16"], ["cand"])
                        yield
                    for h in range(PH):
                        S.op("dve", lambda e, h=h: e.max(tops[:, h, 0:8], cand[:, h, :]), ["cand"], ["tops"])
                        yield
                        S.op("dve", lambda e, h=h: e.max_index(posu[:, h, 0:8], tops[:, h, 0:8], cand[:, h, :]), ["cand", "tops"], ["posu"])
                        yield
                        S.op("dve", lambda e, h=h: e.match_replace(cand2[:, h, :], tops[:, h, 0:8], cand[:, h, :], NEG), ["cand", "tops"], ["cand2"])
                        yield
                        S.op("dve", lambda e, h=h: e.max(tops[:, h, 8:16], cand2[:, h, :]), ["cand2"], ["tops"])
                        yield
                        S.op("dve", lambda e, h=h: e.max_index(posu[:, h, 8:16], tops[:, h, 8:16], cand2[:, h, :]), ["cand2", "tops"], ["posu"])
                        yield
                    S.op("dve", lambda e: e.tensor_single_scalar(pa[:], posu[:], 4, ALU.logical_shift_right), ["posu"], ["pa"])
                    yield
                    S.op("dve", lambda e: e.tensor_single_scalar(pbq[:], posu[:], 15, ALU.bitwise_and), ["posu"], ["pbq"])
                    yield
                    S.cp("dve", paf[:], pa[:], ["pa"], ["paf"])
                    yield
                    S.cp("dve", pbf[:], pbq[:], ["pbq"], ["pbf"])
                    yield
                    iota4 = iota[:].rearrange("p (h k a) -> p h k a", h=PH, k=16)
                    for (pf, pfn, jj, dst, dn) in ((paf, "paf", 0, i1s, "i1s"), (pbf, "pbf", 1, i2s, "i2s")):
                        S.tt("dve", oh[:], pf[:].unsqueeze(3).to_broadcast([128, PH, 16, 16]), iota4, ALU.is_equal, [pfn, "iota"], ["oh"])
                        yield
                        S.tt("dve", oh2[:], oh[:], i4[:, :, jj, :].unsqueeze(2).to_broadcast([128, PH, 16, 16]), ALU.mult, ["oh", "i16f"], ["oh2"])
                        yield
                        S.op("dve", lambda e, dst=dst: e.tensor_reduce(dst[:], oh2[:], AX.X, ALU.add), ["oh2"], [dn])
                        yield
                    S.stt(idf[:], i1s[:].rearrange("p h k -> p (h k)"), 128.0, i2s[:].rearrange("p h k -> p (h k)"), ALU.mult, ALU.add, ["i1s", "i2s"], ["idf"])
                    yield
                    S.ts("dve", idf[:], idf[:], float(base), None, ALU.add, None, ["idf"], ["idf"])
                    yield
                    S.cp("dve", idi[:], idf[:], ["idf"], [idn])
                    yield
                    S.tt("dve", gsm[:], tops[:], tops[:, :, 0:1].to_broadcast([128, PH, 16]), ALU.subtract, ["tops"], [gsn])
                    yield
                    S.act(gsm[:], gsm[:], AF.Exp, [gsn], [gsn])
                    yield
                    S.op("dve", lambda e: e.tensor_reduce(gst[:, :, 0], gsm[:], AX.X, ALU.add), [gsn], ["gst"])
                    yield
                    S.op("dve", lambda e: e.reciprocal(gst[:, :, 1], gst[:, :, 0]), ["gst"], ["gst"])
                    yield
                    S.tt("dve", gsm[:], gsm[:], gst[:, :, 1:2].to_broadcast([128, PH, 16]), ALU.mult, [gsn, "gst"], [gsn])
                    yield

                def start_routing(ti):
                    idi, idn = idir.next()
                    gsm, gsn = gsmr.next()
                    return routing(ti, idi, idn, gsm, gsn), (idi, idn, gsm, gsn)

                def drain(gen, n=None):
                    k = 0
                    for _ in gen:
                        k += 1
                        if n is not None and k >= n:
                            return

                gen, cur = start_routing(rows[0][1])
                drain(gen)
                for ri, (r, ti) in enumerate(rows):
                    idi, idn, gsm, gsn = cur
                    nxt_gen, nxt = (None, None)
                    if ri + 1 < len(rows):
                        nxt_gen, nxt = start_routing(rows[ri + 1][1])
                    x_t, x_n = xnr.next()
                    S.dma(x_t[:], XN[ti * 128:(ti + 1) * 128, :], ["XN"], [x_n])
                    apre, apn = aprer.next()
                    gl, gln = glr.next()
                    gflat = gsm[:].rearrange("p h k -> p (h k)")
                    for s_ in range(128):
                        uv, uvn = uvr.next()
                        dg, dgn = dgr.next()
                        S.op("pool", lambda e, uv=uv, s_=s_, idi=idi: e.indirect_dma_start(
                            out=uv[:], out_offset=None, in_=UVs[layer][:, :],
                            in_offset=bass.IndirectOffsetOnAxis(ap=idi[:, s_:s_ + 1], axis=0)), [idn], [uvn], dma=True)
                        an = "%s_%d" % (apn, s_ % 8)
                        S.stt(uv[:, 0:D], uv[:, 0:D], 1.0, x_t[:], ALU.mult, ALU.mult, [uvn, x_n], [uvn + "u", an], accum_out=apre[:, s_:s_ + 1])
                        S.act(gl[:, s_:s_ + 1], apre[:, s_:s_ + 1], AF.Gelu, [an], [an + "g"])
                        S.act(gl[:, s_:s_ + 1], gl[:, s_:s_ + 1], AF.Copy, [an + "g", gsn], [an + "g"], scale=gflat[:, s_:s_ + 1])
                        S.act(dg[:], identb[:], AF.Copy, ["identb", an + "g"], [dgn], scale=gl[:, s_:s_ + 1])
                        for nb in range(NBK):
                            S.mm(ps[:, 4 + nb, :], dg[:], uv[:, D + nb * 512:D + (nb + 1) * 512], s_ == 0, s_ == 127, [dgn, uvn], [PB(4 + nb)])
                        if nxt_gen is not None:
                            drain(nxt_gen, 2)
                    if nxt_gen is not None:
                        drain(nxt_gen)
                    h_t, h_n = hr.next()
                    S.dma(h_t[:], hbuf[ti * 128:(ti + 1) * 128, :], ["hbuf"], [h_n])
                    for nb in range(NBK):
                        tt_, tn = tmpr.next()
                        S.tt("dve", tt_[:], ps[:, 4 + nb, :], Gate[r][:, nb * 512:(nb + 1) * 512], ALU.mult, [PB(4 + nb), "gate%d" % r], [tn])
                        S.tt("dve", h_t[:, nb * 512:(nb + 1) * 512], h_t[:, nb * 512:(nb + 1) * 512], tt_[:], ALU.add, [h_n, tn], [h_n])
                    S.dma(hbuf[ti * 128:(ti + 1) * 128, :], h_t[:], [h_n], ["hbuf"], eng="act")
                    cur = nxt
            S.barrier()

        for i in range(DEPTH):
            is_attn = (i % 2) == 0
            j = i // 2
            ctx_update = i < last_attn
            lat_rows = [(0, t) for t in range(NLT)]
            ctx_rows = [(1, NLT + t) for t in range(NCT)]
            if is_attn:
                attention_phase(i, j, ctx_update)
                outproj_phase(i, w_o[j], lat_rows + (ctx_rows if ctx_update else []))
            else:
                hyena_spectra(j, "L", L, conv_range=(NE // 128, DEPTH * NE // 128) if i == 1 else None)
                hyena_inproj(i, j, ctx_update)
                hyena_conv(j, "L", L, 0)
                if ctx_update:
                    hyena_spectra(j, "C", C)
                    hyena_conv(j, "C", C, L)
                outproj_phase(i, hy_wout[j], lat_rows + (ctx_rows if ctx_update else []))
            if stop_after == "mix%d" % i:
                break
            peer_phase(i, ctx_update)
            if stop_after == "ffn%d" % i:
                break

        st, alloc = phase()
        with st:
            if stop_after is None:
                gt = alloc("fg", [128, D], F32)
                S.dma(gt[:], fing[0:1, :].to_broadcast([128, D]), [], ["fg"])
            hr = Ring(alloc, "hF", [128, D], F32, 2)
            orr = Ring(alloc, "oF", [128, D], F32, 2)
            junk = alloc("junkF", [128, D], BF16)
            sr = Ring(alloc, "ssF", [128, 2], F32, 2)
            for t in range(NLT):
                ht, hn = hr.next()
                S.dma(ht[:], hbuf[t * 128:(t + 1) * 128, :], ["hbuf"], [hn])
                if stop_after is None:
                    ot, on = orr.next()
                    ss, sn = sr.next()
                    S.act(junk[:], ht[:], AF.Square, [hn], ["junkF", sn], accum_out=ss[:, 0:1])
                    S.ts("dve", ss[:, 1:2], ss[:, 0:1], 1.0 / D, RMS_EPS, ALU.mult, ALU.add, [sn], [sn])
                    S.act(ss[:, 1:2], ss[:, 1:2], AF.Sqrt, [sn], [sn])
                    S.op("dve", lambda e, ss=ss: e.reciprocal(ss[:, 1:2], ss[:, 1:2]), [sn], [sn])
                    S.stt(ot[:], ht[:], ss[:, 1:2], gt[:], ALU.mult, ALU.mult, [hn, sn, "fg"], [on])
                    S.dma(out_d[t * 128:(t + 1) * 128, :], ot[:], [on], ["out"], eng="pool")
                else:
                    S.dma(out_d[t * 128:(t + 1) * 128, :], ht[:], [hn], ["out"], eng="pool")
        S.barrier()
        S.emit()
        nops = S.nops
    return nc, nops


def make_in_maps(inputs, cfg, cores):
    D, L, C = cfg["D"], cfg["L"], cfg["C"]
    KD = D // 128
    f = lambda a: np.ascontiguousarray(np.asarray(a, dtype=np.float32))
    shared = {}
    for k in ("ada_w", "ada_b", "norm_mix_g", "norm_ffn_g", "attn_w_qkv", "attn_w_o", "attn_sink", "hy_w_in",
              "hy_f_w1", "hy_f_w2", "hy_f_w3", "hy_decay", "hy_fbias", "hy_w_out", "peer_w_q"):
        shared[k] = f(inputs[k])
    shared["final_g"] = f(inputs["final_g"]).reshape(1, D)
    cw = f(inputs["hy_conv_w"])
    shared["hy_conv_wT"] = np.ascontiguousarray(cw.reshape(cw.shape[0], 3, 3 * D // 128, 128).transpose(0, 3, 2, 1))
    cbv = f(inputs["hy_conv_b"])
    shared["hy_conv_bT"] = np.ascontiguousarray(cbv.reshape(cbv.shape[0], 3 * D // 128, 128).transpose(0, 2, 1))
    shared["hy_f_b1"] = f(inputs["hy_f_b1"])[..., None]
    shared["hy_f_b2"] = f(inputs["hy_f_b2"])[..., None]
    k1 = f(inputs["peer_keys1"])
    k2 = f(inputs["peer_keys2"])
    ks = np.stack([k1, k2], axis=2)
    shared["peer_keysT"] = np.ascontiguousarray(ks.transpose(0, 4, 1, 2, 3).reshape(ks.shape[0], 128, 16, 128))
    shared["peer_u"] = f(inputs["peer_u"]).reshape(-1, D)
    shared["peer_v"] = f(inputs["peer_v"]).reshape(-1, D)
    shared.update(host_consts(cfg))
    c_ctx = f(inputs["c_ctx"])
    maps = []
    for b in cores:
        m = dict(shared)
        m["x"] = f(inputs["x"][b])
        m["ctx"] = f(inputs["ctx"][b])
        cb = f(inputs["c"][b])
        cT = np.stack([cb.reshape(KD, 128).T, c_ctx.reshape(KD, 128).T], axis=-1)
        m["cT"] = np.ascontiguousarray(cT)
        maps.append(m)
    return maps


_NC_CACHE = {}


def kernel(**inputs):
    cfg = CFG
    key = (cfg["D"], cfg["L"], cfg["C"])
    if key not in _NC_CACHE:
        _NC_CACHE[key] = build_program(cfg)[0]
    nc = _NC_CACHE[key]
    B = inputs["x"].shape[0]
    maps = make_in_maps(inputs, cfg, list(range(B)))
    res = run_bass_kernel_spmd(nc, maps, core_ids=list(range(B)))
    return np.stack([np.asarray(r["out"], dtype=np.float32) for r in res.results], axis=0)
```
